# Optimizing a Trainium2 kernel written in Bass

```python
import math
import jax, jax.numpy as jnp
from jax import lax
import numpy as np

D_MODEL = 1024
BATCH = 8
SEQ = 2048
DEPTH = 1
DEC_BATCH = 128
DEC_SEQ = 8
PAST_LEN = 16384
PAGE_SIZE = 128

S5_WIDTH = D_MODEL // 2
S5_GROUP = 16
S5_GROUPS = S5_WIDTH // S5_GROUP
S5_STATE = 64
CM_WIDTH = D_MODEL - S5_WIDTH
CM_HEADS = 8
CM_HEAD_DIM = CM_WIDTH // CM_HEADS
CHUNK = 128
IN_WIDTH = S5_WIDTH + 2 * CM_WIDTH
MIX_WIDTH = S5_WIDTH + CM_WIDTH
D_FF = -(-8 * D_MODEL // (3 * 256)) * 256
EPS = 1e-6
DT_MIN = 1e-3
DT_MAX = 1e-1

kernel_name = "hymba_s5_chunkgmlp_decoder_step"


def rmsnorm(x, g):
    xf = x.astype(jnp.float32)
    r = lax.rsqrt(jnp.mean(xf * xf, axis=-1, keepdims=True) + EPS)
    return (xf * r * g.astype(jnp.float32)).astype(x.dtype)


def layernorm(x, g, b):
    xf = x.astype(jnp.float32)
    mu = jnp.mean(xf, axis=-1, keepdims=True)
    xc = xf - mu
    r = lax.rsqrt(jnp.mean(xc * xc, axis=-1, keepdims=True) + EPS)
    return (xc * r * g.astype(jnp.float32) + b.astype(jnp.float32)).astype(x.dtype)


def _complex_affine_combine(e1, e2):
    a1r, a1i, b1r, b1i = e1
    a2r, a2i, b2r, b2i = e2
    ar = a2r * a1r - a2i * a1i
    ai = a2r * a1i + a2i * a1r
    br = a2r * b1r - a2i * b1i + b2r
    bi = a2r * b1i + a2i * b1r + b2i
    return (ar, ai, br, bi)


def s5_mixer(xs, h0_re, h0_im, lam_re, lam_im, log_dt, b_re, b_im, c_re, c_im,
             d_skip, w_glu, b_glu):
    f32 = jnp.float32
    Bn, T, _ = xs.shape
    xf = xs.astype(f32)
    u = xf.reshape(Bn, T, S5_GROUPS, S5_GROUP)
    dt = jnp.exp(log_dt.astype(f32))[:, None]
    lr = lam_re.astype(f32)
    li = lam_im.astype(f32)
    mag = jnp.exp(lr * dt)
    ar = mag * jnp.cos(li * dt)
    ai = mag * jnp.sin(li * dt)
    den = lr * lr + li * li
    qr = ((ar - 1.0) * lr + ai * li) / den
    qi = (ai * lr - (ar - 1.0) * li) / den
    br_, bi_ = b_re.astype(f32), b_im.astype(f32)
    bbr = qr[..., None] * br_ - qi[..., None] * bi_
    bbi = qr[..., None] * bi_ + qi[..., None] * br_
    bu_r = jnp.einsum('btgi,gni->btgn', u, bbr)
    bu_i = jnp.einsum('btgi,gni->btgn', u, bbi)
    h0r = h0_re.astype(f32)
    h0i = h0_im.astype(f32)
    bu_r = bu_r.at[:, 0].add(ar * h0r - ai * h0i)
    bu_i = bu_i.at[:, 0].add(ar * h0i + ai * h0r)
    a_r = jnp.broadcast_to(ar, bu_r.shape)
    a_i = jnp.broadcast_to(ai, bu_i.shape)
    _, _, hr, hi = lax.associative_scan(_complex_affine_combine, (a_r, a_i, bu_r, bu_i), axis=1)
    y = (jnp.einsum('btgn,gon->btgo', hr, c_re.astype(f32))
         - jnp.einsum('btgn,gon->btgo', hi, c_im.astype(f32)))
    y = y.reshape(Bn, T, S5_WIDTH) + d_skip.astype(f32) * xf
    g = jax.nn.gelu(y)
    out = g * jax.nn.sigmoid(g @ w_glu.astype(f32) + b_glu.astype(f32))
    return out.astype(xs.dtype), hr[:, -1], hi[:, -1]


def chunk_mixer(uv, ln_g, ln_b, w_s, b_s):
    Bn, T, _ = uv.shape
    uv = jax.nn.gelu(uv)
    u = uv[..., :CM_WIDTH]
    v = layernorm(uv[..., CM_WIDTH:], ln_g, ln_b)
    Lc = T if T < CHUNK else CHUNK
    n_chunks = -(-T // Lc)
    Tp = n_chunks * Lc
    vp = jnp.pad(v, ((0, 0), (0, Tp - T), (0, 0)))
    vc = vp.reshape(Bn, n_chunks, Lc, CM_HEADS, CM_HEAD_DIM)
    mask = jnp.tril(jnp.ones((Lc, Lc), dtype=bool))
    w = jnp.where(mask[None], w_s[:, :Lc, :Lc], jnp.zeros((), w_s.dtype))
    mixed = jnp.einsum('hts,bcshd->bcthd', w, vc)
    mixed = mixed + jnp.transpose(b_s[:, :Lc])[None, None, :, :, None]
    mixed = mixed.reshape(Bn, Tp, CM_WIDTH)[:, :T]
    out = u * mixed
    start = ((T - 1) // CHUNK) * CHUNK
    return out, v[:, start:]


def swiglu(x, w_gate, w_up, w_down):
    return (jax.nn.silu(x @ w_gate) * (x @ w_up)) @ w_down


def setup_inputs(seed: int = 0) -> dict:
    key = jax.random.key(seed)
    ks = jax.random.split(key, 32)
    f32 = jnp.float32
    nrm = lambda k, shape, s: jax.random.normal(k, shape, f32) * s
    x_prompt = nrm(ks[0], (BATCH, SEQ, D_MODEL), 1.0)
    x_sample = nrm(ks[1], (DEC_BATCH, DEC_SEQ, D_MODEL), 1.0)
    state_s5_re = nrm(ks[2], (DEPTH, DEC_BATCH, S5_GROUPS, S5_STATE), 0.1)
    state_s5_im = nrm(ks[3], (DEPTH, DEC_BATCH, S5_GROUPS, S5_STATE), 0.1)
    norm1 = 1.0 + nrm(ks[4], (DEPTH, D_MODEL), 0.02)
    w_in = nrm(ks[5], (DEPTH, D_MODEL, IN_WIDTH), D_MODEL ** -0.5)
    lam_re = -0.5 * jnp.exp(nrm(ks[6], (DEPTH, S5_GROUPS, S5_STATE), 0.05))
    lam_im = jnp.broadcast_to(math.pi * jnp.arange(S5_STATE, dtype=f32),
                              (DEPTH, S5_GROUPS, S5_STATE)) + nrm(ks[7], (DEPTH, S5_GROUPS, S5_STATE), 0.01)
    log_dt = jax.random.uniform(ks[8], (DEPTH, S5_GROUPS), f32,
                                math.log(DT_MIN), math.log(DT_MAX))
    b_re = nrm(ks[9], (DEPTH, S5_GROUPS, S5_STATE, S5_GROUP), (2 * S5_GROUP) ** -0.5)
    b_im = nrm(ks[10], (DEPTH, S5_GROUPS, S5_STATE, S5_GROUP), (2 * S5_GROUP) ** -0.5)
    c_re = nrm(ks[11], (DEPTH, S5_GROUPS, S5_GROUP, S5_STATE), (2 * S5_STATE) ** -0.5)
    c_im = nrm(ks[12], (DEPTH, S5_GROUPS, S5_GROUP, S5_STATE), (2 * S5_STATE) ** -0.5)
    d_skip = nrm(ks[13], (DEPTH, S5_WIDTH), 1.0)
    w_glu = nrm(ks[14], (DEPTH, S5_WIDTH, S5_WIDTH), S5_WIDTH ** -0.5)
    b_glu = nrm(ks[15], (DEPTH, S5_WIDTH), 0.01)
    cm_ln_g = 1.0 + nrm(ks[16], (DEPTH, CM_WIDTH), 0.02)
    cm_ln_b = nrm(ks[17], (DEPTH, CM_WIDTH), 0.01)
    w_s = nrm(ks[18], (DEPTH, CM_HEADS, CHUNK, CHUNK), CHUNK ** -0.5)
    b_s = 1.0 + nrm(ks[19], (DEPTH, CM_HEADS, CHUNK), 0.01)
    g_s5 = 1.0 + nrm(ks[20], (DEPTH, S5_WIDTH), 0.02)
    g_cm = 1.0 + nrm(ks[21], (DEPTH, CM_WIDTH), 0.02)
    w_out = nrm(ks[22], (DEPTH, MIX_WIDTH, D_MODEL), MIX_WIDTH ** -0.5)
    norm2 = 1.0 + nrm(ks[23], (DEPTH, D_MODEL), 0.02)
    w_gate = nrm(ks[24], (DEPTH, D_MODEL, D_FF), D_MODEL ** -0.5)
    w_up = nrm(ks[25], (DEPTH, D_MODEL, D_FF), D_MODEL ** -0.5)
    w_down = nrm(ks[26], (DEPTH, D_FF, D_MODEL), D_FF ** -0.5)
    norm_f = 1.0 + nrm(ks[27], (D_MODEL,), 0.02)
    return {"x_prompt": x_prompt, "x_sample": x_sample,
            "state_s5_re": state_s5_re, "state_s5_im": state_s5_im,
            "norm1": norm1, "w_in": w_in, "lam_re": lam_re, "lam_im": lam_im,
            "log_dt": log_dt, "b_re": b_re, "b_im": b_im, "c_re": c_re, "c_im": c_im,
            "d_skip": d_skip, "w_glu": w_glu, "b_glu": b_glu,
            "cm_ln_g": cm_ln_g, "cm_ln_b": cm_ln_b, "w_s": w_s, "b_s": b_s,
            "g_s5": g_s5, "g_cm": g_cm, "w_out": w_out, "norm2": norm2,
            "w_gate": w_gate, "w_up": w_up, "w_down": w_down, "norm_f": norm_f}


def reference(x_prompt, x_sample, state_s5_re, state_s5_im, norm1, w_in, lam_re, lam_im,
              log_dt, b_re, b_im, c_re, c_im, d_skip, w_glu, b_glu, cm_ln_g, cm_ln_b,
              w_s, b_s, g_s5, g_cm, w_out, norm2, w_gate, w_up, w_down, norm_f):
    def layer(x, h0r, h0i, l):
        h = rmsnorm(x, norm1[l])
        p = h @ w_in[l]
        o_s5, hr, hi = s5_mixer(p[..., :S5_WIDTH], h0r, h0i, lam_re[l], lam_im[l], log_dt[l],
                                b_re[l], b_im[l], c_re[l], c_im[l], d_skip[l], w_glu[l], b_glu[l])
        o_cm, v_rows = chunk_mixer(p[..., S5_WIDTH:], cm_ln_g[l], cm_ln_b[l], w_s[l], b_s[l])
        mix = jnp.concatenate([rmsnorm(o_s5, g_s5[l]), rmsnorm(o_cm, g_cm[l])], axis=-1)
        x = x + mix @ w_out[l]
        x = x + swiglu(rmsnorm(x, norm2[l]), w_gate[l], w_up[l], w_down[l])
        return x, hr, hi, v_rows

    xp, xs = x_prompt, x_sample
    zeros_state = jnp.zeros((BATCH, S5_GROUPS, S5_STATE), jnp.float32)
    pr, pi_, pv, sr, si, sv = [], [], [], [], [], []
    for l in range(DEPTH):
        xp, hr, hi, vr = layer(xp, zeros_state, zeros_state, l)
        pr.append(hr); pi_.append(hi); pv.append(vr)
        xs, hr, hi, vr = layer(xs, state_s5_re[l], state_s5_im[l], l)
        sr.append(hr); si.append(hi); sv.append(vr)
    y_prompt = rmsnorm(xp, norm_f)
    y_sample = rmsnorm(xs, norm_f)
    return (y_prompt, y_sample, jnp.stack(pr), jnp.stack(pi_), jnp.stack(pv),
            jnp.stack(sr), jnp.stack(si), jnp.stack(sv))
```

```python
import os
import numpy as np
from contextlib import ExitStack
import concourse.bass as bass
import concourse.mybir as mybir
from concourse.bass_utils import run_bass_kernel_spmd

F32 = mybir.dt.float32
BF16 = mybir.dt.bfloat16
AF = mybir.ActivationFunctionType
ALU = mybir.AluOpType

NT = 17
TOK = 2176
NFF = 22
EPS = 1e-6
PI = float(np.pi)
TWO_PI = 2.0 * PI
MAGIC = 12582912.0
CW_C1 = 6.28125
CW_C2 = TWO_PI - 6.28125
PI_LO = 3.1415925

C_ID = 0
C_TM = 128
C_CM = 256
C_IO = 384
C_EPS = 640
C_EPS4 = 641
C_TAU = 642
C_ONE = 659
C_PM = 723
C_HPI = 725
CW = 726

KDEBUG = os.environ.get("KDEBUG", "")


def make_consts():
    c = np.zeros((128, CW), np.float32)
    c[:, C_ID:C_ID + 128] = np.eye(128, dtype=np.float32)
    p = np.arange(128)
    c[:, C_TM:C_TM + 128] = (p[None, :] // 16 >= p[:, None] // 16).astype(np.float32)
    c[:, C_CM:C_CM + 128] = (p[None, :] >= p[:, None]).astype(np.float32)
    c[:, C_IO:C_IO + 256] = np.arange(256, dtype=np.float32)[None, :]
    c[:, C_EPS] = EPS
    c[:, C_EPS4] = 4 * EPS
    c[:, C_TAU:C_TAU + 17] = np.array(list(range(9)) + list(range(7, -1, -1)), np.float32)[None, :]
    c[:, C_ONE:C_ONE + 64] = 1.0
    c[0:64, C_PM] = 1.0
    c[64:128, C_PM + 1] = 1.0
    c[:, C_HPI] = PI / 2
    return c


class Res:
    __slots__ = ("name", "w", "rs")

    def __init__(self, name):
        self.name = name
        self.w = None
        self.rs = {}


class Eng:
    def __init__(self, name, sem):
        self.name = name
        self.sem = sem
        self.cnt = 0
        self.waited = {}
        self.ops = []


class DSem:
    def __init__(self, sem):
        self.sem = sem
        self.cnt = 0
        self.last = None


class _Single:
    def __init__(self, fn):
        self.fn = fn

    def __call__(self, e):
        return self.fn(e)


class Prog:
    def __init__(self, nc, es, ndma=40):
        self.nc = nc
        self.E = {n: Eng(n, es.enter_context(nc.semaphore("sem_" + n))) for n in ("pe", "act", "dve", "pool", "sp")}
        self.dsems_hw = [DSem(es.enter_context(nc.semaphore("dq%d" % i))) for i in range(ndma)]
        self.dsems_sw = [DSem(es.enter_context(nc.semaphore("dw%d" % i))) for i in range(16)]
        self.dsems = self.dsems_hw + self.dsems_sw
        self.rr = 0
        self.rr_sw = 0
        self.res = {}
        self.out_toks = []
        self.limit = None
        self.nops = 0
        self.lazy = []
        self.last_ds = None

    def r(self, *key):
        x = self.res.get(key)
        if x is None:
            x = Res(key)
            self.res[key] = x
        return x

    def _deps(self, eng, reads, writes, is_dma):
        need = {}

        def add(tok, kind):
            if tok is None:
                return
            sem, val, teng, tdma = tok
            if not tdma and not is_dma and teng == eng:
                if eng == "pe":
                    return
                if kind != "raw" and eng in os.environ.get("KRAWONLY", "").split(","):
                    return
            k = id(sem)
            cur = need.get(k)
            if cur is None or cur[1] < val:
                need[k] = (sem, val)

        for r in reads:
            add(r.w, "raw")
        for w in writes:
            add(w.w, "waw")
            for t in w.rs.values():
                add(t, "war")
        return need

    def _finish(self, E, need, fn, inc, tok, reads, writes):
        waits = []
        for k, (sem, val) in need.items():
            if E.waited.get(k, 0) < val:
                E.waited[k] = val
                waits.append((sem, val))
        E.ops.append((waits, fn, inc))
        k = id(tok[0])
        for r in reads:
            cur = r.rs.get(k)
            if cur is None or cur[1] < tok[1]:
                r.rs[k] = tok
        for w in writes:
            w.w = tok
            w.rs = {}
        return tok

    def op(self, eng, fn, reads=(), writes=()):
        if eng != "pe":
            fn = _Single(fn)
        self.nops += 1
        if self.limit is not None and self.nops > self.limit:
            return None
        E = self.E[eng]
        need = self._deps(eng, reads, writes, False)
        E.cnt += 1
        tok = (E.sem, E.cnt, eng, False)
        return self._finish(E, need, fn, (E.sem, 1), tok, reads, writes)

    def dma(self, q, fns, reads=(), writes=(), out=False):
        if not isinstance(fns, (list, tuple)):
            fns = [fns]
        self.nops += 1
        if self.limit is not None and self.nops > self.limit and not out:
            return None
        E = self.E[q]
        need = self._deps(q, reads, writes, True)
        if q == "pool":
            ds = self.dsems_sw[self.rr_sw % len(self.dsems_sw)]
            self.rr_sw += 1
        else:
            ds = self.dsems_hw[self.rr % len(self.dsems_hw)]
            self.rr += 1
        if ds.last is not None:
            k = id(ds.sem)
            cur = need.get(k)
            if cur is None or cur[1] < ds.last[1]:
                need[k] = (ds.sem, ds.last[1])
        ds.cnt += 16 * len(fns)
        tok = (ds.sem, ds.cnt, q, True)
        ds.last = tok
        self.last_ds = ds
        if ds in self.lazy:
            self.lazy.remove(ds)

        def fn(e, fns=fns, sem=ds.sem):
            last = None
            for f in fns:
                last = f(e)
                last.then_inc(sem, 16)
            return None

        self._finish(E, need, fn, None, tok, reads, writes)
        if out:
            self.out_toks.append(tok)
        return tok

    def wait_all_dma(self, q="sp"):
        E = self.E[q]
        waits = []
        for ds in self.dsems:
            if ds in self.lazy:
                continue
            if ds.last is not None and E.waited.get(id(ds.sem), 0) < ds.last[1]:
                E.waited[id(ds.sem)] = ds.last[1]
                waits.append((ds.sem, ds.last[1]))
        if waits:
            E.ops.append((waits, None, None))

    def emit(self):
        self.wait_all_dma("sp")
        with self.nc.Block() as block:
            regs = (("pe", block.tensor), ("act", block.scalar), ("dve", block.vector),
                    ("pool", block.gpsimd), ("sp", block.sync))
            for name, reg in regs:
                E = self.E[name]
                ops = E.ops
                E.ops = []
                if not ops:
                    continue

                def f(e, ops=ops):
                    for waits, fn, inc in ops:
                        attach = None
                        if isinstance(fn, _Single) and waits:
                            attach = waits[-1]
                            waits = waits[:-1]
                        for sem, val in waits:
                            e.wait_ge(sem, val)
                        if fn is None:
                            continue
                        ins = fn(e)
                        if attach is not None:
                            ins._wait_ge(attach[0], attach[1])
                        if inc is not None:
                            ins.then_inc(inc[0], inc[1])

                reg(f)


def build(debug=""):
    nc = bass.Bass("TRN2", target_bir_lowering=False)
    D = {}

    def din(name, shape):
        D[name] = nc.dram_tensor(name, list(shape), F32, kind="ExternalInput").ap()

    def dout(name, shape):
        D[name] = nc.dram_tensor(name, list(shape), F32, kind="ExternalOutput").ap()

    din("xp", (2048, 1024)); din("xs", (128, 1024)); din("h0r", (16, 2048)); din("h0i", (16, 2048))
    din("norm1", (1, 1024)); din("w_in", (1024, 1536)); din("lam_re", (32, 64)); din("lam_im", (32, 64))
    din("log_dt", (1, 32)); din("b_re", (2048, 16)); din("b_im", (2048, 16)); din("c_re", (512, 64)); din("c_im", (512, 64))
    din("d_skip", (1, 512)); din("w_glu", (512, 512)); din("b_glu", (1, 512)); din("cm_ln_g", (1, 512)); din("cm_ln_b", (1, 512))
    din("w_s", (1024, 128)); din("b_s", (8, 128)); din("g_s5", (1, 512)); din("g_cm", (1, 512)); din("w_out", (1024, 1024))
    din("norm2", (1, 1024)); din("w_gate", (1024, 2816)); din("w_up", (1024, 2816)); din("w_down", (2816, 1024)); din("norm_f", (1, 1024))
    din("consts", (128, CW))
    dout("yp", (2048, 1024)); dout("ys", (128, 1024)); dout("pr", (32, 64)); dout("pi", (32, 64)); dout("pv", (128, 512))
    dout("sr", (16, 2048)); dout("si", (16, 2048)); dout("sv", (128, 512))
    scr1 = nc.dram_tensor("scr1", [TOK], F32, kind="Internal").ap()
    scr2 = nc.dram_tensor("scr2", [TOK], F32, kind="Internal").ap()
    dbg = {}

    def xsrc(t):
        return D["xp"][t * 128:(t + 1) * 128, :] if t < 16 else D["xs"][:, :]

    def ydst(t):
        return D["yp"][t * 128:(t + 1) * 128, :] if t < 16 else D["ys"][:, :]

    with ExitStack() as top:
        P = Prog(nc, top)
        r = P.r

        def sb(es, name, shape, dt=F32):
            return es.enter_context(nc.sbuf_tensor(name, list(shape), dt))

        PS = [top.enter_context(nc.psum_tensor("ps%d" % i, [128, 512], F32)) for i in range(8)]
        PSB = [p[:].bitcast(BF16) for p in PS]

        def psr(i):
            return r("ps", i)

        cst = sb(top, "cst", (128, CW))
        identb = sb(top, "identb", (128, 128), BF16)
        onesb = sb(top, "onesb", (1, 128), BF16)
        actA = sb(top, "actA", (128, 8, TOK), BF16)
        mixT = sb(top, "mixT", (128, 8, TOK), BF16)
        mixs5 = mixT[:, 0:4, :]
        mixcm = mixT[:, 4:8, :]
        ssq1 = sb(top, "ssq1", (128, NT)); rstd1 = sb(top, "rstd1", (128, NT))
        ssqcm = sb(top, "ssqcm", (128, NT)); rstdcm = sb(top, "rstdcm", (128, NT))
        ssq5 = sb(top, "ssq5", (128, 24)); ssq5n = sb(top, "ssq5n", (128, NT)); rstd5 = sb(top, "rstd5", (128, NT))
        H0T = [sb(top, "H0T%d" % i, (128, 16, 16)) for i in range(2)]
        lr = sb(top, "lr", (128, 16)); li = sb(top, "li", (128, 16)); ldt = sb(top, "ldt", (128, 16))
        Br = sb(top, "Br", (128, 16, 16)); Bi = sb(top, "Bi", (128, 16, 16))
        Cn = [sb(top, "Cn%d" % i, (128, 4, 64)) for i in range(2)]
        dcol = sb(top, "dcol", (128, 32))
        mid = ExitStack()
        P8 = sb(mid, "P8", (128, 3, 4096), BF16)
        identf = cst[:, C_ID:C_ID + 128]
        epsc = cst[:, C_EPS:C_EPS + 1]
        eps4c = cst[:, C_EPS4:C_EPS4 + 1]

        P.dma("sp", lambda e: e.dma_start(out=cst[:], in_=D["consts"][:, :]), writes=[r("cst")])
        def _ncd(e, **kw):
            with nc.allow_non_contiguous_dma(reason="small strided param load"):
                return e.dma_start(**kw)
        def load_s5_params():
            P.dma("sp", [lambda e: _ncd(e, out=lr[:], in_=D["lam_re"].rearrange("(P e) n -> (e n) P", e=2)),
                          lambda e: _ncd(e, out=li[:], in_=D["lam_im"].rearrange("(P e) n -> (e n) P", e=2)),
                          lambda e: _ncd(e, out=ldt[0:64, :], in_=D["log_dt"][0:1, 0:32:2].partition_broadcast(64)),
                          lambda e: _ncd(e, out=ldt[64:128, :], in_=D["log_dt"][0:1, 1:32:2].partition_broadcast(64))],
                  writes=[r("lam")])
            P.dma("sp", [lambda e: _ncd(e, out=Br[:], in_=D["b_re"].rearrange("(P e n) i -> (e n) P i", e=2, n=64)),
                          lambda e: _ncd(e, out=Bi[:], in_=D["b_im"].rearrange("(P e n) i -> (e n) P i", e=2, n=64)),
                          lambda e: _ncd(e, out=Cn[0][:], in_=D["c_re"].rearrange("(t r) n -> r t n", r=128)),
                          lambda e: _ncd(e, out=Cn[1][:], in_=D["c_im"].rearrange("(t r) n -> r t n", r=128))],
                  writes=[r("BC")])
            P.dma("sp", [(lambda s_: (lambda e: _ncd(e, out=dcol[16 * s_:16 * s_ + 16, :], in_=D["d_skip"][0:1, :].rearrange("o (g i) -> (o i) g", i=16))))(s_)
                          for s_ in range(8)], writes=[r("dcol")])
        P.op("dve", lambda e: e.tensor_copy(out=identb[:], in_=identf), reads=[r("cst")], writes=[r("identb")])
        P.op("dve", lambda e: e.memset(onesb[:], 1.0), writes=[r("onesb")])
        P.op("dve", lambda e: e.memset(ssq5[:], 0.0), writes=[r("ssq5")])

        def pe_transposes(e, out_bf, srcs, npart=128):
            last = None
            for i, s in enumerate(srcs):
                last = e.transpose(out_bf[i], s, identb[0:npart, 0:npart])
            return last

        with ExitStack() as es:
            win = sb(es, "win", (128, 8, 1536), BF16)
            srt1 = sb(es, "srt1", (128, NT)); rstd8 = sb(es, "rstd8", (128, 24))
            g1bc = sb(es, "g1bc", (128, 1024)); lngbc = sb(es, "lngbc", (128, 512)); lnbbc = sb(es, "lnbbc", (128, 512))
            gcmbc = sb(es, "gcmbc", (128, 512))
            biasP = sb(es, "biasP", (128, 512)); biasS = sb(es, "biasS", (128, 512))
            bsT = sb(es, "bsT", (128, 8)); bsTs = sb(es, "bsTs", (128, 8))
            WT = sb(es, "WT", (128, 8, 128), BF16); WTs = sb(es, "WTs", (128, 8, 128), BF16)
            xt = [sb(es, "xt%d" % i, (128, 1024)) for i in range(2)]
            hb = [sb(es, "hb%d" % i, (128, 1024), BF16) for i in range(2)]
            junk = sb(es, "junk", (128, 1024), BF16)
            wsn = junk[:].rearrange("p (h s) -> p h s", h=8)
            NSL = 8
            ut = sb(es, "ut", (128, NSL, 512)); vt = sb(es, "vt", (128, NSL, 512))
            vbf = sb(es, "vbf", (128, NSL, 512), BF16)
            tmpc = [sb(es, "tmpc0", (128, 512))] * 2
            vout = tmpc
            ob = [sb(es, "ob%d" % i, (128, 512), BF16) for i in range(4)]
            st6 = sb(es, "st6", (128, NT, 6)); mv = sb(es, "mv", (128, NT, 2))
            sdln = sb(es, "sdln", (128, NT)); rsln = sb(es, "rsln", (128, NT))

            P.dma("pool", [lambda e: e.dma_start(out=wsn, in_=D["w_s"].rearrange("(h t) s -> t h s", h=8))], writes=[r("junk")])
            wv = D["w_in"].rearrange("(kt p) n -> p kt n", p=128)
            for k_ in range(8):
                P.dma("pool", [lambda e, k_=k_: e.dma_start(out=win[:, k_, 512:1536], in_=wv[:, k_, 512:1536])], writes=[r("win_uv", k_)])
            P.dma("pool", [lambda e: e.dma_start(out=win[:, :, 0:512], in_=wv[:, :, 0:512])], writes=[r("win_s5")])
            P.dma("sp", [lambda e: e.dma_start(out=g1bc[:], in_=D["norm1"][0:1, :].partition_broadcast(128))], writes=[r("bc1")])
            for t_ in range(2):
                P.dma("sp", [lambda e, t_=t_: e.dma_start(out=xt[t_ % 2][:], in_=xsrc(t_))], writes=[r("xt", t_ % 2)])
            P.dma("sp", [lambda e: e.dma_start(out=lngbc[:], in_=D["cm_ln_g"][0:1, :].partition_broadcast(128)),
                         lambda e: e.dma_start(out=lnbbc[:], in_=D["cm_ln_b"][0:1, :].partition_broadcast(128)),
                         lambda e: e.dma_start(out=gcmbc[:], in_=D["g_cm"][0:1, :].partition_broadcast(128))],
                  writes=[r("bc")])

            def ld_bs(e):
                with nc.allow_non_contiguous_dma(reason="tiny bias transposes"):
                    last = e.dma_start(out=bsT[:], in_=D["b_s"].rearrange("h t -> t h"))
                return last
            P.dma("sp", [ld_bs], writes=[r("bsT")])

            def ld_bss(q):
                def f(e):
                    with nc.allow_non_contiguous_dma(reason="tiny bias transposes"):
                        return e.dma_start(out=bsTs[8 * q:8 * q + 8, :], in_=D["b_s"][:, 0:8].rearrange("h t -> t h"))
                return f
            P.dma("act", [ld_bss(q) for q in range(16)], writes=[r("bsTs")])
            P.op("dve", lambda e: e.tensor_copy(out=biasP[:].rearrange("p (h d) -> p h d", h=8), in_=bsT[:].unsqueeze(2).to_broadcast([128, 8, 64])),
                 reads=[r("bsT")], writes=[r("biasP")])
            P.op("dve", lambda e: e.tensor_copy(out=biasS[:].rearrange("p (h d) -> p h d", h=8), in_=bsTs[:].unsqueeze(2).to_broadcast([128, 8, 64])),
                 reads=[r("bsTs")], writes=[r("biasS")])
            P.op("pe", lambda e: pe_transposes(e, [PSB[0][:, h * 128:(h + 1) * 128] for h in range(8)], [wsn[:, h, :] for h in range(8)]),
                 reads=[r("junk"), r("identb")], writes=[psr(0)])
            P.op("dve", lambda e: e.tensor_tensor(out=WT[:], in0=PSB[0][:, 0:1024].rearrange("p (h t) -> p h t", h=8),
                                                  in1=cst[:, C_CM:C_CM + 128].unsqueeze(1).to_broadcast([128, 8, 128]), op=ALU.mult),
                 reads=[psr(0), r("cst")], writes=[r("WT")])
            def build_WTs():
                P.op("dve", lambda e: e.memset(WTs[:], 0.0), writes=[r("WTs")])
                P.dma("sp", [(lambda q: (lambda e: e.dma_start(out=WTs[8 * q:8 * q + 8, :, 8 * q:8 * q + 8], in_=WT[0:8, :, 0:8])))(q) for q in range(16)],
                      reads=[r("WT")], writes=[r("WTs")])

            groups = [[0, 1, 2, 3], [4, 5, 6, 7], [8, 9, 10, 11], [12, 13, 14, 15], [16]]
            slot_of = {}
            for gi, g in enumerate(groups):
                for j, t in enumerate(g):
                    slot_of[t] = (gi % 2) * 4 + j

            def stage_A(g):
                for t in g:
                    xs_ = xt[t % 2]; rx = r("xt", t % 2); hbt = hb[t % 2]; rhb = r("hb", t % 2); pb = t % 2
                    if t >= 2:
                        P.dma("sp", [lambda e, xs_=xs_, t=t: e.dma_start(out=xs_[:], in_=xsrc(t))], writes=[rx])
                    P.op("act", lambda e, xs_=xs_, t=t: e.activation(out=junk[:], in_=xs_[:], func=AF.Square, accum_out=ssq1[:, t:t + 1]),
                         reads=[rx], writes=[r("junk"), r("ssq1", t)])
                    P.op("dve", lambda e, xs_=xs_, hbt=hbt: e.tensor_tensor(out=hbt[:], in0=xs_[:], in1=g1bc[:], op=ALU.mult),
                         reads=[rx, r("bc1")], writes=[rhb])
                    P.op("pe", lambda e, hbt=hbt, pb=pb: pe_transposes(e, [PSB[pb][:, k * 128:(k + 1) * 128] for k in range(8)],
                                                                   [hbt[:, k * 128:(k + 1) * 128] for k in range(8)]),
                         reads=[rhb, r("identb")], writes=[psr(pb)])
                    P.op("dve", lambda e, t=t, pb=pb: e.tensor_copy(out=actA[:, :, t * 128:(t + 1) * 128],
                                                                  in_=PSB[pb][:, 0:1024].rearrange("p (k c) -> p k c", k=8)),
                         reads=[psr(pb)], writes=[r("hT", t)])
                c0, c1 = g[0], g[-1] + 1
                P.op("act", lambda e: e.activation(out=srt1[:, c0:c1], in_=ssq1[:, c0:c1], func=AF.Sqrt, bias=epsc, scale=1.0 / 1024),
                     reads=[r("ssq1", t) for t in g] + [r("cst")], writes=[r("srt1", c0)])
                P.op("dve", lambda e: e.reciprocal(out=rstd1[:, c0:c1], in_=srt1[:, c0:c1]), reads=[r("srt1", c0)], writes=[r("rstd1", t) for t in g])

            def stage_B1(g):
                for t in g:
                    sl = slot_of[t]; bu = 2 + 2 * (t % 2); bv = bu + 1

                    def mm(e, t=t, bank=bu, c0=512):
                        last = None
                        for k in range(8):
                            last = e.matmul(PS[bank][:, :], lhsT=actA[:, k, t * 128:(t + 1) * 128], rhs=win[:, k, c0:c0 + 512],
                                            start=(k == 0), stop=(k == 7))
                        return last
                    P.op("pe", lambda e, t=t, bu=bu: mm(e, t, bu, 512), reads=[r("hT", t)] + [r("win_uv", k_) for k_ in range(8)], writes=[psr(bu)])
                    P.op("pe", lambda e, t=t, bv=bv: mm(e, t, bv, 1024), reads=[r("hT", t)] + [r("win_uv", k_) for k_ in range(8)], writes=[psr(bv)])
                    P.op("act", lambda e, t=t, sl=sl, bu=bu: e.activation(out=ut[:, sl, :], in_=PS[bu][:, :], func=AF.Gelu_apprx_tanh, scale=rstd1[:, t:t + 1]),
                         reads=[psr(bu), r("rstd1", t)], writes=[r("ut", sl)])
                    P.op("act", lambda e, t=t, sl=sl, bv=bv: e.activation(out=vt[:, sl, :], in_=PS[bv][:, :], func=AF.Gelu_apprx_tanh, scale=rstd1[:, t:t + 1]),
                         reads=[psr(bv), r("rstd1", t)], writes=[r("vt", sl)])
                    P.op("dve", lambda e, t=t, sl=sl: e.bn_stats(out=st6[:, t, :], in_=vt[:, sl, :]), reads=[r("vt", sl)], writes=[r("st6", t)])
                    P.op("dve", lambda e, t=t: e.bn_aggr(out=mv[:, t, :], in_=st6[:, t, :]), reads=[r("st6", t)], writes=[r("mv", t)])

            def stage_LN(g):
                c0, c1 = g[0], g[-1] + 1
                P.op("act", lambda e: e.activation(out=sdln[:, c0:c1], in_=mv[:, c0:c1, 1], func=AF.Sqrt, bias=epsc, scale=1.0),
                     reads=[r("mv", t) for t in g] + [r("cst")], writes=[r("sdln", c0)])
                P.op("dve", lambda e: e.reciprocal(out=rsln[:, c0:c1], in_=sdln[:, c0:c1]), reads=[r("sdln", c0)], writes=[r("rsln", c0)])
                for t in g:
                    sl = slot_of[t]
                    P.op("dve", lambda e, t=t, sl=sl: e.tensor_scalar(out=vt[:, sl, :], in0=vt[:, sl, :], scalar1=mv[:, t, 0:1], scalar2=rsln[:, t:t + 1],
                                                                    op0=ALU.subtract, op1=ALU.mult),
                         reads=[r("vt", sl), r("mv", t), r("rsln", c0)], writes=[r("vt", sl)])
                    lne = "dve" if t % 2 == 0 else "pool"
                    P.op(lne, lambda e, sl=sl: e.tensor_tensor(out=vt[:, sl, :], in0=vt[:, sl, :], in1=lngbc[:], op=ALU.mult),
                         reads=[r("vt", sl), r("bc")], writes=[r("vt", sl)])
                    P.op(lne, lambda e, sl=sl: e.tensor_tensor(out=vbf[:, sl, :], in0=vt[:, sl, :], in1=lnbbc[:], op=ALU.add),
                         reads=[r("vt", sl), r("bc")], writes=[r("vbf", sl)])
                    if t >= 15:
                        vo = vout[t - 15]; dn = "pv" if t == 15 else "sv"
                        P.op("pool", lambda e, sl=sl, vo=vo: e.tensor_tensor(out=vo[:], in0=vt[:, sl, :], in1=lnbbc[:], op=ALU.add),
                             reads=[r("vt", sl), r("bc")], writes=[r("tmpc")])
                        P.dma("sp", [lambda e, vo=vo, dn=dn: e.dma_start(out=D[dn][:, :], in_=vo[:])], reads=[r("tmpc")], out=True)

            def stage_C1(g):
                for j_, t in enumerate(g):
                    sl = slot_of[t]; Wm = WT if t < 16 else WTs; bias = biasP if t < 16 else biasS
                    rW = r("WT") if t < 16 else r("WTs"); rb = r("biasP") if t < 16 else r("biasS")
                    tc_ = tmpc[0]; rtc = r("tmpc"); obt = ob[j_]; rob = r("ob", j_); mb = 6 + t % 2

                    def mm(e, sl=sl, Wm=Wm, mb=mb):
                        last = None
                        for h in range(8):
                            last = e.matmul(PS[mb][:, h * 64:(h + 1) * 64], lhsT=Wm[:, h, :], rhs=vbf[:, sl, h * 64:(h + 1) * 64], start=True, stop=True)
                        return last
                    P.op("pe", mm, reads=[r("vbf", sl), rW], writes=[psr(mb)])
                    P.op("dve", lambda e, tc_=tc_, bias=bias, mb=mb: e.tensor_tensor(out=tc_[:], in0=PS[mb][:, :], in1=bias[:], op=ALU.add),
                         reads=[psr(mb), rb], writes=[rtc])
                    P.op("dve", lambda e, tc_=tc_, sl=sl: e.tensor_tensor(out=ut[:, sl, :], in0=tc_[:], in1=ut[:, sl, :], op=ALU.mult),
                         reads=[rtc, r("ut", sl)], writes=[r("ut", sl)])
                    P.op("act", lambda e, t=t, sl=sl: e.activation(out=junk[:, 0:512], in_=ut[:, sl, :], func=AF.Square, accum_out=ssqcm[:, t:t + 1]),
                         reads=[r("ut", sl)], writes=[r("junk"), r("ssqcm", t)])
                    P.op("pool", lambda e, sl=sl, obt=obt: e.tensor_tensor(out=obt[:], in0=ut[:, sl, :], in1=gcmbc[:], op=ALU.mult),
                         reads=[r("ut", sl), r("bc")], writes=[rob])

            def stage_C2(g):
                for j_, t in enumerate(g):
                    obt = ob[j_]; rob = r("ob", j_); tb = t % 2
                    P.op("pe", lambda e, obt=obt, tb=tb: pe_transposes(e, [PSB[tb][:, j * 128:(j + 1) * 128] for j in range(4)],
                                                                     [obt[:, j * 128:(j + 1) * 128] for j in range(4)]),
                         reads=[rob, r("identb")], writes=[psr(tb)])
                    P.op("act", lambda e, t=t, tb=tb: e.activation(out=mixcm[:, :, t * 128:(t + 1) * 128], in_=PSB[tb][:, 0:512].rearrange("p (j c) -> p j c", j=4),
                                                                   func=AF.Copy),
                         reads=[psr(tb)], writes=[r("mixcm", t)])

            stage_A(groups[0])
            NG = len(groups)
            for gi, g in enumerate(groups):
                if gi + 1 < NG:
                    stage_A(groups[gi + 1])
                if gi == 1:
                    build_WTs()
                if gi == NG - 2:
                    load_s5_params()
                stage_B1(g)
                if gi >= 2:
                    stage_C2(groups[gi - 2])
                if gi >= 1:
                    stage_C1(groups[gi - 1])
                stage_LN(g)
            if NG >= 2:
                stage_C2(groups[NG - 2])
            stage_C1(groups[NG - 1])
            stage_C2(groups[NG - 1])

            def st_rs(e):
                with nc.allow_non_contiguous_dma(reason="tiny stat relayout"):
                    return e.dma_start(out=scr1.rearrange("(t p) -> p t", p=128), in_=rstd1[:])
            P.dma("act", [st_rs], reads=[r("rstd1", t) for t in range(NT)], writes=[r("scr1")])

            P.dma("act", [lambda e: e.dma_start(out=rstd8[:, 0:16].rearrange("p (a s) -> p a s", a=2),
                                               in_=scr1[0:2048].rearrange("(a b s) -> b a s", a=2, s=8)),
                         lambda e: e.dma_start(out=rstd8[0:16, 16:24], in_=scr1[2048:2176].rearrange("(q s) -> q s", s=8))],
                  reads=[r("scr1")], writes=[r("rstd8")])
            P8v = P8[:].rearrange("p a (g s i) -> p a g s i", g=32, s=8)
            for bt in range(3):
                npart = 128 if bt < 2 else 16
                for s in range(8):
                    idx = bt * 8 + s; bank = 2 + (idx % 4)
                    if bt < 2:
                        tiles = range(bt * 8, bt * 8 + 8)
                        base = bt * 1024 + s; end = bt * 1024 + 1024
                    else:
                        tiles = [16]
                        base = 2048 + s; end = 2176

                    def mm(e, bank=bank, base=base, end=end, npart=npart):
                        last = None
                        for k in range(8):
                            last = e.matmul(PS[bank][0:npart, :], lhsT=actA[:, k, base:end:8], rhs=win[:, k, 0:512], start=(k == 0), stop=(k == 7))
                        return last
                    P.op("pe", mm, reads=[r("hT", t) for t in tiles] + [r("win_s5")], writes=[psr(bank)])
                    eng = "act" if s % 2 == 0 else "dve"
                    if eng == "act":
                        P.op("act", lambda e, bank=bank, bt=bt, s=s, idx=idx, npart=npart: e.activation(
                            out=P8v[0:npart, bt, :, s, :], in_=PS[bank][0:npart, :].rearrange("p (g i) -> p g i", g=32), func=AF.Copy,
                            scale=rstd8[0:npart, idx:idx + 1]), reads=[psr(bank), r("rstd8")], writes=[r("P8", bt, s)])
                    else:
                        P.op("dve", lambda e, bank=bank, bt=bt, s=s, idx=idx, npart=npart: e.tensor_scalar(
                            out=P8v[0:npart, bt, :, s, :], in0=PS[bank][0:npart, :].rearrange("p (g i) -> p g i", g=32),
                            scalar1=rstd8[0:npart, idx:idx + 1], scalar2=None, op0=ALU.mult), reads=[psr(bank), r("rstd8")], writes=[r("P8", bt, s)])
            if "ab" in debug:
                dbg["hT"] = nc.dram_tensor("dbg_hT", [128, 8 * TOK], BF16, kind="ExternalOutput").ap()
                dbg["mixcm"] = nc.dram_tensor("dbg_mixcm", [128, 4, TOK], BF16, kind="ExternalOutput").ap()
                dbg["P8"] = nc.dram_tensor("dbg_P8", [128, 3 * 4096], BF16, kind="ExternalOutput").ap()
                P.dma("sp", [lambda e: e.dma_start(out=dbg["hT"][:, :], in_=actA[:].rearrange("p k t -> p (k t)"))], reads=[r("hT", t) for t in range(NT)], out=True)
                P.dma("sp", [lambda e: e.dma_start(out=dbg["mixcm"][:, :, :], in_=mixT[:, 4:8, :])], reads=[r("mixcm", t) for t in range(NT)], out=True)
                P.dma("sp", [lambda e: e.dma_start(out=dbg["P8"][:, :], in_=P8[:].rearrange("p a c -> p (a c)"))], reads=[r("P8", bt, s) for bt in range(3) for s in range(8)], out=True)
            P.emit()
        if debug == "ab":
            mid.close()
            return nc

        scrF = actA[:].rearrange("p k t -> p (k t)").bitcast(F32)
        scrB = actA[:].rearrange("p k t -> p (k t)")

        def dv(fn, reads, writes, eng="dve"):
            return P.op(eng, fn, reads=[r(*x) if isinstance(x, tuple) else r(x) for x in reads],
                        writes=[r(*x) if isinstance(x, tuple) else r(x) for x in writes])

        def ncdma(e, **kw):
            with nc.allow_non_contiguous_dma(reason="small strided param load"):
                return e.dma_start(**kw)

        W1re = sb(mid, "W1re", (128, 32, 128), BF16); W1im = sb(mid, "W1im", (128, 32, 128), BF16)
        Tm = sb(mid, "Tm", (128, 32, 128), BF16)
        W2rb = sb(mid, "W2rb", (128, 16, 128), BF16); W2ib = sb(mid, "W2ib", (128, 16, 128), BF16)
        s5c = sb(mid, "s5c", (128, 4, 16))
        wglu = sb(mid, "wglu", (128, 4, 512), BF16)
        bglub = sb(mid, "bglub", (1, 512), BF16)
        gs5bc = sb(mid, "gs5bc", (128, 512))
        with ExitStack() as es:
            NTAU = 17
            dtt = sb(es, "dtt", (128, 16))
            lrdt = sb(es, "lrdt", (128, 16)); lidt = sb(es, "lidt", (128, 16))
            CTn = [sb(es, "CTn%d" % i, (64, 512)) for i in range(2)]
            CT = [sb(es, "CT%d" % i, (128, 16, 16)) for i in range(2)]
            tabs = {n: sb(es, "tab_" + n, (128, NTAU, 16)) for n in
                    ("ARGM", "ANG", "MAGP", "MAGM", "SIN", "COS", "APR", "API", "AMR", "AMI", "RS", "RC", "K", "Y")}
            sm = {n: sb(es, "sm_" + n, (128, 16)) for n in ("am1", "den", "t", "rden", "qr", "qi", "u1", "u2")}
            QB = [sb(es, "QB%d" % i, (128, 16, 16)) for i in range(2)]
            big = {n: sb(es, "big_" + n, (128, 16, 8, 16)) for n in ("W2r", "W2i")}
            for k_, n in enumerate(("Lr", "Li", "t1", "t2")):
                big[n] = scrF[:, k_ * 2048:(k_ + 1) * 2048].rearrange("p (P s o) -> p P s o", P=16, s=8)
            big["M1r"] = big["Lr"]; big["M1i"] = big["Li"]
            tmpT = [sb(es, "tmpT%d" % i, (128, 4, 128)) for i in range(2)]
            Lm = [sb(es, "Lm%d" % i, (128, 4, 2, 128)) for i in range(2)]
            bglu32 = sb(es, "bglu32", (1, 512))

            if os.environ.get("KSTOP"):
                P.nops = 0
                P.limit = int(os.environ["KSTOP"])
            for _ in range(int(os.environ.get("KPAD", "0"))):
                P.E["sp"].ops.append(([(P.E["sp"].sem, 0)], None, None))
            P.dma("pool", [lambda e: e.dma_start(out=wglu[:], in_=D["w_glu"].rearrange("(kt p) n -> p kt n", p=128)),
                           lambda e: e.dma_start(out=bglub[:], in_=D["b_glu"][0:1, :])], writes=[r("wglu")])
            P.dma("pool", [lambda e: e.dma_start(out=gs5bc[:], in_=D["g_s5"][0:1, :].partition_broadcast(128))], writes=[r("gs5bc")])

            T = tabs
            dv(lambda e: e.activation(out=dtt[:], in_=ldt[:], func=AF.Exp), ["lam"], ["dtt"], "act")
            dv(lambda e: e.tensor_tensor(out=lrdt[:], in0=lr[:], in1=dtt[:], op=ALU.mult), ["lam", "dtt"], ["lrdt"])
            dv(lambda e: e.tensor_tensor(out=lidt[:], in0=li[:], in1=dtt[:], op=ALU.mult), ["lam", "dtt"], ["lidt"])
            taub = cst[:, C_TAU:C_TAU + NTAU].unsqueeze(2).to_broadcast([128, NTAU, 16])
            dv(lambda e: e.tensor_tensor(out=T["ARGM"][:], in0=taub, in1=lrdt[:].unsqueeze(1).to_broadcast([128, NTAU, 16]), op=ALU.mult),
               ["cst", "lrdt"], ["ARGM"])
            dv(lambda e: e.tensor_tensor(out=T["ANG"][:], in0=taub, in1=lidt[:].unsqueeze(1).to_broadcast([128, NTAU, 16]), op=ALU.mult),
               ["cst", "lidt"], ["ANG"])
            dv(lambda e: e.activation(out=T["MAGP"][:], in_=T["ARGM"][:], func=AF.Exp), ["ARGM"], ["MAGP"], "act")
            dv(lambda e: e.activation(out=T["MAGM"][:], in_=T["ARGM"][:], func=AF.Exp, scale=-1.0), ["ARGM"], ["MAGM"], "act")

            def range_reduce(dst, src, shift, rn_dst, rn_src, Y, K, rY, rK, eng="dve"):
                if shift != 0.0:
                    dv(lambda e: e.tensor_scalar_add(out=Y, in0=src, scalar1=shift), [rn_src], [rY], eng)
                    y = Y; ry = rY
                else:
                    y = src; ry = rn_src
                dv(lambda e: e.tensor_scalar(out=K, in0=y, scalar1=1.0 / TWO_PI, scalar2=MAGIC, op0=ALU.mult, op1=ALU.add), [ry], [rK], eng)
                dv(lambda e: e.tensor_scalar_add(out=K, in0=K, scalar1=-MAGIC), [rK], [rK], eng)
                dv(lambda e: e.scalar_tensor_tensor(out=dst, in0=K, scalar=-CW_C1, in1=y, op0=ALU.mult, op1=ALU.add), [rK, ry], [rn_dst], "dve")
                dv(lambda e: e.scalar_tensor_tensor(out=dst, in0=K, scalar=-CW_C2, in1=dst, op0=ALU.mult, op1=ALU.add), [rK, rn_dst], [rn_dst], "dve")
                dv(lambda e: e.tensor_scalar(out=dst, in0=dst, scalar1=PI_LO, scalar2=-PI_LO, op0=ALU.min, op1=ALU.max), [rn_dst], [rn_dst], eng)

            range_reduce(T["RS"][:], T["ANG"][:], 0.0, "RS", "ANG", T["Y"][:], T["K"][:], "Y", "K")
            dv(lambda e: e.activation(out=T["SIN"][:], in_=T["RS"][:], func=AF.Sin), ["RS"], ["SIN"], "act")
            range_reduce(T["RC"][:], T["ANG"][:], PI / 2, "RC", "ANG", T["Y"][:], T["K"][:], "Y", "K")
            dv(lambda e: e.activation(out=T["COS"][:], in_=T["RC"][:], func=AF.Sin), ["RC"], ["COS"], "act")
            dv(lambda e: e.tensor_tensor(out=T["APR"][:], in0=T["MAGP"][:], in1=T["COS"][:], op=ALU.mult), ["MAGP", "COS"], ["APR"])
            dv(lambda e: e.tensor_tensor(out=T["API"][:], in0=T["MAGP"][:], in1=T["SIN"][:], op=ALU.mult), ["MAGP", "SIN"], ["API"])
            dv(lambda e: e.tensor_tensor(out=T["AMR"][:], in0=T["MAGM"][:], in1=T["COS"][:], op=ALU.mult), ["MAGM", "COS"], ["AMR"])
            dv(lambda e: e.scalar_tensor_tensor(out=T["AMI"][:], in0=T["MAGM"][:], scalar=-1.0, in1=T["SIN"][:], op0=ALU.mult, op1=ALU.mult),
               ["MAGM", "SIN"], ["AMI"])
            dv(lambda e: e.tensor_copy(out=s5c[:, 0, :], in_=T["MAGP"][:, 8, :]), ["MAGP"], ["s5c"])
            dv(lambda e: e.tensor_copy(out=s5c[:, 1, :], in_=T["RS"][:, 8, :]), ["RS"], ["s5c"])
            dv(lambda e: e.tensor_copy(out=s5c[:, 2, :], in_=T["APR"][:, 8, :]), ["APR"], ["s5c"])
            dv(lambda e: e.tensor_copy(out=s5c[:, 3, :], in_=T["API"][:, 8, :]), ["API"], ["s5c"])
            ar1 = T["APR"][:, 1, :]; ai1 = T["API"][:, 1, :]
            dv(lambda e: e.tensor_scalar_add(out=sm["am1"][:], in0=ar1, scalar1=-1.0), ["APR"], ["am1"])
            dv(lambda e: e.tensor_tensor(out=sm["den"][:], in0=lr[:], in1=lr[:], op=ALU.mult), ["lam"], ["den"])
            dv(lambda e: e.tensor_tensor(out=sm["t"][:], in0=li[:], in1=li[:], op=ALU.mult), ["lam"], ["t"])
            dv(lambda e: e.tensor_tensor(out=sm["den"][:], in0=sm["den"][:], in1=sm["t"][:], op=ALU.add), ["den", "t"], ["den"])
            dv(lambda e: e.reciprocal(out=sm["rden"][:], in_=sm["den"][:]), ["den"], ["rden"])
            dv(lambda e: e.tensor_tensor(out=sm["u1"][:], in0=sm["am1"][:], in1=lr[:], op=ALU.mult), ["am1", "lam"], ["u1"])
            dv(lambda e: e.tensor_tensor(out=sm["u2"][:], in0=ai1, in1=li[:], op=ALU.mult), ["API", "lam"], ["u2"])
            dv(lambda e: e.tensor_tensor(out=sm["u1"][:], in0=sm["u1"][:], in1=sm["u2"][:], op=ALU.add), ["u1", "u2"], ["u1"])
            dv(lambda e: e.tensor_tensor(out=sm["qr"][:], in0=sm["u1"][:], in1=sm["rden"][:], op=ALU.mult), ["u1", "rden"], ["qr"])
            dv(lambda e: e.tensor_tensor(out=sm["u1"][:], in0=ai1, in1=lr[:], op=ALU.mult), ["API", "lam", "qr"], ["u1"])
            dv(lambda e: e.tensor_tensor(out=sm["u2"][:], in0=sm["am1"][:], in1=li[:], op=ALU.mult), ["am1", "lam"], ["u2"])
            dv(lambda e: e.tensor_tensor(out=sm["u1"][:], in0=sm["u1"][:], in1=sm["u2"][:], op=ALU.subtract), ["u1", "u2"], ["u1"])
            dv(lambda e: e.tensor_tensor(out=sm["qi"][:], in0=sm["u1"][:], in1=sm["rden"][:], op=ALU.mult), ["u1", "rden"], ["qi"])
            qrb = sm["qr"][:].unsqueeze(2).to_broadcast([128, 16, 16]); qib = sm["qi"][:].unsqueeze(2).to_broadcast([128, 16, 16])
            t1s = big["t1"][:, :, 0, :]; t2s = big["t2"][:, :, 0, :]
            dv(lambda e: e.tensor_tensor(out=t1s, in0=Br[:], in1=qrb, op=ALU.mult), ["BC", "qr"], ["t1"])
            dv(lambda e: e.tensor_tensor(out=t2s, in0=Bi[:], in1=qib, op=ALU.mult), ["BC", "qi"], ["t2"])
            dv(lambda e: e.tensor_tensor(out=QB[0][:], in0=t1s, in1=t2s, op=ALU.subtract), ["t1", "t2"], ["QB0"])
            dv(lambda e: e.tensor_tensor(out=t1s, in0=Bi[:], in1=qrb, op=ALU.mult), ["BC", "qr", "QB0"], ["t1"])
            dv(lambda e: e.tensor_tensor(out=t2s, in0=Br[:], in1=qib, op=ALU.mult), ["BC", "qi", "QB0"], ["t2"])
            dv(lambda e: e.tensor_tensor(out=QB[1][:], in0=t1s, in1=t2s, op=ALU.add), ["t1", "t2"], ["QB1"])
            for c_ in range(2):
                def mmct(e, c_=c_):
                    last = None
                    for t_ in range(4):
                        last = e.matmul(PS[c_][0:64, t_ * 128:(t_ + 1) * 128], lhsT=Cn[c_][:, t_, :], rhs=identf, start=True, stop=True)
                    return last
                P.op("pe", mmct, reads=[r("BC"), r("cst")], writes=[psr(c_)])
                dv(lambda e, c_=c_: e.tensor_copy(out=CTn[c_][:], in_=PS[c_][0:64, :]), [("ps", c_)], [("CTn", c_)])
                ctv = CTn[c_][:].rearrange("n (P e o) -> n P e o", e=2, o=16)
                P.dma("act", [lambda e, c_=c_, ctv=ctv: e.dma_start(out=CT[c_][0:64, :, :], in_=ctv[:, :, 0, :]),
                             lambda e, c_=c_, ctv=ctv: e.dma_start(out=CT[c_][64:128, :, :], in_=ctv[:, :, 1, :])],
                      reads=[r("CTn", c_)], writes=[r("CT", c_)])

            def tauv(name, lo):
                return T[name][:].rearrange("p t P -> p P t")[:, :, lo:lo + 8].unsqueeze(3).to_broadcast([128, 16, 8, 16])

            def cplx_mul(outr, outi, rn_or, rn_oi, Xr, Xi, rn_x, tr, ti, lo, neg_imag=False):
                xr = Xr[:].unsqueeze(2).to_broadcast([128, 16, 8, 16]); xi = Xi[:].unsqueeze(2).to_broadcast([128, 16, 8, 16])
                ar_ = tauv(tr, lo); ai_ = tauv(ti, lo)
                dv(lambda e: e.tensor_tensor(out=big["t1"][:], in0=xr, in1=ar_, op=ALU.mult), rn_x + [tr, rn_or, rn_oi], ["t1"])
                dv(lambda e: e.tensor_tensor(out=big["t2"][:], in0=xi, in1=ai_, op=ALU.mult), rn_x + [ti, rn_or, rn_oi], ["t2"], "pool")
                dv(lambda e: e.tensor_tensor(out=outr[:], in0=big["t1"][:], in1=big["t2"][:], op=ALU.subtract), ["t1", "t2"], [rn_or])
                dv(lambda e: e.tensor_tensor(out=big["t1"][:], in0=xr, in1=ai_, op=ALU.mult), rn_x + [ti, rn_or], ["t1"])
                dv(lambda e: e.tensor_tensor(out=big["t2"][:], in0=xi, in1=ar_, op=ALU.mult), rn_x + [tr, rn_or], ["t2"], "pool")
                if neg_imag:
                    dv(lambda e: e.scalar_tensor_tensor(out=outi[:], in0=big["t1"][:], scalar=-1.0, in1=big["t2"][:], op0=ALU.mult, op1=ALU.subtract),
                       ["t1", "t2"], [rn_oi])
                else:
                    dv(lambda e: e.tensor_tensor(out=outi[:], in0=big["t1"][:], in1=big["t2"][:], op=ALU.add), ["t1", "t2"], [rn_oi])

            cplx_mul(big["W2r"], big["W2i"], "W2r", "W2i", CT[0], CT[1], [("CT", 0), ("CT", 1)], "APR", "API", 1, neg_imag=True)
            cplx_mul(big["M1r"], big["M1i"], "Lr", "Li", QB[0], QB[1], ["QB0", "QB1"], "APR", "API", 9)
            dv(lambda e: e.memset(W1re[:], 0.0), [], ["W1re"])
            dv(lambda e: e.memset(W1im[:], 0.0), [], ["W1im"], "pool")
            for c_, (M1, W1, rn) in enumerate(((big["M1r"], W1re, "W1re"), (big["M1i"], W1im, "W1im"))):
                for quad in range(4):
                    bank = 6 + quad % 2

                    def mmW(e, quad=quad, bank=bank, M1=M1):
                        last = None
                        for pl in range(4):
                            Pp = quad * 4 + pl
                            last = e.matmul(PS[bank][:, pl * 128:(pl + 1) * 128], lhsT=M1[:, Pp, :, :].rearrange("p s i -> p (s i)"), rhs=identf,
                                            start=True, stop=True)
                        return last
                    P.op("pe", mmW, reads=[r("Lr"), r("Li"), r("cst")], writes=[psr(bank)])
                    w1v = W1[:].rearrange("p (P e) c -> p P e c", e=2)
                    psv = PS[bank][:, :].rearrange("p (P c) -> p P c", P=4)
                    dv(lambda e, w1v=w1v, psv=psv, quad=quad: e.tensor_copy(out=w1v[:, quad * 4:quad * 4 + 4, 0, 0:64], in_=psv[:, :, 0:64]),
                       [("ps", bank)], [rn])
                    dv(lambda e, w1v=w1v, psv=psv, quad=quad: e.tensor_copy(out=w1v[:, quad * 4:quad * 4 + 4, 1, 64:128], in_=psv[:, :, 64:128]),
                       [("ps", bank)], [rn])
            cplx_mul(big["Lr"], big["Li"], "Lr", "Li", QB[0], QB[1], ["QB0", "QB1"], "AMR", "AMI", 1)
            dv(lambda e: e.activation(out=W2rb[:], in_=big["W2r"][:].rearrange("p P s o -> p P (s o)"), func=AF.Copy), ["W2r"], ["W2rb"], "act")
            _srcname = os.environ.get("KSRC", "W2i")
            _dst = {"W2ib": W2ib, "Tm": Tm[:, 0:16, :], "W1im": W1im[:, 0:16, :]}[os.environ.get("KDST", "W2ib")]
            dv(lambda e: e.activation(out=_dst[:] if os.environ.get("KDST", "W2ib") == "W2ib" else _dst, in_=big[_srcname][:].rearrange("p P s o -> p P (s o)"), func=AF.Copy), [_srcname], ["W2ib"], "act")
            def t_copies(quad):
                lmb = Lm[quad % 2]
                for gl in range(4):
                    g_ = quad * 4 + gl; Pp = g_ // 2; ee = g_ % 2
                    dv(lambda e, lmb=lmb, gl=gl, Pp=Pp, ee=ee: e.tensor_scalar(out=lmb[:, gl, 0, :], in0=big["Lr"][:, Pp, :, :].rearrange("p s i -> p (s i)"),
                                                                     scalar1=cst[:, C_PM + ee:C_PM + ee + 1], scalar2=None, op0=ALU.mult),
                       ["Lr", "cst"], [("Lm", quad % 2)])
                    dv(lambda e, lmb=lmb, gl=gl, Pp=Pp, ee=ee: e.tensor_scalar(out=lmb[:, gl, 1, :], in0=big["Li"][:, Pp, :, :].rearrange("p s i -> p (s i)"),
                                                                     scalar1=cst[:, C_PM + ee:C_PM + ee + 1], scalar2=None, op0=ALU.mult),
                       ["Li", "cst"], [("Lm", quad % 2)])

            def t_mm(quad):
                bank = 2 + quad % 4
                lmb = Lm[quad % 2]

                def mmT(e, quad=quad, bank=bank, lmb=lmb):
                    last = None
                    for gl in range(4):
                        g_ = quad * 4 + gl; Pp = g_ // 2
                        e.matmul(PS[bank][:, gl * 128:(gl + 1) * 128], lhsT=lmb[:, gl, 0, :],
                                 rhs=big["W2r"][:, Pp, :, :].rearrange("p s o -> p (s o)"), start=True, stop=False)
                        last = e.matmul(PS[bank][:, gl * 128:(gl + 1) * 128], lhsT=lmb[:, gl, 1, :],
                                        rhs=big["W2i"][:, Pp, :, :].rearrange("p s o -> p (s o)"), start=False, stop=True)
                    return last
                P.op("pe", mmT, reads=[r("Lm", quad % 2), r("W2r"), r("W2i")], writes=[psr(bank)])

            def t_evac(quad):
                bank = 2 + quad % 4
                tT = tmpT[quad % 2]
                dv(lambda e, bank=bank, tT=tT: e.tensor_tensor(out=tT[:], in0=PS[bank][:, :].rearrange("p (g c) -> p g c", g=4),
                                                               in1=cst[:, C_TM:C_TM + 128].unsqueeze(1).to_broadcast([128, 4, 128]), op=ALU.mult),
                   [("ps", bank), "cst"], [("tmpT", quad % 2)])
                for gl in range(4):
                    g_ = quad * 4 + gl
                    dv(lambda e, g_=g_, gl=gl, tT=tT: e.scalar_tensor_tensor(out=Tm[:, g_, :], in0=identf, scalar=dcol[:, g_:g_ + 1], in1=tT[:, gl, :],
                                                                             op0=ALU.mult, op1=ALU.add), [("tmpT", quad % 2), "dcol", "cst"], [("Tm", g_)])
            t_copies(0); t_mm(0)
            for quad in range(8):
                if quad + 1 < 8:
                    t_copies(quad + 1); t_mm(quad + 1)
                t_evac(quad)
            if "setup" in debug:
                _dl = [("Tm", Tm, 4096), ("W1re", W1re, 4096), ("W1im", W1im, 4096), ("W2rb", W2rb, 2048), ("W2ib", W2ib, 2048)]
                if os.environ.get("KNODUMP"):
                    _dl = [x for x in _dl if x[0] not in os.environ["KNODUMP"].split(",")]
                for nm, t_, n_ in _dl:
                    dbg[nm] = nc.dram_tensor("dbg_" + nm, [128, n_], BF16, kind="ExternalOutput").ap()
                    P.dma("sp", [lambda e, nm=nm, t_=t_: e.dma_start(out=dbg[nm][:, :], in_=t_[:].rearrange("p a b -> p (a b)"))],
                          reads=[P.res[k] for k in list(P.res) if k[0] == nm], out=True)
                dbg["s5c"] = nc.dram_tensor("dbg_s5c", [128, 64], F32, kind="ExternalOutput").ap()
                P.dma("sp", [lambda e: e.dma_start(out=dbg["s5c"][:, :], in_=s5c[:].rearrange("p a b -> p (a b)"))], reads=[r("s5c")], out=True)
            P.emit()
        if debug.endswith("setup"):
            mid.close()
            return nc

        with ExitStack() as esd:
            U = sb(esd, "U", (128, 32, 256), BF16); Us = sb(esd, "Us", (128, 32, 16), BF16)
            Hp = [sb(esd, "Hp%d" % i, (128, 16, 256), BF16) for i in range(2)]
            Hs = [sb(esd, "Hs%d" % i, (128, 16, 16), BF16) for i in range(2)]
            Hf = sb(esd, "Hf", (128, 2, 16))
            xs_sb = scrF[:, 8192:8704].rearrange("p (c x) -> p c x", c=2)
            P.dma("sp", [(lambda c_, Pp: (lambda e: _ncd(e, out=H0T[c_][:, Pp, :],
                                                         in_=D["h0r" if c_ == 0 else "h0i"][:, Pp * 128:(Pp + 1) * 128].rearrange("q p -> p q"))))(c_, Pp)
                         for c_ in range(2) for Pp in range(16)], writes=[r("H0T")])
            for c_ in range(2):
                dv(lambda e, c_=c_: e.memset(Hp[c_][:, :, 0:1], 0.0), [], [("Hp0", c_)], "pool")
            for bt in range(2):
                for oc in range(4):
                    bank = (bt * 4 + oc) % 2
                    P.op("pe", lambda e, bt=bt, oc=oc, bank=bank: pe_transposes(
                        e, [PSB[bank][:, gl * 128:(gl + 1) * 128] for gl in range(8)],
                        [P8[:, bt, (oc * 8 + gl) * 128:(oc * 8 + gl + 1) * 128] for gl in range(8)]),
                        reads=[r("P8", bt, s_) for s_ in range(8)] + [r("identb")], writes=[psr(bank)])
                    if oc % 2 == 0:
                        P.op("act", lambda e, bt=bt, oc=oc, bank=bank: e.activation(out=U[:, oc * 8:(oc + 1) * 8, bt * 128:(bt + 1) * 128],
                                                                                     in_=PSB[bank][:, 0:1024].rearrange("p (g c) -> p g c", g=8), func=AF.Copy),
                             reads=[psr(bank)], writes=[r("U", bt, oc)])
                    else:
                        P.op("dve", lambda e, bt=bt, oc=oc, bank=bank: e.tensor_copy(out=U[:, oc * 8:(oc + 1) * 8, bt * 128:(bt + 1) * 128],
                                                                                      in_=PSB[bank][:, 0:1024].rearrange("p (g c) -> p g c", g=8)),
                             reads=[psr(bank)], writes=[r("U", bt, oc)])

            def tr_s(e):
                last = None
                for g_ in range(32):
                    last = e.transpose(PSB[0][:, g_ * 16:(g_ + 1) * 16], P8[0:16, 2, g_ * 128:(g_ + 1) * 128], identb[0:16, 0:16])
                return last
            P.op("pe", tr_s, reads=[r("P8", 2, s_) for s_ in range(8)] + [r("identb")], writes=[psr(0)])
            dv(lambda e: e.tensor_copy(out=Us[:].rearrange("p g q -> p (g q)"), in_=PSB[0][:, 0:512]), [("ps", 0)], ["Us"])

            for c_, W1 in enumerate((W1re, W1im)):
                def mmxs(e, c_=c_, W1=W1):
                    last = None
                    for Pp in range(16):
                        e.matmul(PS[6 + c_][:, Pp * 16:(Pp + 1) * 16], lhsT=W1[:, 2 * Pp, :], rhs=Us[:, 2 * Pp, :], start=True, stop=False)
                        last = e.matmul(PS[6 + c_][:, Pp * 16:(Pp + 1) * 16], lhsT=W1[:, 2 * Pp + 1, :], rhs=Us[:, 2 * Pp + 1, :], start=False, stop=True)
                    return last
                P.op("pe", mmxs, reads=[r("Us"), r("W1re"), r("W1im")], writes=[psr(6 + c_)])
                dv(lambda e, c_=c_: e.tensor_copy(out=xs_sb[:, c_, :], in_=PS[6 + c_][:, 0:256]), [("ps", 6 + c_)], [("xs", c_)])
            T1 = scrF[:, 0:2048].rearrange("p (a b) -> p a b", a=8); T2 = scrF[:, 2048:4096].rearrange("p (a b) -> p a b", a=8)
            PH = scrF[:, 4096:6144].rearrange("p (a b) -> p a b", a=8); KK = scrF[:, 6144:8192].rearrange("p (a b) -> p a b", a=8)
            with ExitStack() as esh:
                Xr = sb(esh, "Xr", (128, 8, 256)); Xi = sb(esh, "Xi", (128, 8, 256))
                CS = sb(esh, "CS", (128, 8, 256)); SN = sb(esh, "SN", (128, 8, 256))
                X = (Xr, Xi)
                P8f2 = P8[:].rearrange("p a c -> p (a c)")
                CS1 = P8f2[:, 0:4096].bitcast(F32).rearrange("p (a b) -> p a b", a=8)
                SN1 = P8f2[:, 4096:8192].bitcast(F32).rearrange("p (a b) -> p a b", a=8)
                tabs_cs = (CS[:], CS1); tabs_sn = (SN[:], SN1)
                dv(lambda e: e.memset(KK[:, 0, 0:1], 0.0), [], ["KK"] + [("P8", bt, s_) for bt in range(3) for s_ in range(8)])
                for hf in range(2):
                    P0 = 8 * hf
                    CSh = tabs_cs[hf]; SNh = tabs_sn[hf]; rcs = ("CS", hf); rsn = ("SN", hf)
                    dv(lambda e, P0=P0: e.tensor_tensor(out=PH, in0=cst[:, C_IO:C_IO + 256].unsqueeze(1).to_broadcast([128, 8, 256]),
                                                        in1=s5c[:, 1, P0:P0 + 8].unsqueeze(2).to_broadcast([128, 8, 256]), op=ALU.mult),
                       ["cst", "s5c"], ["PH"], "pool")
                    range_reduce(SNh, PH, 0.0, rsn, "PH", T1, KK, "T1", "KK")
                    dv(lambda e, CSh=CSh, SNh=SNh: e.activation(out=CSh, in_=SNh, func=AF.Abs), [rsn], [rcs], "act")
                    dv(lambda e, CSh=CSh: e.activation(out=CSh, in_=CSh, func=AF.Sin, scale=-1.0, bias=cst[:, C_HPI:C_HPI + 1]), [rcs, "cst"], [rcs], "act")
                    dv(lambda e, SNh=SNh: e.activation(out=SNh, in_=SNh, func=AF.Sin), [rsn, rcs], [rsn], "act")
                for hf in range(2):
                    P0 = 8 * hf
                    CSh = tabs_cs[hf]; SNh = tabs_sn[hf]; rcs = ("CS", hf); rsn = ("SN", hf)
                    for bt in range(2):
                        for quad in range(2):
                            for c_, W1 in enumerate((W1re, W1im)):
                                bank = 2 + ((bt * 2 + quad) * 2 + c_) % 4

                                def mmx(e, bt=bt, quad=quad, W1=W1, bank=bank, P0=P0):
                                    last = None
                                    for pl in range(4):
                                        Pp = P0 + 4 * quad + pl
                                        e.matmul(PS[bank][:, pl * 128:(pl + 1) * 128], lhsT=W1[:, 2 * Pp, :], rhs=U[:, 2 * Pp, bt * 128:(bt + 1) * 128],
                                                 start=True, stop=False)
                                        last = e.matmul(PS[bank][:, pl * 128:(pl + 1) * 128], lhsT=W1[:, 2 * Pp + 1, :],
                                                        rhs=U[:, 2 * Pp + 1, bt * 128:(bt + 1) * 128], start=False, stop=True)
                                    return last
                                P.op("pe", mmx, reads=[r("U", bt, oc) for oc in range(4)] + [r("W1re"), r("W1im")], writes=[psr(bank)])
                                Xc = X[c_]
                                if c_ == 0:
                                    P.op("act", lambda e, Xc=Xc, quad=quad, bt=bt, bank=bank: e.activation(
                                        out=Xc[:, 4 * quad:4 * quad + 4, bt * 128:(bt + 1) * 128], in_=PS[bank][:, :].rearrange("p (a b) -> p a b", a=4), func=AF.Copy),
                                        reads=[psr(bank)], writes=[r("X", c_)])
                                else:
                                    P.op("dve", lambda e, Xc=Xc, quad=quad, bt=bt, bank=bank: e.tensor_copy(
                                        out=Xc[:, 4 * quad:4 * quad + 4, bt * 128:(bt + 1) * 128], in_=PS[bank][:, :].rearrange("p (a b) -> p a b", a=4)),
                                        reads=[psr(bank)], writes=[r("X", c_)])
                    dv(lambda e, CSh=CSh, SNh=SNh: e.tensor_tensor(out=T1, in0=CSh, in1=Xr[:], op=ALU.mult), [rcs, ("X", 0)], ["T1"])
                    dv(lambda e, CSh=CSh, SNh=SNh: e.tensor_tensor(out=T2, in0=SNh, in1=Xi[:], op=ALU.mult), [rsn, ("X", 1)], ["T2"], "pool")
                    dv(lambda e: e.tensor_tensor(out=T1, in0=T1, in1=T2, op=ALU.add), ["T1", "T2"], ["T1"])
                    dv(lambda e, CSh=CSh, SNh=SNh: e.tensor_tensor(out=T2, in0=CSh, in1=Xi[:], op=ALU.mult), [rcs, ("X", 1)], ["T2"], "pool")
                    dv(lambda e, CSh=CSh, SNh=SNh: e.tensor_tensor(out=Xi[:], in0=SNh, in1=Xr[:], op=ALU.mult), [rsn, ("X", 0)], [("X", 1)])
                    dv(lambda e: e.tensor_tensor(out=T2, in0=T2, in1=Xi[:], op=ALU.subtract), ["T2", ("X", 1)], ["T2"])
                    for pl in range(8):
                        Pp = P0 + pl
                        dv(lambda e, pl=pl, Pp=Pp: e.tensor_tensor_scan(out=Xr[:, pl, :], data0=s5c[:, 0, Pp:Pp + 1].to_broadcast([128, 256]),
                                                                        data1=T1[:, pl, :], initial=0.0, op0=ALU.mult, op1=ALU.add),
                           ["T1", "s5c"], [("X", 0)])
                        dv(lambda e, pl=pl, Pp=Pp: e.tensor_tensor_scan(out=Xi[:, pl, :], data0=s5c[:, 0, Pp:Pp + 1].to_broadcast([128, 256]),
                                                                        data1=T2[:, pl, :], initial=0.0, op0=ALU.mult, op1=ALU.add),
                           ["T2", "s5c"], [("X", 1)])
                    dv(lambda e, CSh=CSh, SNh=SNh: e.tensor_tensor(out=T1, in0=CSh, in1=Xr[:], op=ALU.mult), [rcs, ("X", 0)], ["T1"])
                    dv(lambda e, CSh=CSh, SNh=SNh: e.tensor_tensor(out=T2, in0=SNh, in1=Xi[:], op=ALU.mult), [rsn, ("X", 1)], ["T2"], "pool")
                    dv(lambda e, P0=P0: e.tensor_tensor(out=Hp[0][:, P0:P0 + 8, 1:256], in0=T1[:, :, 0:255], in1=T2[:, :, 0:255], op=ALU.subtract),
                       ["T1", "T2"], [("Hp", 0, hf)])
                    dv(lambda e, P0=P0: e.tensor_tensor(out=Hf[:, 0, P0:P0 + 8], in0=T1[:, :, 255], in1=T2[:, :, 255], op=ALU.subtract),
                       ["T1", "T2"], [("Hf", 0, hf)])
                    dv(lambda e, CSh=CSh, SNh=SNh: e.tensor_tensor(out=T1, in0=CSh, in1=Xi[:], op=ALU.mult), [rcs, ("X", 1), ("Hp", 0, hf), ("Hf", 0, hf)], ["T1"])
                    dv(lambda e, CSh=CSh, SNh=SNh: e.tensor_tensor(out=T2, in0=SNh, in1=Xr[:], op=ALU.mult), [rsn, ("X", 0), ("Hp", 0, hf), ("Hf", 0, hf)], ["T2"], "pool")
                    dv(lambda e, P0=P0: e.tensor_tensor(out=Hp[1][:, P0:P0 + 8, 1:256], in0=T1[:, :, 0:255], in1=T2[:, :, 0:255], op=ALU.add),
                       ["T1", "T2"], [("Hp", 1, hf)])
                    dv(lambda e, P0=P0: e.tensor_tensor(out=Hf[:, 1, P0:P0 + 8], in0=T1[:, :, 255], in1=T2[:, :, 255], op=ALU.add),
                       ["T1", "T2"], [("Hf", 1, hf)])
                P.dma("sp", [lambda e: ncdma(e, out=D["pr"].rearrange("(P e) n -> (e n) P", e=2), in_=Hf[:, 0, :]),
                             lambda e: ncdma(e, out=D["pi"].rearrange("(P e) n -> (e n) P", e=2), in_=Hf[:, 1, :])],
                      reads=[r("Hf", c_, hf) for c_ in range(2) for hf in range(2)], out=True)
            if "da" in debug:
                for nm, t_ in (("U", U), ("Hp0", Hp[0]), ("Hp1", Hp[1])):
                    dbg[nm] = nc.dram_tensor("dbg_" + nm, list(t_.shape), BF16, kind="ExternalOutput").ap()
                    P.dma("sp", [lambda e, nm=nm, t_=t_: e.dma_start(out=dbg[nm][:, :, :], in_=t_[:])],
                          reads=[P.res[k] for k in list(P.res) if k[0] in ("U", "Hp", "Hp0")], out=True)

            P.emit()
            if debug.endswith("da"):
                esd.close(); mid.close()
                return nc
            with ExitStack() as esb:
                P8f = P8[:].rearrange("p a c -> p (a c)")
                G8_bufs = (scrF[:, 0:4096], P8f[:, 0:8192].bitcast(F32))
                gT = scrB[:, 8192:12288].rearrange("p (j c) -> p j c", j=4)
                G8b0 = sb(esb, "G8b", (128, 4096), BF16); O8b = sb(esb, "O8b", (128, 4096), BF16)
                G8b_bufs = (G8b0[:], P8f[:, 8192:12288])
                tmpz = [sb(esb, "tmpz%d" % i, (128, 512)) for i in range(2)]
                junk2 = sb(esb, "junk2", (128, 512), BF16)
                for bt in range(3):
                    if bt == 2:
                        for c_ in range(2):
                            dv(lambda e, c_=c_: e.tensor_copy(out=Hs[c_][:], in_=H0T[c_][:]), ["H0T"], [("Hs", c_)], "pool")
                    bf_ = bt % 2
                    G8 = G8_bufs[bf_]; G8b = G8b_bufs[bf_]
                    G8v = G8.rearrange("p (s g o) -> p g s o", s=8, g=32)
                    G8bv = G8b.rearrange("p (s g o) -> p g s o", s=8, g=32)
                    npt = 128 if bt < 2 else 16
                    for gq in range(8):
                        bank = 4 + gq % 2

                        def mmy(e, bt=bt, gq=gq, bank=bank, npt=npt):
                            last = None
                            for gl in range(4):
                                g_ = 4 * gq + gl; Pp = g_ // 2; ee = g_ % 2; lo, hi = ee * 64, ee * 64 + 64
                                o_ = PS[bank][0:npt, gl * 128:(gl + 1) * 128]
                                if bt < 2:
                                    u_ = U[:, g_, bt * 128:(bt + 1) * 128]; hr_ = Hp[0][lo:hi, Pp, bt * 128:(bt + 1) * 128]; hi_ = Hp[1][lo:hi, Pp, bt * 128:(bt + 1) * 128]
                                else:
                                    u_ = Us[:, g_, :]; hr_ = Hs[0][lo:hi, Pp, :]; hi_ = Hs[1][lo:hi, Pp, :]
                                e.matmul(o_, lhsT=u_, rhs=Tm[:, g_, :], start=True, stop=False)
                                e.matmul(o_, lhsT=hr_, rhs=W2rb[lo:hi, Pp, :], start=False, stop=False)
                                last = e.matmul(o_, lhsT=hi_, rhs=W2ib[lo:hi, Pp, :], start=False, stop=True)
                            return last
                        rd = ([r("U", bt, oc) for oc in range(4)] + [r("Hp", c_, hf) for c_ in range(2) for hf in range(2)] + [r("Hp0", 0), r("Hp0", 1)]) if bt < 2 \
                            else [r("Us"), r("Hs", 0), r("Hs", 1)]
                        P.op("pe", mmy, reads=rd + [r("Tm", g_) for g_ in range(4 * gq, 4 * gq + 4)] + [r("W2rb"), r("W2ib")], writes=[psr(bank)])
                        P.op("act", lambda e, gq=gq, bank=bank, npt=npt, G8v=G8v: e.activation(
                            out=G8v[0:npt, 4 * gq:4 * gq + 4, :, :], in_=PS[bank][0:npt, :].rearrange("p (g s o) -> p g s o", g=4, s=8), func=AF.Gelu_apprx_tanh),
                            reads=[psr(bank)], writes=[r("G8", bf_, gq)])
                        P.op("dve", lambda e, gq=gq, npt=npt, G8bv=G8bv, G8v=G8v: e.tensor_copy(
                            out=G8bv[0:npt, 4 * gq:4 * gq + 4, :, :], in_=G8v[0:npt, 4 * gq:4 * gq + 4, :, :]),
                            reads=[r("G8", bf_, gq)], writes=[r("G8b", bf_, gq)])
                    if bt == 1:
                        sft0 = tmpz[0][:, 0:256].rearrange("p (a b) -> p a b", a=16); sft1 = tmpz[1][:, 0:256].rearrange("p (a b) -> p a b", a=16)
                        Sf = (scrF[:, 6144:6400].rearrange("p (a b) -> p a b", a=16), scrF[:, 6400:6656].rearrange("p (a b) -> p a b", a=16))
                        Sout = scrF[0:16, 6656:8704]
                        a8rb = s5c[:, 2, :].unsqueeze(2).to_broadcast([128, 16, 16]); a8ib = s5c[:, 3, :].unsqueeze(2).to_broadcast([128, 16, 16])
                        xsv = [xs_sb[:, c_, :].rearrange("p (a b) -> p a b", a=16) for c_ in range(2)]
                        rt0, rt1 = ("tmpz", 0), ("tmpz", 1)
                        dv(lambda e: e.tensor_tensor(out=sft0, in0=H0T[0][:], in1=a8rb, op=ALU.mult), ["H0T", "s5c"], [rt0])
                        dv(lambda e: e.tensor_tensor(out=sft1, in0=H0T[1][:], in1=a8ib, op=ALU.mult), ["H0T", "s5c"], [rt1])
                        dv(lambda e: e.tensor_tensor(out=sft0, in0=sft0, in1=sft1, op=ALU.subtract), [rt0, rt1], [rt0])
                        dv(lambda e: e.tensor_tensor(out=Sf[0], in0=sft0, in1=xsv[0], op=ALU.add), [rt0, ("xs", 0)], [("Sf", 0)])
                        dv(lambda e: e.tensor_tensor(out=sft0, in0=H0T[1][:], in1=a8rb, op=ALU.mult), ["H0T", "s5c", ("Sf", 0)], [rt0])
                        dv(lambda e: e.tensor_tensor(out=sft1, in0=H0T[0][:], in1=a8ib, op=ALU.mult), ["H0T", "s5c", ("Sf", 0)], [rt1])
                        dv(lambda e: e.tensor_tensor(out=sft0, in0=sft0, in1=sft1, op=ALU.add), [rt0, rt1], [rt0])
                        dv(lambda e: e.tensor_tensor(out=Sf[1], in0=sft0, in1=xsv[1], op=ALU.add), [rt0, ("xs", 1)], [("Sf", 1)])
                        for c_ in range(2):
                            def mmso(e, c_=c_):
                                last = None
                                for Pp in range(16):
                                    last = e.matmul(PS[Pp // 4][0:16, (Pp % 4) * 128:(Pp % 4 + 1) * 128], lhsT=Sf[c_][:, Pp, :], rhs=identf, start=True, stop=True)
                                return last
                            P.op("pe", mmso, reads=[r("Sf", c_), r("cst")], writes=[psr(0), psr(1), psr(2), psr(3)])
                            for bk in range(4):
                                dv(lambda e, bk=bk: e.tensor_copy(out=Sout[:, bk * 512:(bk + 1) * 512], in_=PS[bk][0:16, :]), [("ps", bk)], ["Sout", ("xs", 0), ("xs", 1)])
                            dn = "sr" if c_ == 0 else "si"
                            P.dma("sp", [lambda e, dn=dn: e.dma_start(out=D[dn][:, :], in_=Sout)], reads=[r("Sout")], out=True)

                    def phaseA(s_, bt=bt, npt=npt, bf_=bf_, G8=G8, G8b=G8b):
                        bk = s_ % 2; idx = bt * 8 + s_
                        P.op("pe", lambda e, s_=s_, bk=bk, npt=npt: pe_transposes(
                            e, [PSB[bk][:, j * npt:(j + 1) * npt] for j in range(4)],
                            [G8b[0:npt, s_ * 512 + j * 128:s_ * 512 + (j + 1) * 128] for j in range(4)], npart=npt),
                            reads=[r("G8b", bf_, gq) for gq in range(8)] + [r("identb")], writes=[psr(bk)])
                        dv(lambda e, s_=s_, bk=bk, npt=npt: e.tensor_copy(out=gT[:, :, s_ * npt:(s_ + 1) * npt],
                                                                          in_=PSB[bk][:, 0:4 * npt].rearrange("p (j c) -> p j c", j=4)),
                           [("ps", bk)], [("gT", s_)])
                        zb = 6 + s_ % 2

                        def mmz(e, s_=s_, zb=zb, npt=npt):
                            for j in range(4):
                                e.matmul(PS[zb][0:npt, :], lhsT=gT[:, j, s_ * npt:(s_ + 1) * npt], rhs=wglu[:, j, :], start=(j == 0), stop=False)
                            return e.matmul(PS[zb][0:npt, :], lhsT=onesb[0:1, 0:npt], rhs=bglub[0:1, :], start=False, stop=True)
                        P.op("pe", mmz, reads=[r("gT", s_), r("wglu"), r("onesb")], writes=[psr(zb)])

                    def phaseB(s_, bt=bt, npt=npt, bf_=bf_, G8=G8, G8b=G8b):
                        bk = s_ % 2; idx = bt * 8 + s_; zb = 6 + s_ % 2
                        tz = tmpz[s_ % 2]
                        P.op("act", lambda e, tz=tz, zb=zb, npt=npt: e.activation(out=tz[0:npt, :], in_=PS[zb][0:npt, :], func=AF.Tanh, scale=0.5),
                             reads=[psr(zb)], writes=[r("tmpz", s_ % 2)])
                        g8s = G8[0:npt, s_ * 512:(s_ + 1) * 512]
                        dv(lambda e, tz=tz, g8s=g8s, npt=npt: e.scalar_tensor_tensor(out=g8s, in0=tz[0:npt, :], scalar=1.0, in1=g8s, op0=ALU.add, op1=ALU.mult),
                           [("tmpz", s_ % 2)] + [("G8", bf_, gq) for gq in range(8)], [("G8s", bf_, s_)])
                        P.op("act", lambda e, g8s=g8s, idx=idx, npt=npt: e.activation(out=junk2[0:npt, :], in_=g8s, func=AF.Square, accum_out=ssq5[0:npt, idx:idx + 1]),
                             reads=[r("G8s", bf_, s_)], writes=[r("junk2"), r("ssq5")])
                        dv(lambda e, g8s=g8s, s_=s_, npt=npt: e.tensor_tensor(out=O8b[0:npt, s_ * 512:(s_ + 1) * 512], in0=g8s, in1=gs5bc[0:npt, :], op=ALU.mult),
                           [("G8s", bf_, s_), "gs5bc"], [("O8b", s_)], "pool")
                        bk2 = 2 + s_ % 2
                        P.op("pe", lambda e, s_=s_, bk2=bk2, npt=npt: pe_transposes(
                            e, [PSB[bk2][:, j * npt:(j + 1) * npt] for j in range(4)],
                            [O8b[0:npt, s_ * 512 + j * 128:s_ * 512 + (j + 1) * 128] for j in range(4)], npart=npt),
                            reads=[r("O8b", s_), r("identb")], writes=[psr(bk2)])

                    def phaseC(s_, bt=bt, npt=npt):
                        bk2 = 2 + s_ % 2
                        if bt < 2:
                            mo = mixT[:, 0:4, bt * 1024 + s_:bt * 1024 + 1024:8]
                        else:
                            mo = mixT[:, 0:4, 2048 + s_:2176:8]
                        P.op("act", lambda e, mo=mo, bk2=bk2, npt=npt: e.activation(out=mo, in_=PSB[bk2][:, 0:4 * npt].rearrange("p (j c) -> p j c", j=4), func=AF.Copy),
                             reads=[psr(bk2)], writes=[r("mixs5", bt, s_)])

                    phaseA(0)
                    for s_ in range(8):
                        if s_ + 1 < 8:
                            phaseA(s_ + 1)
                        phaseB(s_)
                        if s_ >= 1:
                            phaseC(s_ - 1)
                    phaseC(7)
                    for gq in range(8):
                        dst = r("G8", bf_, gq)
                        for s_ in range(8):
                            src = r("G8s", bf_, s_)
                            for tok in list(src.rs.values()) + ([src.w] if src.w is not None else []):
                                k = id(tok[0])
                                if k not in dst.rs or dst.rs[k][1] < tok[1]:
                                    dst.rs[k] = tok
                if "db" in debug:
                    dbg["mixs5"] = nc.dram_tensor("dbg_mixs5", [128, 4, TOK], BF16, kind="ExternalOutput").ap()
                    P.dma("sp", [lambda e: e.dma_start(out=dbg["mixs5"][:, :, :], in_=mixT[:, 0:4, :])],
                          reads=[r("mixs5", bt, s_) for bt in range(3) for s_ in range(8)], out=True)
                    dbg["ssq5"] = nc.dram_tensor("dbg_ssq5", [128, 24], F32, kind="ExternalOutput").ap()
                    P.dma("sp", [lambda e: e.dma_start(out=dbg["ssq5"][:, :], in_=ssq5[:])], reads=[r("ssq5")], out=True)
                P.emit()
        mid.close()
        if debug.endswith("d"):
            return nc

        X1 = sb(top, "X1", (128, NT, 1024))
        ssq2 = sb(top, "ssq2", (128, NT)); rstd2 = sb(top, "rstd2", (128, NT)); srtE = sb(top, "srtE", (128, NT))
        ssqf = sb(top, "ssqf", (128, NT)); rstdf = sb(top, "rstdf", (128, NT))
        wd = sb(top, "wd", (128, 8, 1024), BF16)
        wg = [sb(top, "wg%d" % i, (128, 8, 128), BF16) for i in range(3)]
        wu = [sb(top, "wu%d" % i, (128, 8, 128), BF16) for i in range(3)]
        def load_f(f):
            sl = f % 3
            P.dma("pool", [lambda e, f=f, sl=sl: e.dma_start(out=wg[sl][:], in_=D["w_gate"][:, f * 128:(f + 1) * 128].rearrange("(kt p) n -> p kt n", p=128)),
                           lambda e, f=f, sl=sl: e.dma_start(out=wu[sl][:], in_=D["w_up"][:, f * 128:(f + 1) * 128].rearrange("(kt p) n -> p kt n", p=128))],
                  writes=[r("wgu", sl)])

        def load_wd(f, fl):
            P.dma("pool", [lambda e, f=f, fl=fl: e.dma_start(out=wd[:, fl, :], in_=D["w_down"][f * 128:(f + 1) * 128, :])], writes=[r("wd", fl)])

        with ExitStack() as es:
            wout = sb(es, "wout", (128, 8, 1024), BF16)
            g2bc = sb(es, "g2bc", (128, 1024))
            xt = [sb(es, "ext%d" % i, (128, 1024)) for i in range(2)]
            xb = [sb(es, "exb%d" % i, (128, 1024), BF16) for i in range(2)]
            wov = D["w_out"].rearrange("(kt p) n -> p kt n", p=128)
            P.dma("pool", [lambda e: e.dma_start(out=wout[:, 4:8, :], in_=wov[:, 4:8, :])], writes=[r("wout", 1)])
            P.dma("pool", [lambda e: e.dma_start(out=wout[:, 0:4, :], in_=wov[:, 0:4, :])], writes=[r("wout", 0)])
            P.dma("sp", [lambda e: e.dma_start(out=g2bc[:], in_=D["norm2"][0:1, :].partition_broadcast(128))], writes=[r("g2bc")])
            P.dma("sp", [lambda e: ncdma(e, out=scr2[0:2048].rearrange("(a b s) -> b a s", a=2, s=8), in_=ssq5[:, 0:16].rearrange("p (a s) -> p a s", a=2)),
                         lambda e: ncdma(e, out=scr2[2048:2176].rearrange("(q s) -> q s", s=8), in_=ssq5[0:16, 16:24])],
                  reads=[r("ssq5")], writes=[r("scr2")])
            P.dma("sp", [lambda e: ncdma(e, out=ssq5n[:], in_=scr2.rearrange("(t p) -> p t", p=128))], reads=[r("scr2")], writes=[r("ssq5n")])
            dv(lambda e: e.activation(out=srtE[:], in_=ssq5n[:], func=AF.Sqrt, bias=eps4c, scale=1.0 / 512), ["ssq5n", "cst"], ["srtE"], "act")
            dv(lambda e: e.reciprocal(out=rstd5[:], in_=srtE[:]), ["srtE"], ["rstd5"])
            dv(lambda e: e.activation(out=srtE[:], in_=ssqcm[:], func=AF.Sqrt, bias=epsc, scale=1.0 / 512),
               [("ssqcm", t) for t in range(NT)] + ["cst", "rstd5"], ["srtE"], "act")
            dv(lambda e: e.reciprocal(out=rstdcm[:], in_=srtE[:]), ["srtE"], ["rstdcm"])
            for t in range(NT):
                xs_ = xt[t % 2]; rx = r("ext", t % 2); b0 = 4 * (t % 2)
                P.dma("sp", [lambda e, xs_=xs_, t=t: e.dma_start(out=xs_[:], in_=xsrc(t))], writes=[rx])
                for part in range(2):
                    for half in range(2):
                        bank = b0 + 2 * part + half
                        k0 = 4 if part == 0 else 0

                        def mmo(e, t=t, bank=bank, k0=k0, half=half):
                            last = None
                            for j in range(4):
                                last = e.matmul(PS[bank][:, :], lhsT=mixT[:, k0 + j, t * 128:(t + 1) * 128], rhs=wout[:, k0 + j, half * 512:(half + 1) * 512],
                                                start=(j == 0), stop=(j == 3))
                            return last
                        rd = [r("mixcm", t)] if part == 0 else [r("mixs5", bt, s_) for bt in range(3) for s_ in range(8)]
                        P.op("pe", mmo, reads=rd + [r("wout", 1 - part)], writes=[psr(bank)])
                for half in range(2):
                    hs = slice(half * 512, (half + 1) * 512)
                    dv(lambda e, t=t, hs=hs, xs_=xs_, bank=b0 + half: e.scalar_tensor_tensor(out=X1[:, t, hs], in0=PS[bank][:, :], scalar=rstdcm[:, t:t + 1], in1=xs_[:, hs],
                                                                                     op0=ALU.mult, op1=ALU.add),
                       [("ps", b0 + half), "rstdcm", ("ext", t % 2)], [("X1", t)])
                    dv(lambda e, t=t, hs=hs, bank=b0 + 2 + half: e.scalar_tensor_tensor(out=X1[:, t, hs], in0=PS[bank][:, :], scalar=rstd5[:, t:t + 1], in1=X1[:, t, hs],
                                                                                op0=ALU.mult, op1=ALU.add),
                       [("ps", b0 + 2 + half), "rstd5", ("X1", t)], [("X1", t)])
                dv(lambda e, t=t: e.activation(out=xb[t % 2][:], in_=X1[:, t, :], func=AF.Square, accum_out=ssq2[:, t:t + 1]), [("X1", t)], [("exb", t % 2), ("ssq2", t)], "act")
            for f in range(3):
                load_f(f)
            for fl in range(8):
                load_wd(fl, fl)
            dv(lambda e: e.activation(out=srtE[:], in_=ssq2[:], func=AF.Sqrt, bias=epsc, scale=1.0 / 1024),
               [("ssq2", t) for t in range(NT)] + ["cst", "rstdcm"], ["srtE"], "act")
            dv(lambda e: e.reciprocal(out=rstd2[:], in_=srtE[:]), ["srtE"], ["rstd2"])
            for t in range(NT):
                xbt = xb[t % 2]; pb = t % 2
                dv(lambda e, t=t, xbt=xbt: e.scalar_tensor_tensor(out=xbt[:], in0=X1[:, t, :], scalar=rstd2[:, t:t + 1], in1=g2bc[:], op0=ALU.mult, op1=ALU.mult),
                   [("X1", t), "rstd2", "g2bc"], [("exb", t % 2)])
                P.op("pe", lambda e, xbt=xbt, pb=pb: pe_transposes(e, [PSB[pb][:, k * 128:(k + 1) * 128] for k in range(8)],
                                                               [xbt[:, k * 128:(k + 1) * 128] for k in range(8)]),
                     reads=[r("exb", t % 2), r("identb")], writes=[psr(pb)])
                P.op("act", lambda e, t=t, pb=pb: e.activation(out=actA[:, :, t * 128:(t + 1) * 128], in_=PSB[pb][:, 0:1024].rearrange("p (k c) -> p k c", k=8), func=AF.Copy),
                     reads=[psr(pb)], writes=[r("x2T", t)])
            if "e" in debug.split("-"):
                dbg["X1"] = nc.dram_tensor("dbg_X1", [128, NT, 1024], F32, kind="ExternalOutput").ap()
                P.dma("sp", [lambda e: e.dma_start(out=dbg["X1"][:, :, :], in_=X1[:])], reads=[r("X1", t) for t in range(NT)], out=True)
                dbg["x2T"] = nc.dram_tensor("dbg_x2T", [128, 8, TOK], BF16, kind="ExternalOutput").ap()
                P.dma("sp", [lambda e: e.dma_start(out=dbg["x2T"][:, :, :], in_=actA[:])], reads=[r("x2T", t) for t in range(NT)], out=True)
            P.emit()
        if debug.endswith("-e"):
            return nc

        with ExitStack() as es:
            hF = mixT
            sg = [sb(es, "sg%d" % i, (128, 512)) for i in range(2)]
            gfbc = sb(es, "gfbc", (128, 1024))
            yo = [sb(es, "yo%d" % i, (128, 1024)) for i in range(2)]
            gjunk = sb(es, "gjunk", (128, 1024), BF16)
            P.dma("sp", [lambda e: e.dma_start(out=gfbc[:], in_=D["norm_f"][0:1, :].partition_broadcast(128))], writes=[r("gfbc")])
            fgroups = [(0, 8), (8, 16), (16, 22)]
            tgs = [(0, 512), (512, 1024), (1024, 1536), (1536, 2048), (2048, 2176)]

            cnt = 0
            for gi, (f0, f1) in enumerate(fgroups):
                nf = f1 - f0
                if gi > 0:
                    for fl in range(nf):
                        load_wd(f0 + fl, fl)
                for f in range(f0, f1):
                    sl = f % 3; fl = f - f0
                    for ti, (c0, c1) in enumerate(tgs):
                        n = c1 - c0; bg = cnt % 2; bu = 2 + cnt % 2; sgt = sg[cnt % 2]; rsg = r("sg", cnt % 2); cnt += 1
                        tiles = list(range(c0 // 128, (c1 + 127) // 128))

                        def mmgu(e, W, bank, sl=sl, c0=c0, c1=c1, n=n):
                            last = None
                            for k in range(8):
                                last = e.matmul(PS[bank][:, 0:n], lhsT=W[sl][:, k, :], rhs=actA[:, k, c0:c1], start=(k == 0), stop=(k == 7))
                            return last
                        P.op("pe", lambda e, bg=bg, mmgu=mmgu: mmgu(e, wg, bg), reads=[r("x2T", t) for t in tiles] + [r("wgu", sl)], writes=[psr(bg)])
                        P.op("pe", lambda e, bu=bu, mmgu=mmgu: mmgu(e, wu, bu), reads=[r("x2T", t) for t in tiles] + [r("wgu", sl)], writes=[psr(bu)])
                        P.op("act", lambda e, sgt=sgt, bg=bg, n=n: e.activation(out=sgt[:, 0:n], in_=PS[bg][:, 0:n], func=AF.Silu), reads=[psr(bg)], writes=[rsg])
                        P.op("dve", lambda e, sgt=sgt, bu=bu, n=n, fl=fl, c0=c0, c1=c1: e.tensor_tensor(out=hF[:, fl, c0:c1], in0=sgt[:, 0:n], in1=PS[bu][:, 0:n], op=ALU.mult),
                             reads=[rsg, psr(bu)], writes=[r("hF", fl, ti)])
                    if f + 3 < NFF:
                        load_f(f + 3)
                for t in range(NT):
                    ti = min(t // 4, 4)
                    for half in range(2):
                        bank = 4 + (t * 2 + half) % 4

                        def mmd(e, t=t, half=half, bank=bank, nf=nf):
                            last = None
                            for fl in range(nf):
                                last = e.matmul(PS[bank][:, :], lhsT=hF[:, fl, t * 128:(t + 1) * 128], rhs=wd[:, fl, half * 512:(half + 1) * 512],
                                                start=(fl == 0), stop=(fl == nf - 1))
                            return last
                        P.op("pe", mmd, reads=[r("hF", fl, ti) for fl in range(nf)] + [r("wd", fl) for fl in range(nf)], writes=[psr(bank)])
                        hs = slice(half * 512, (half + 1) * 512)
                        dv(lambda e, t=t, hs=hs, bank=bank: e.tensor_tensor(out=X1[:, t, hs], in0=X1[:, t, hs], in1=PS[bank][:, :], op=ALU.add),
                           [("ps", bank), ("X1", t)], [("X1", t)])
                    if gi == len(fgroups) - 1:
                        yot = yo[t % 2]
                        dv(lambda e, t=t: e.activation(out=gjunk[:], in_=X1[:, t, :], func=AF.Square, accum_out=ssqf[:, t:t + 1]), [("X1", t)], ["gjunk", ("ssqf", t)], "act")
                        dv(lambda e, t=t: e.activation(out=srtE[:, t:t + 1], in_=ssqf[:, t:t + 1], func=AF.Sqrt, bias=epsc, scale=1.0 / 1024),
                           [("ssqf", t), "cst"], [("srtf", t)], "act")
                        dv(lambda e, t=t: e.reciprocal(out=rstdf[:, t:t + 1], in_=srtE[:, t:t + 1]), [("srtf", t)], [("rstdf", t)])
                        dv(lambda e, t=t, yot=yot: e.scalar_tensor_tensor(out=yot[:], in0=X1[:, t, :], scalar=rstdf[:, t:t + 1], in1=gfbc[:], op0=ALU.mult, op1=ALU.mult),
                           [("X1", t), ("rstdf", t), "gfbc"], [("yo", t % 2)])
                        P.dma("sp", [lambda e, t=t, yot=yot: e.dma_start(out=ydst(t), in_=yot[:])], reads=[r("yo", t % 2)], out=True)
            P.emit()


    return nc


_CACHE = {}


def _prep_inputs(inputs, c):
    f = lambda a: np.ascontiguousarray(np.asarray(a, dtype=np.float32))
    m = {
        "xp": f(inputs["x_prompt"][c]),
        "xs": f(inputs["x_sample"][16 * c:16 * c + 16]).reshape(128, 1024),
        "h0r": f(inputs["state_s5_re"][0, 16 * c:16 * c + 16]).reshape(16, 2048),
        "h0i": f(inputs["state_s5_im"][0, 16 * c:16 * c + 16]).reshape(16, 2048),
        "norm1": f(inputs["norm1"]).reshape(1, 1024),
        "w_in": f(inputs["w_in"][0]),
        "lam_re": f(inputs["lam_re"][0]), "lam_im": f(inputs["lam_im"][0]),
        "log_dt": f(inputs["log_dt"]).reshape(1, 32),
        "b_re": f(inputs["b_re"][0]).reshape(2048, 16), "b_im": f(inputs["b_im"][0]).reshape(2048, 16),
        "c_re": f(inputs["c_re"][0]).reshape(512, 64), "c_im": f(inputs["c_im"][0]).reshape(512, 64),
        "d_skip": f(inputs["d_skip"]).reshape(1, 512), "w_glu": f(inputs["w_glu"][0]), "b_glu": f(inputs["b_glu"]).reshape(1, 512),
        "cm_ln_g": f(inputs["cm_ln_g"]).reshape(1, 512), "cm_ln_b": f(inputs["cm_ln_b"]).reshape(1, 512),
        "w_s": f(inputs["w_s"][0]).reshape(1024, 128), "b_s": f(inputs["b_s"][0]),
        "g_s5": f(inputs["g_s5"]).reshape(1, 512), "g_cm": f(inputs["g_cm"]).reshape(1, 512),
        "w_out": f(inputs["w_out"][0]), "norm2": f(inputs["norm2"]).reshape(1, 1024),
        "w_gate": f(inputs["w_gate"][0]), "w_up": f(inputs["w_up"][0]), "w_down": f(inputs["w_down"][0]),
        "norm_f": f(inputs["norm_f"]).reshape(1, 1024),
        "consts": make_consts(),
    }
    return m


def kernel(**inputs):
    nc = build(KDEBUG)
    in_maps = [_prep_inputs(inputs, c) for c in range(8)]
    res = run_bass_kernel_spmd(nc, in_maps, core_ids=list(range(8)))
    R = res.results
    yp = np.stack([R[c]["yp"] for c in range(8)]).astype(np.float32)
    ys = np.concatenate([R[c]["ys"].reshape(16, 8, 1024) for c in range(8)]).astype(np.float32)
    pr = np.stack([R[c]["pr"] for c in range(8)])[None].astype(np.float32)
    pi = np.stack([R[c]["pi"] for c in range(8)])[None].astype(np.float32)
    pv = np.stack([R[c]["pv"] for c in range(8)])[None].astype(np.float32)
    sr = np.concatenate([R[c]["sr"].reshape(16, 32, 64) for c in range(8)])[None].astype(np.float32)
    si = np.concatenate([R[c]["si"].reshape(16, 32, 64) for c in range(8)])[None].astype(np.float32)
    sv = np.concatenate([R[c]["sv"].reshape(16, 8, 512) for c in range(8)])[None].astype(np.float32)
    return (yp, ys, pr, pi, pv, sr, si, sv)
```

```python
import os
import numpy as np
from contextlib import ExitStack
import concourse.bass as bass
import concourse.mybir as mybir
from concourse.bass_utils import run_bass_kernel_spmd

F32 = mybir.dt.float32
BF16 = mybir.dt.bfloat16
AF = mybir.ActivationFunctionType
ALU = mybir.AluOpType

NT = 17
TOK = 2176
NFF = 22
EPS = 1e-6
PI = float(np.pi)
TWO_PI = 2.0 * PI
MAGIC = 12582912.0
CW_C1 = 6.28125
CW_C2 = TWO_PI - 6.28125
PI_LO = 3.1415925

C_ID = 0
C_TM = 128
C_CM = 256
C_IO = 384
C_EPS = 640
C_EPS4 = 641
C_TAU = 642
C_ONE = 659
C_PM = 723
C_HPI = 725
CW = 726

KDEBUG = os.environ.get("KDEBUG", "")


def make_consts():
    c = np.zeros((128, CW), np.float32)
    c[:, C_ID:C_ID + 128] = np.eye(128, dtype=np.float32)
    p = np.arange(128)
    c[:, C_TM:C_TM + 128] = (p[None, :] // 16 >= p[:, None] // 16).astype(np.float32)
    c[:, C_CM:C_CM + 128] = (p[None, :] >= p[:, None]).astype(np.float32)
    c[:, C_IO:C_IO + 256] = np.arange(256, dtype=np.float32)[None, :]
    c[:, C_EPS] = EPS
    c[:, C_EPS4] = 4 * EPS
    c[:, C_TAU:C_TAU + 17] = np.array(list(range(9)) + list(range(7, -1, -1)), np.float32)[None, :]
    c[:, C_ONE:C_ONE + 64] = 1.0
    c[0:64, C_PM] = 1.0
    c[64:128, C_PM + 1] = 1.0
    c[:, C_HPI] = PI / 2
    return c


class Res:
    __slots__ = ("name", "w", "rs")

    def __init__(self, name):
        self.name = name
        self.w = None
        self.rs = {}


class Eng:
    def __init__(self, name, sem):
        self.name = name
        self.sem = sem
        self.cnt = 0
        self.waited = {}
        self.ops = []


class DSem:
    def __init__(self, sem):
        self.sem = sem
        self.cnt = 0
        self.last = None


class _Single:
    def __init__(self, fn):
        self.fn = fn

    def __call__(self, e):
        return self.fn(e)


class Prog:
    def __init__(self, nc, es, ndma=40):
        self.nc = nc
        self.E = {n: Eng(n, es.enter_context(nc.semaphore("sem_" + n))) for n in ("pe", "act", "dve", "pool", "sp")}
        self.dsems_hw = [DSem(es.enter_context(nc.semaphore("dq%d" % i))) for i in range(ndma)]
        self.dsems_sw = [DSem(es.enter_context(nc.semaphore("dw%d" % i))) for i in range(16)]
        self.dsems = self.dsems_hw + self.dsems_sw
        self.rr = 0
        self.rr_sw = 0
        self.res = {}
        self.out_toks = []
        self.limit = None
        self.nops = 0
        self.lazy = []
        self.last_ds = None

    def r(self, *key):
        x = self.res.get(key)
        if x is None:
            x = Res(key)
            self.res[key] = x
        return x

    def _deps(self, eng, reads, writes, is_dma):
        need = {}

        def add(tok, kind):
            if tok is None:
                return
            sem, val, teng, tdma = tok
            if not tdma and not is_dma and teng == eng:
                if eng == "pe":
                    return
                if kind != "raw" and eng in os.environ.get("KRAWONLY", "").split(","):
                    return
            k = id(sem)
            cur = need.get(k)
            if cur is None or cur[1] < val:
                need[k] = (sem, val)

        for r in reads:
            add(r.w, "raw")
        for w in writes:
            add(w.w, "waw")
            for t in w.rs.values():
                add(t, "war")
        return need

    def _finish(self, E, need, fn, inc, tok, reads, writes):
        waits = []
        for k, (sem, val) in need.items():
            if E.waited.get(k, 0) < val:
                E.waited[k] = val
                waits.append((sem, val))
        E.ops.append((waits, fn, inc))
        k = id(tok[0])
        for r in reads:
            cur = r.rs.get(k)
            if cur is None or cur[1] < tok[1]:
                r.rs[k] = tok
        for w in writes:
            w.w = tok
            w.rs = {}
        return tok

    def op(self, eng, fn, reads=(), writes=()):
        if eng != "pe":
            fn = _Single(fn)
        self.nops += 1
        if self.limit is not None and self.nops > self.limit:
            return None
        E = self.E[eng]
        need = self._deps(eng, reads, writes, False)
        E.cnt += 1
        tok = (E.sem, E.cnt, eng, False)
        return self._finish(E, need, fn, (E.sem, 1), tok, reads, writes)

    def dma(self, q, fns, reads=(), writes=(), out=False):
        if not isinstance(fns, (list, tuple)):
            fns = [fns]
        self.nops += 1
        if self.limit is not None and self.nops > self.limit and not out:
            return None
        E = self.E[q]
        need = self._deps(q, reads, writes, True)
        if q == "pool":
            ds = self.dsems_sw[self.rr_sw % len(self.dsems_sw)]
            self.rr_sw += 1
        else:
            ds = self.dsems_hw[self.rr % len(self.dsems_hw)]
            self.rr += 1
        if ds.last is not None:
            k = id(ds.sem)
            cur = need.get(k)
            if cur is None or cur[1] < ds.last[1]:
                need[k] = (ds.sem, ds.last[1])
        ds.cnt += 16 * len(fns)
        tok = (ds.sem, ds.cnt, q, True)
        ds.last = tok
        self.last_ds = ds
        if ds in self.lazy:
            self.lazy.remove(ds)

        def fn(e, fns=fns, sem=ds.sem):
            last = None
            for f in fns:
                last = f(e)
                last.then_inc(sem, 16)
            return None

        self._finish(E, need, fn, None, tok, reads, writes)
        if out:
            self.out_toks.append(tok)
        return tok

    def wait_all_dma(self, q="sp"):
        E = self.E[q]
        waits = []
        for ds in self.dsems:
            if ds in self.lazy:
                continue
            if ds.last is not None and E.waited.get(id(ds.sem), 0) < ds.last[1]:
                E.waited[id(ds.sem)] = ds.last[1]
                waits.append((ds.sem, ds.last[1]))
        if waits:
            E.ops.append((waits, None, None))

    def emit(self):
        self.wait_all_dma("sp")
        with self.nc.Block() as block:
            regs = (("pe", block.tensor), ("act", block.scalar), ("dve", block.vector),
                    ("pool", block.gpsimd), ("sp", block.sync))
            for name, reg in regs:
                E = self.E[name]
                ops = E.ops
                E.ops = []
                if not ops:
                    continue

                def f(e, ops=ops):
                    for waits, fn, inc in ops:
                        attach = None
                        if isinstance(fn, _Single) and waits:
                            attach = waits[-1]
                            waits = waits[:-1]
                        for sem, val in waits:
                            e.wait_ge(sem, val)
                        if fn is None:
                            continue
                        ins = fn(e)
                        if attach is not None:
                            ins._wait_ge(attach[0], attach[1])
                        if inc is not None:
                            ins.then_inc(inc[0], inc[1])

                reg(f)


def build(debug=""):
    nc = bass.Bass("TRN2", target_bir_lowering=False)
    D = {}

    def din(name, shape):
        D[name] = nc.dram_tensor(name, list(shape), F32, kind="ExternalInput").ap()

    def dout(name, shape):
        D[name] = nc.dram_tensor(name, list(shape), F32, kind="ExternalOutput").ap()

    din("xp", (2048, 1024)); din("xs", (128, 1024)); din("h0r", (16, 2048)); din("h0i", (16, 2048))
    din("norm1", (1, 1024)); din("w_in", (1024, 1536)); din("lam_re", (32, 64)); din("lam_im", (32, 64))
    din("log_dt", (1, 32)); din("b_re", (2048, 16)); din("b_im", (2048, 16)); din("c_re", (512, 64)); din("c_im", (512, 64))
    din("d_skip", (1, 512)); din("w_glu", (512, 512)); din("b_glu", (1, 512)); din("cm_ln_g", (1, 512)); din("cm_ln_b", (1, 512))
    din("w_s", (1024, 128)); din("b_s", (8, 128)); din("g_s5", (1, 512)); din("g_cm", (1, 512)); din("w_out", (1024, 1024))
    din("norm2", (1, 1024)); din("w_gate", (1024, 2816)); din("w_up", (1024, 2816)); din("w_down", (2816, 1024)); din("norm_f", (1, 1024))
    din("consts", (128, CW))
    dout("yp", (2048, 1024)); dout("ys", (128, 1024)); dout("pr", (32, 64)); dout("pi", (32, 64)); dout("pv", (128, 512))
    dout("sr", (16, 2048)); dout("si", (16, 2048)); dout("sv", (128, 512))
    scr1 = nc.dram_tensor("scr1", [TOK], F32, kind="Internal").ap()
    scr2 = nc.dram_tensor("scr2", [TOK], F32, kind="Internal").ap()
    dbg = {}

    def xsrc(t):
        return D["xp"][t * 128:(t + 1) * 128, :] if t < 16 else D["xs"][:, :]

    def ydst(t):
        return D["yp"][t * 128:(t + 1) * 128, :] if t < 16 else D["ys"][:, :]

    with ExitStack() as top:
        P = Prog(nc, top)
        r = P.r

        def sb(es, name, shape, dt=F32):
            return es.enter_context(nc.sbuf_tensor(name, list(shape), dt))

        PS = [top.enter_context(nc.psum_tensor("ps%d" % i, [128, 512], F32)) for i in range(8)]
        PSB = [p[:].bitcast(BF16) for p in PS]

        def psr(i):
            return r("ps", i)

        cst = sb(top, "cst", (128, CW))
        identb = sb(top, "identb", (128, 128), BF16)
        onesb = sb(top, "onesb", (1, 128), BF16)
        actA = sb(top, "actA", (128, 8, TOK), BF16)
        mixT = sb(top, "mixT", (128, 8, TOK), BF16)
        mixs5 = mixT[:, 0:4, :]
        mixcm = mixT[:, 4:8, :]
        ssq1 = sb(top, "ssq1", (128, NT)); rstd1 = sb(top, "rstd1", (128, NT))
        ssqcm = sb(top, "ssqcm", (128, NT)); rstdcm = sb(top, "rstdcm", (128, NT))
        ssq5 = sb(top, "ssq5", (128, 24)); ssq5n = sb(top, "ssq5n", (128, NT)); rstd5 = sb(top, "rstd5", (128, NT))
        H0T = [sb(top, "H0T%d" % i, (128, 16, 16)) for i in range(2)]
        lr = sb(top, "lr", (128, 16)); li = sb(top, "li", (128, 16)); ldt = sb(top, "ldt", (128, 16))
        Br = sb(top, "Br", (128, 16, 16)); Bi = sb(top, "Bi", (128, 16, 16))
        Cn = [sb(top, "Cn%d" % i, (128, 4, 64)) for i in range(2)]
        dcol = sb(top, "dcol", (128, 32))
        mid = ExitStack()
        P8 = sb(mid, "P8", (128, 3, 4096), BF16)
        identf = cst[:, C_ID:C_ID + 128]
        epsc = cst[:, C_EPS:C_EPS + 1]
        eps4c = cst[:, C_EPS4:C_EPS4 + 1]

        P.dma("sp", lambda e: e.dma_start(out=cst[:], in_=D["consts"][:, :]), writes=[r("cst")])
        def _ncd(e, **kw):
            with nc.allow_non_contiguous_dma(reason="small strided param load"):
                return e.dma_start(**kw)
        def load_s5_params():
            P.dma("sp", [lambda e: _ncd(e, out=lr[:], in_=D["lam_re"].rearrange("(P e) n -> (e n) P", e=2)),
                          lambda e: _ncd(e, out=li[:], in_=D["lam_im"].rearrange("(P e) n -> (e n) P", e=2)),
                          lambda e: _ncd(e, out=ldt[0:64, :], in_=D["log_dt"][0:1, 0:32:2].partition_broadcast(64)),
                          lambda e: _ncd(e, out=ldt[64:128, :], in_=D["log_dt"][0:1, 1:32:2].partition_broadcast(64))],
                  writes=[r("lam")])
            P.dma("sp", [lambda e: _ncd(e, out=Br[:], in_=D["b_re"].rearrange("(P e n) i -> (e n) P i", e=2, n=64)),
                          lambda e: _ncd(e, out=Bi[:], in_=D["b_im"].rearrange("(P e n) i -> (e n) P i", e=2, n=64)),
                          lambda e: _ncd(e, out=Cn[0][:], in_=D["c_re"].rearrange("(t r) n -> r t n", r=128)),
                          lambda e: _ncd(e, out=Cn[1][:], in_=D["c_im"].rearrange("(t r) n -> r t n", r=128))],
                  writes=[r("BC")])
            P.dma("sp", [(lambda s_: (lambda e: _ncd(e, out=dcol[16 * s_:16 * s_ + 16, :], in_=D["d_skip"][0:1, :].rearrange("o (g i) -> (o i) g", i=16))))(s_)
                          for s_ in range(8)], writes=[r("dcol")])
        P.op("dve", lambda e: e.tensor_copy(out=identb[:], in_=identf), reads=[r("cst")], writes=[r("identb")])
        P.op("dve", lambda e: e.memset(onesb[:], 1.0), writes=[r("onesb")])
        P.op("dve", lambda e: e.memset(ssq5[:], 0.0), writes=[r("ssq5")])

        def pe_transposes(e, out_bf, srcs, npart=128):
            last = None
            for i, s in enumerate(srcs):
                last = e.transpose(out_bf[i], s, identb[0:npart, 0:npart])
            return last

        with ExitStack() as es:
            win = sb(es, "win", (128, 8, 1536), BF16)
            srt1 = sb(es, "srt1", (128, NT)); rstd8 = sb(es, "rstd8", (128, 24))
            g1bc = sb(es, "g1bc", (128, 1024)); lngbc = sb(es, "lngbc", (128, 512)); lnbbc = sb(es, "lnbbc", (128, 512))
            gcmbc = sb(es, "gcmbc", (128, 512))
            biasP = sb(es, "biasP", (128, 512)); biasS = sb(es, "biasS", (128, 512))
            bsT = sb(es, "bsT", (128, 8)); bsTs = sb(es, "bsTs", (128, 8))
            WT = sb(es, "WT", (128, 8, 128), BF16); WTs = sb(es, "WTs", (128, 8, 128), BF16)
            xt = [sb(es, "xt%d" % i, (128, 1024)) for i in range(2)]
            hb = [sb(es, "hb%d" % i, (128, 1024), BF16) for i in range(2)]
            junk = sb(es, "junk", (128, 1024), BF16)
            wsn = junk[:].rearrange("p (h s) -> p h s", h=8)
            NSL = 8
            ut = sb(es, "ut", (128, NSL, 512)); vt = sb(es, "vt", (128, NSL, 512))
            vbf = sb(es, "vbf", (128, NSL, 512), BF16)
            tmpc = [sb(es, "tmpc0", (128, 512))] * 2
            vout = tmpc
            ob = [sb(es, "ob%d" % i, (128, 512), BF16) for i in range(4)]
            st6 = sb(es, "st6", (128, NT, 6)); mv = sb(es, "mv", (128, NT, 2))
            sdln = sb(es, "sdln", (128, NT)); rsln = sb(es, "rsln", (128, NT))

            P.dma("pool", [lambda e: e.dma_start(out=wsn, in_=D["w_s"].rearrange("(h t) s -> t h s", h=8))], writes=[r("junk")])
            wv = D["w_in"].rearrange("(kt p) n -> p kt n", p=128)
            for k_ in range(8):
                P.dma("pool", [lambda e, k_=k_: e.dma_start(out=win[:, k_, 512:1536], in_=wv[:, k_, 512:1536])], writes=[r("win_uv", k_)])
            P.dma("pool", [lambda e: e.dma_start(out=win[:, :, 0:512], in_=wv[:, :, 0:512])], writes=[r("win_s5")])
            P.dma("sp", [lambda e: e.dma_start(out=g1bc[:], in_=D["norm1"][0:1, :].partition_broadcast(128))], writes=[r("bc1")])
            for t_ in range(2):
                P.dma("sp", [lambda e, t_=t_: e.dma_start(out=xt[t_ % 2][:], in_=xsrc(t_))], writes=[r("xt", t_ % 2)])
            P.dma("sp", [lambda e: e.dma_start(out=lngbc[:], in_=D["cm_ln_g"][0:1, :].partition_broadcast(128)),
                         lambda e: e.dma_start(out=lnbbc[:], in_=D["cm_ln_b"][0:1, :].partition_broadcast(128)),
                         lambda e: e.dma_start(out=gcmbc[:], in_=D["g_cm"][0:1, :].partition_broadcast(128))],
                  writes=[r("bc")])

            def ld_bs(e):
                with nc.allow_non_contiguous_dma(reason="tiny bias transposes"):
                    last = e.dma_start(out=bsT[:], in_=D["b_s"].rearrange("h t -> t h"))
                return last
            P.dma("sp", [ld_bs], writes=[r("bsT")])

            def ld_bss(q):
                def f(e):
                    with nc.allow_non_contiguous_dma(reason="tiny bias transposes"):
                        return e.dma_start(out=bsTs[8 * q:8 * q + 8, :], in_=D["b_s"][:, 0:8].rearrange("h t -> t h"))
                return f
            P.dma("act", [ld_bss(q) for q in range(16)], writes=[r("bsTs")])
            P.op("dve", lambda e: e.tensor_copy(out=biasP[:].rearrange("p (h d) -> p h d", h=8), in_=bsT[:].unsqueeze(2).to_broadcast([128, 8, 64])),
                 reads=[r("bsT")], writes=[r("biasP")])
            P.op("dve", lambda e: e.tensor_copy(out=biasS[:].rearrange("p (h d) -> p h d", h=8), in_=bsTs[:].unsqueeze(2).to_broadcast([128, 8, 64])),
                 reads=[r("bsTs")], writes=[r("biasS")])
            P.op("pe", lambda e: pe_transposes(e, [PSB[0][:, h * 128:(h + 1) * 128] for h in range(8)], [wsn[:, h, :] for h in range(8)]),
                 reads=[r("junk"), r("identb")], writes=[psr(0)])
            P.op("dve", lambda e: e.tensor_tensor(out=WT[:], in0=PSB[0][:, 0:1024].rearrange("p (h t) -> p h t", h=8),
                                                  in1=cst[:, C_CM:C_CM + 128].unsqueeze(1).to_broadcast([128, 8, 128]), op=ALU.mult),
                 reads=[psr(0), r("cst")], writes=[r("WT")])
            def build_WTs():
                P.op("dve", lambda e: e.memset(WTs[:], 0.0), writes=[r("WTs")])
                P.dma("sp", [(lambda q: (lambda e: e.dma_start(out=WTs[8 * q:8 * q + 8, :, 8 * q:8 * q + 8], in_=WT[0:8, :, 0:8])))(q) for q in range(16)],
                      reads=[r("WT")], writes=[r("WTs")])

            groups = [[0, 1, 2, 3], [4, 5, 6, 7], [8, 9, 10, 11], [12, 13, 14, 15], [16]]
            slot_of = {}
            for gi, g in enumerate(groups):
                for j, t in enumerate(g):
                    slot_of[t] = (gi % 2) * 4 + j

            def stage_A(g):
                for t in g:
                    xs_ = xt[t % 2]; rx = r("xt", t % 2); hbt = hb[t % 2]; rhb = r("hb", t % 2); pb = t % 2
                    if t >= 2:
                        P.dma("sp", [lambda e, xs_=xs_, t=t: e.dma_start(out=xs_[:], in_=xsrc(t))], writes=[rx])
                    P.op("act", lambda e, xs_=xs_, t=t: e.activation(out=junk[:], in_=xs_[:], func=AF.Square, accum_out=ssq1[:, t:t + 1]),
                         reads=[rx], writes=[r("junk"), r("ssq1", t)])
                    P.op("dve", lambda e, xs_=xs_, hbt=hbt: e.tensor_tensor(out=hbt[:], in0=xs_[:], in1=g1bc[:], op=ALU.mult),
                         reads=[rx, r("bc1")], writes=[rhb])
                    P.op("pe", lambda e, hbt=hbt, pb=pb: pe_transposes(e, [PSB[pb][:, k * 128:(k + 1) * 128] for k in range(8)],
                                                                   [hbt[:, k * 128:(k + 1) * 128] for k in range(8)]),
                         reads=[rhb, r("identb")], writes=[psr(pb)])
                    P.op("dve", lambda e, t=t, pb=pb: e.tensor_copy(out=actA[:, :, t * 128:(t + 1) * 128],
                                                                  in_=PSB[pb][:, 0:1024].rearrange("p (k c) -> p k c", k=8)),
                         reads=[psr(pb)], writes=[r("hT", t)])
                c0, c1 = g[0], g[-1] + 1
                P.op("act", lambda e: e.activation(out=srt1[:, c0:c1], in_=ssq1[:, c0:c1], func=AF.Sqrt, bias=epsc, scale=1.0 / 1024),
                     reads=[r("ssq1", t) for t in g] + [r("cst")], writes=[r("srt1", c0)])
                P.op("dve", lambda e: e.reciprocal(out=rstd1[:, c0:c1], in_=srt1[:, c0:c1]), reads=[r("srt1", c0)], writes=[r("rstd1", t) for t in g])

            def stage_B1(g):
                for t in g:
                    sl = slot_of[t]; bu = 2 + 2 * (t % 2); bv = bu + 1

                    def mm(e, t=t, bank=bu, c0=512):
                        last = None
                        for k in range(8):
                            last = e.matmul(PS[bank][:, :], lhsT=actA[:, k, t * 128:(t + 1) * 128], rhs=win[:, k, c0:c0 + 512],
                                            start=(k == 0), stop=(k == 7))
                        return last
                    P.op("pe", lambda e, t=t, bu=bu: mm(e, t, bu, 512), reads=[r("hT", t)] + [r("win_uv", k_) for k_ in range(8)], writes=[psr(bu)])
                    P.op("pe", lambda e, t=t, bv=bv: mm(e, t, bv, 1024), reads=[r("hT", t)] + [r("win_uv", k_) for k_ in range(8)], writes=[psr(bv)])
                    P.op("act", lambda e, t=t, sl=sl, bu=bu: e.activation(out=ut[:, sl, :], in_=PS[bu][:, :], func=AF.Gelu_apprx_tanh, scale=rstd1[:, t:t + 1]),
                         reads=[psr(bu), r("rstd1", t)], writes=[r("ut", sl)])
                    P.op("act", lambda e, t=t, sl=sl, bv=bv: e.activation(out=vt[:, sl, :], in_=PS[bv][:, :], func=AF.Gelu_apprx_tanh, scale=rstd1[:, t:t + 1]),
                         reads=[psr(bv), r("rstd1", t)], writes=[r("vt", sl)])
                    P.op("dve", lambda e, t=t, sl=sl: e.bn_stats(out=st6[:, t, :], in_=vt[:, sl, :]), reads=[r("vt", sl)], writes=[r("st6", t)])
                    P.op("dve", lambda e, t=t: e.bn_aggr(out=mv[:, t, :], in_=st6[:, t, :]), reads=[r("st6", t)], writes=[r("mv", t)])

            def stage_LN(g):
                c0, c1 = g[0], g[-1] + 1
                P.op("act", lambda e: e.activation(out=sdln[:, c0:c1], in_=mv[:, c0:c1, 1], func=AF.Sqrt, bias=epsc, scale=1.0),
                     reads=[r("mv", t) for t in g] + [r("cst")], writes=[r("sdln", c0)])
                P.op("dve", lambda e: e.reciprocal(out=rsln[:, c0:c1], in_=sdln[:, c0:c1]), reads=[r("sdln", c0)], writes=[r("rsln", c0)])
                for t in g:
                    sl = slot_of[t]
                    P.op("dve", lambda e, t=t, sl=sl: e.tensor_scalar(out=vt[:, sl, :], in0=vt[:, sl, :], scalar1=mv[:, t, 0:1], scalar2=rsln[:, t:t + 1],
                                                                    op0=ALU.subtract, op1=ALU.mult),
                         reads=[r("vt", sl), r("mv", t), r("rsln", c0)], writes=[r("vt", sl)])
                    lne = "dve" if t % 2 == 0 else "pool"
                    P.op(lne, lambda e, sl=sl: e.tensor_tensor(out=vt[:, sl, :], in0=vt[:, sl, :], in1=lngbc[:], op=ALU.mult),
                         reads=[r("vt", sl), r("bc")], writes=[r("vt", sl)])
                    P.op(lne, lambda e, sl=sl: e.tensor_tensor(out=vbf[:, sl, :], in0=vt[:, sl, :], in1=lnbbc[:], op=ALU.add),
                         reads=[r("vt", sl), r("bc")], writes=[r("vbf", sl)])
                    if t >= 15:
                        vo = vout[t - 15]; dn = "pv" if t == 15 else "sv"
                        P.op("pool", lambda e, sl=sl, vo=vo: e.tensor_tensor(out=vo[:], in0=vt[:, sl, :], in1=lnbbc[:], op=ALU.add),
                             reads=[r("vt", sl), r("bc")], writes=[r("tmpc")])
                        P.dma("sp", [lambda e, vo=vo, dn=dn: e.dma_start(out=D[dn][:, :], in_=vo[:])], reads=[r("tmpc")], out=True)

            def stage_C1(g):
                for j_, t in enumerate(g):
                    sl = slot_of[t]; Wm = WT if t < 16 else WTs; bias = biasP if t < 16 else biasS
                    rW = r("WT") if t < 16 else r("WTs"); rb = r("biasP") if t < 16 else r("biasS")
                    tc_ = tmpc[0]; rtc = r("tmpc"); obt = ob[j_]; rob = r("ob", j_); mb = 6 + t % 2

                    def mm(e, sl=sl, Wm=Wm, mb=mb):
                        last = None
                        for h in range(8):
                            last = e.matmul(PS[mb][:, h * 64:(h + 1) * 64], lhsT=Wm[:, h, :], rhs=vbf[:, sl, h * 64:(h + 1) * 64], start=True, stop=True)
                        return last
                    P.op("pe", mm, reads=[r("vbf", sl), rW], writes=[psr(mb)])
                    P.op("dve", lambda e, tc_=tc_, bias=bias, mb=mb: e.tensor_tensor(out=tc_[:], in0=PS[mb][:, :], in1=bias[:], op=ALU.add),
                         reads=[psr(mb), rb], writes=[rtc])
                    P.op("dve", lambda e, tc_=tc_, sl=sl: e.tensor_tensor(out=ut[:, sl, :], in0=tc_[:], in1=ut[:, sl, :], op=ALU.mult),
                         reads=[rtc, r("ut", sl)], writes=[r("ut", sl)])
                    P.op("act", lambda e, t=t, sl=sl: e.activation(out=junk[:, 0:512], in_=ut[:, sl, :], func=AF.Square, accum_out=ssqcm[:, t:t + 1]),
                         reads=[r("ut", sl)], writes=[r("junk"), r("ssqcm", t)])
                    P.op("pool", lambda e, sl=sl, obt=obt: e.tensor_tensor(out=obt[:], in0=ut[:, sl, :], in1=gcmbc[:], op=ALU.mult),
                         reads=[r("ut", sl), r("bc")], writes=[rob])

            def stage_C2(g):
                for j_, t in enumerate(g):
                    obt = ob[j_]; rob = r("ob", j_); tb = t % 2
                    P.op("pe", lambda e, obt=obt, tb=tb: pe_transposes(e, [PSB[tb][:, j * 128:(j + 1) * 128] for j in range(4)],
                                                                     [obt[:, j * 128:(j + 1) * 128] for j in range(4)]),
                         reads=[rob, r("identb")], writes=[psr(tb)])
                    P.op("act", lambda e, t=t, tb=tb: e.activation(out=mixcm[:, :, t * 128:(t + 1) * 128], in_=PSB[tb][:, 0:512].rearrange("p (j c) -> p j c", j=4),
                                                                   func=AF.Copy),
                         reads=[psr(tb)], writes=[r("mixcm", t)])

            stage_A(groups[0])
            NG = len(groups)
            for gi, g in enumerate(groups):
                if gi + 1 < NG:
                    stage_A(groups[gi + 1])
                if gi == 1:
                    build_WTs()
                if gi == NG - 2:
                    load_s5_params()
                stage_B1(g)
                if gi >= 2:
                    stage_C2(groups[gi - 2])
                if gi >= 1:
                    stage_C1(groups[gi - 1])
                stage_LN(g)
            if NG >= 2:
                stage_C2(groups[NG - 2])
            stage_C1(groups[NG - 1])
            stage_C2(groups[NG - 1])

            def st_rs(e):
                with nc.allow_non_contiguous_dma(reason="tiny stat relayout"):
                    return e.dma_start(out=scr1.rearrange("(t p) -> p t", p=128), in_=rstd1[:])
            P.dma("act", [st_rs], reads=[r("rstd1", t) for t in range(NT)], writes=[r("scr1")])

            P.dma("act", [lambda e: e.dma_start(out=rstd8[:, 0:16].rearrange("p (a s) -> p a s", a=2),
                                               in_=scr1[0:2048].rearrange("(a b s) -> b a s", a=2, s=8)),
                         lambda e: e.dma_start(out=rstd8[0:16, 16:24], in_=scr1[2048:2176].rearrange("(q s) -> q s", s=8))],
                  reads=[r("scr1")], writes=[r("rstd8")])
            P8v = P8[:].rearrange("p a (g s i) -> p a g s i", g=32, s=8)
            for bt in range(3):
                npart = 128 if bt < 2 else 16
                for s in range(8):
                    idx = bt * 8 + s; bank = 2 + (idx % 4)
                    if bt < 2:
                        tiles = range(bt * 8, bt * 8 + 8)
                        base = bt * 1024 + s; end = bt * 1024 + 1024
                    else:
                        tiles = [16]
                        base = 2048 + s; end = 2176

                    def mm(e, bank=bank, base=base, end=end, npart=npart):
                        last = None
                        for k in range(8):
                            last = e.matmul(PS[bank][0:npart, :], lhsT=actA[:, k, base:end:8], rhs=win[:, k, 0:512], start=(k == 0), stop=(k == 7))
                        return last
                    P.op("pe", mm, reads=[r("hT", t) for t in tiles] + [r("win_s5")], writes=[psr(bank)])
                    eng = "act" if s % 2 == 0 else "dve"
                    if eng == "act":
                        P.op("act", lambda e, bank=bank, bt=bt, s=s, idx=idx, npart=npart: e.activation(
                            out=P8v[0:npart, bt, :, s, :], in_=PS[bank][0:npart, :].rearrange("p (g i) -> p g i", g=32), func=AF.Copy,
                            scale=rstd8[0:npart, idx:idx + 1]), reads=[psr(bank), r("rstd8")], writes=[r("P8", bt, s)])
                    else:
                        P.op("dve", lambda e, bank=bank, bt=bt, s=s, idx=idx, npart=npart: e.tensor_scalar(
                            out=P8v[0:npart, bt, :, s, :], in0=PS[bank][0:npart, :].rearrange("p (g i) -> p g i", g=32),
                            scalar1=rstd8[0:npart, idx:idx + 1], scalar2=None, op0=ALU.mult), reads=[psr(bank), r("rstd8")], writes=[r("P8", bt, s)])
            if "ab" in debug:
                dbg["hT"] = nc.dram_tensor("dbg_hT", [128, 8 * TOK], BF16, kind="ExternalOutput").ap()
                dbg["mixcm"] = nc.dram_tensor("dbg_mixcm", [128, 4, TOK], BF16, kind="ExternalOutput").ap()
                dbg["P8"] = nc.dram_tensor("dbg_P8", [128, 3 * 4096], BF16, kind="ExternalOutput").ap()
                P.dma("sp", [lambda e: e.dma_start(out=dbg["hT"][:, :], in_=actA[:].rearrange("p k t -> p (k t)"))], reads=[r("hT", t) for t in range(NT)], out=True)
                P.dma("sp", [lambda e: e.dma_start(out=dbg["mixcm"][:, :, :], in_=mixT[:, 4:8, :])], reads=[r("mixcm", t) for t in range(NT)], out=True)
                P.dma("sp", [lambda e: e.dma_start(out=dbg["P8"][:, :], in_=P8[:].rearrange("p a c -> p (a c)"))], reads=[r("P8", bt, s) for bt in range(3) for s in range(8)], out=True)
            P.emit()
        if debug == "ab":
            mid.close()
            return nc

        scrF = actA[:].rearrange("p k t -> p (k t)").bitcast(F32)
        scrB = actA[:].rearrange("p k t -> p (k t)")

        def dv(fn, reads, writes, eng="dve"):
            return P.op(eng, fn, reads=[r(*x) if isinstance(x, tuple) else r(x) for x in reads],
                        writes=[r(*x) if isinstance(x, tuple) else r(x) for x in writes])

        def ncdma(e, **kw):
            with nc.allow_non_contiguous_dma(reason="small strided param load"):
                return e.dma_start(**kw)

        W1re = sb(mid, "W1re", (128, 32, 128), BF16); W1im = sb(mid, "W1im", (128, 32, 128), BF16)
        Tm = sb(mid, "Tm", (128, 32, 128), BF16)
        W2rb = sb(mid, "W2rb", (128, 16, 128), BF16); W2ib = sb(mid, "W2ib", (128, 16, 128), BF16)
        s5c = sb(mid, "s5c", (128, 4, 16))
        wglu = sb(mid, "wglu", (128, 4, 512), BF16)
        bglub = sb(mid, "bglub", (1, 512), BF16)
        gs5bc = sb(mid, "gs5bc", (128, 512))
        with ExitStack() as es:
            NTAU = 17
            dtt = sb(es, "dtt", (128, 16))
            lrdt = sb(es, "lrdt", (128, 16)); lidt = sb(es, "lidt", (128, 16))
            CTn = [sb(es, "CTn%d" % i, (64, 512)) for i in range(2)]
            CT = [sb(es, "CT%d" % i, (128, 16, 16)) for i in range(2)]
            tabs = {n: sb(es, "tab_" + n, (128, NTAU, 16)) for n in
                    ("ARGM", "ANG", "MAGP", "MAGM", "SIN", "COS", "APR", "API", "AMR", "AMI", "RS", "RC", "K", "Y")}
            sm = {n: sb(es, "sm_" + n, (128, 16)) for n in ("am1", "den", "t", "rden", "qr", "qi", "u1", "u2")}
            QB = [sb(es, "QB%d" % i, (128, 16, 16)) for i in range(2)]
            big = {n: sb(es, "big_" + n, (128, 16, 8, 16)) for n in ("W2r", "W2i")}
            for k_, n in enumerate(("Lr", "Li", "t1", "t2")):
                big[n] = scrF[:, k_ * 2048:(k_ + 1) * 2048].rearrange("p (P s o) -> p P s o", P=16, s=8)
            big["M1r"] = big["Lr"]; big["M1i"] = big["Li"]
            tmpT = [sb(es, "tmpT%d" % i, (128, 4, 128)) for i in range(2)]
            Lm = [sb(es, "Lm%d" % i, (128, 4, 2, 128)) for i in range(2)]
            bglu32 = sb(es, "bglu32", (1, 512))

            if os.environ.get("KSTOP"):
                P.nops = 0
                P.limit = int(os.environ["KSTOP"])
            for _ in range(int(os.environ.get("KPAD", "0"))):
                P.E["sp"].ops.append(([(P.E["sp"].sem, 0)], None, None))
            P.dma("sp", [(lambda c_, Pp: (lambda e: _ncd(e, out=H0T[c_][:, Pp, :],
                                                         in_=D["h0r" if c_ == 0 else "h0i"][:, Pp * 128:(Pp + 1) * 128].rearrange("q p -> p q"))))(c_, Pp)
                         for c_ in (0,) for Pp in range(16)], writes=[r("H0T", 0)])
            P.lazy.append(P.last_ds)
            P.dma("pool", [lambda e: e.dma_start(out=wglu[:], in_=D["w_glu"].rearrange("(kt p) n -> p kt n", p=128)),
                           lambda e: e.dma_start(out=bglub[:], in_=D["b_glu"][0:1, :])], writes=[r("wglu")])
            P.dma("pool", [lambda e: e.dma_start(out=gs5bc[:], in_=D["g_s5"][0:1, :].partition_broadcast(128))], writes=[r("gs5bc")])

            T = tabs
            dv(lambda e: e.activation(out=dtt[:], in_=ldt[:], func=AF.Exp), ["lam"], ["dtt"], "act")
            dv(lambda e: e.tensor_tensor(out=lrdt[:], in0=lr[:], in1=dtt[:], op=ALU.mult), ["lam", "dtt"], ["lrdt"])
            dv(lambda e: e.tensor_tensor(out=lidt[:], in0=li[:], in1=dtt[:], op=ALU.mult), ["lam", "dtt"], ["lidt"])
            taub = cst[:, C_TAU:C_TAU + NTAU].unsqueeze(2).to_broadcast([128, NTAU, 16])
            dv(lambda e: e.tensor_tensor(out=T["ARGM"][:], in0=taub, in1=lrdt[:].unsqueeze(1).to_broadcast([128, NTAU, 16]), op=ALU.mult),
               ["cst", "lrdt"], ["ARGM"])
            dv(lambda e: e.tensor_tensor(out=T["ANG"][:], in0=taub, in1=lidt[:].unsqueeze(1).to_broadcast([128, NTAU, 16]), op=ALU.mult),
               ["cst", "lidt"], ["ANG"])
            dv(lambda e: e.activation(out=T["MAGP"][:], in_=T["ARGM"][:], func=AF.Exp), ["ARGM"], ["MAGP"], "act")
            dv(lambda e: e.activation(out=T["MAGM"][:], in_=T["ARGM"][:], func=AF.Exp, scale=-1.0), ["ARGM"], ["MAGM"], "act")

            def range_reduce(dst, src, shift, rn_dst, rn_src, Y, K, rY, rK, eng="dve"):
                if shift != 0.0:
                    dv(lambda e: e.tensor_scalar_add(out=Y, in0=src, scalar1=shift), [rn_src], [rY], eng)
                    y = Y; ry = rY
                else:
                    y = src; ry = rn_src
                dv(lambda e: e.tensor_scalar(out=K, in0=y, scalar1=1.0 / TWO_PI, scalar2=MAGIC, op0=ALU.mult, op1=ALU.add), [ry], [rK], eng)
                dv(lambda e: e.tensor_scalar_add(out=K, in0=K, scalar1=-MAGIC), [rK], [rK], eng)
                dv(lambda e: e.scalar_tensor_tensor(out=dst, in0=K, scalar=-CW_C1, in1=y, op0=ALU.mult, op1=ALU.add), [rK, ry], [rn_dst], "dve")
                dv(lambda e: e.scalar_tensor_tensor(out=dst, in0=K, scalar=-CW_C2, in1=dst, op0=ALU.mult, op1=ALU.add), [rK, rn_dst], [rn_dst], "dve")
                dv(lambda e: e.tensor_scalar(out=dst, in0=dst, scalar1=PI_LO, scalar2=-PI_LO, op0=ALU.min, op1=ALU.max), [rn_dst], [rn_dst], eng)

            range_reduce(T["RS"][:], T["ANG"][:], 0.0, "RS", "ANG", T["Y"][:], T["K"][:], "Y", "K")
            dv(lambda e: e.activation(out=T["SIN"][:], in_=T["RS"][:], func=AF.Sin), ["RS"], ["SIN"], "act")
            range_reduce(T["RC"][:], T["ANG"][:], PI / 2, "RC", "ANG", T["Y"][:], T["K"][:], "Y", "K")
            dv(lambda e: e.activation(out=T["COS"][:], in_=T["RC"][:], func=AF.Sin), ["RC"], ["COS"], "act")
            dv(lambda e: e.tensor_tensor(out=T["APR"][:], in0=T["MAGP"][:], in1=T["COS"][:], op=ALU.mult), ["MAGP", "COS"], ["APR"])
            dv(lambda e: e.tensor_tensor(out=T["API"][:], in0=T["MAGP"][:], in1=T["SIN"][:], op=ALU.mult), ["MAGP", "SIN"], ["API"])
            dv(lambda e: e.tensor_tensor(out=T["AMR"][:], in0=T["MAGM"][:], in1=T["COS"][:], op=ALU.mult), ["MAGM", "COS"], ["AMR"])
            dv(lambda e: e.scalar_tensor_tensor(out=T["AMI"][:], in0=T["MAGM"][:], scalar=-1.0, in1=T["SIN"][:], op0=ALU.mult, op1=ALU.mult),
               ["MAGM", "SIN"], ["AMI"])
            dv(lambda e: e.tensor_copy(out=s5c[:, 0, :], in_=T["MAGP"][:, 8, :]), ["MAGP"], ["s5c"])
            dv(lambda e: e.tensor_copy(out=s5c[:, 1, :], in_=T["RS"][:, 8, :]), ["RS"], ["s5c"])
            dv(lambda e: e.tensor_copy(out=s5c[:, 2, :], in_=T["APR"][:, 8, :]), ["APR"], ["s5c"])
            dv(lambda e: e.tensor_copy(out=s5c[:, 3, :], in_=T["API"][:, 8, :]), ["API"], ["s5c"])
            ar1 = T["APR"][:, 1, :]; ai1 = T["API"][:, 1, :]
            dv(lambda e: e.tensor_scalar_add(out=sm["am1"][:], in0=ar1, scalar1=-1.0), ["APR"], ["am1"])
            dv(lambda e: e.tensor_tensor(out=sm["den"][:], in0=lr[:], in1=lr[:], op=ALU.mult), ["lam"], ["den"])
            dv(lambda e: e.tensor_tensor(out=sm["t"][:], in0=li[:], in1=li[:], op=ALU.mult), ["lam"], ["t"])
            dv(lambda e: e.tensor_tensor(out=sm["den"][:], in0=sm["den"][:], in1=sm["t"][:], op=ALU.add), ["den", "t"], ["den"])
            dv(lambda e: e.reciprocal(out=sm["rden"][:], in_=sm["den"][:]), ["den"], ["rden"])
            dv(lambda e: e.tensor_tensor(out=sm["u1"][:], in0=sm["am1"][:], in1=lr[:], op=ALU.mult), ["am1", "lam"], ["u1"])
            dv(lambda e: e.tensor_tensor(out=sm["u2"][:], in0=ai1, in1=li[:], op=ALU.mult), ["API", "lam"], ["u2"])
            dv(lambda e: e.tensor_tensor(out=sm["u1"][:], in0=sm["u1"][:], in1=sm["u2"][:], op=ALU.add), ["u1", "u2"], ["u1"])
            dv(lambda e: e.tensor_tensor(out=sm["qr"][:], in0=sm["u1"][:], in1=sm["rden"][:], op=ALU.mult), ["u1", "rden"], ["qr"])
            dv(lambda e: e.tensor_tensor(out=sm["u1"][:], in0=ai1, in1=lr[:], op=ALU.mult), ["API", "lam", "qr"], ["u1"])
            dv(lambda e: e.tensor_tensor(out=sm["u2"][:], in0=sm["am1"][:], in1=li[:], op=ALU.mult), ["am1", "lam"], ["u2"])
            dv(lambda e: e.tensor_tensor(out=sm["u1"][:], in0=sm["u1"][:], in1=sm["u2"][:], op=ALU.subtract), ["u1", "u2"], ["u1"])
            dv(lambda e: e.tensor_tensor(out=sm["qi"][:], in0=sm["u1"][:], in1=sm["rden"][:], op=ALU.mult), ["u1", "rden"], ["qi"])
            qrb = sm["qr"][:].unsqueeze(2).to_broadcast([128, 16, 16]); qib = sm["qi"][:].unsqueeze(2).to_broadcast([128, 16, 16])
            t1s = big["t1"][:, :, 0, :]; t2s = big["t2"][:, :, 0, :]
            dv(lambda e: e.tensor_tensor(out=t1s, in0=Br[:], in1=qrb, op=ALU.mult), ["BC", "qr"], ["t1"])
            dv(lambda e: e.tensor_tensor(out=t2s, in0=Bi[:], in1=qib, op=ALU.mult), ["BC", "qi"], ["t2"])
            dv(lambda e: e.tensor_tensor(out=QB[0][:], in0=t1s, in1=t2s, op=ALU.subtract), ["t1", "t2"], ["QB0"])
            dv(lambda e: e.tensor_tensor(out=t1s, in0=Bi[:], in1=qrb, op=ALU.mult), ["BC", "qr", "QB0"], ["t1"])
            dv(lambda e: e.tensor_tensor(out=t2s, in0=Br[:], in1=qib, op=ALU.mult), ["BC", "qi", "QB0"], ["t2"])
            dv(lambda e: e.tensor_tensor(out=QB[1][:], in0=t1s, in1=t2s, op=ALU.add), ["t1", "t2"], ["QB1"])
            for c_ in range(2):
                def mmct(e, c_=c_):
                    last = None
                    for t_ in range(4):
                        last = e.matmul(PS[c_][0:64, t_ * 128:(t_ + 1) * 128], lhsT=Cn[c_][:, t_, :], rhs=identf, start=True, stop=True)
                    return last
                P.op("pe", mmct, reads=[r("BC"), r("cst")], writes=[psr(c_)])
                dv(lambda e, c_=c_: e.tensor_copy(out=CTn[c_][:], in_=PS[c_][0:64, :]), [("ps", c_)], [("CTn", c_)])
                ctv = CTn[c_][:].rearrange("n (P e o) -> n P e o", e=2, o=16)
                P.dma("act", [lambda e, c_=c_, ctv=ctv: e.dma_start(out=CT[c_][0:64, :, :], in_=ctv[:, :, 0, :]),
                             lambda e, c_=c_, ctv=ctv: e.dma_start(out=CT[c_][64:128, :, :], in_=ctv[:, :, 1, :])],
                      reads=[r("CTn", c_)], writes=[r("CT", c_)])

            def tauv(name, lo):
                return T[name][:].rearrange("p t P -> p P t")[:, :, lo:lo + 8].unsqueeze(3).to_broadcast([128, 16, 8, 16])

            def cplx_mul(outr, outi, rn_or, rn_oi, Xr, Xi, rn_x, tr, ti, lo, neg_imag=False):
                xr = Xr[:].unsqueeze(2).to_broadcast([128, 16, 8, 16]); xi = Xi[:].unsqueeze(2).to_broadcast([128, 16, 8, 16])
                ar_ = tauv(tr, lo); ai_ = tauv(ti, lo)
                dv(lambda e: e.tensor_tensor(out=big["t1"][:], in0=xr, in1=ar_, op=ALU.mult), rn_x + [tr, rn_or, rn_oi], ["t1"])
                dv(lambda e: e.tensor_tensor(out=big["t2"][:], in0=xi, in1=ai_, op=ALU.mult), rn_x + [ti, rn_or, rn_oi], ["t2"], "pool")
                dv(lambda e: e.tensor_tensor(out=outr[:], in0=big["t1"][:], in1=big["t2"][:], op=ALU.subtract), ["t1", "t2"], [rn_or])
                dv(lambda e: e.tensor_tensor(out=big["t1"][:], in0=xr, in1=ai_, op=ALU.mult), rn_x + [ti, rn_or], ["t1"])
                dv(lambda e: e.tensor_tensor(out=big["t2"][:], in0=xi, in1=ar_, op=ALU.mult), rn_x + [tr, rn_or], ["t2"], "pool")
                if neg_imag:
                    dv(lambda e: e.scalar_tensor_tensor(out=outi[:], in0=big["t1"][:], scalar=-1.0, in1=big["t2"][:], op0=ALU.mult, op1=ALU.subtract),
                       ["t1", "t2"], [rn_oi])
                else:
                    dv(lambda e: e.tensor_tensor(out=outi[:], in0=big["t1"][:], in1=big["t2"][:], op=ALU.add), ["t1", "t2"], [rn_oi])

            cplx_mul(big["W2r"], big["W2i"], "W2r", "W2i", CT[0], CT[1], [("CT", 0), ("CT", 1)], "APR", "API", 1, neg_imag=True)
            cplx_mul(big["M1r"], big["M1i"], "Lr", "Li", QB[0], QB[1], ["QB0", "QB1"], "APR", "API", 9)
            dv(lambda e: e.memset(W1re[:], 0.0), [], ["W1re"])
            dv(lambda e: e.memset(W1im[:], 0.0), [], ["W1im"], "pool")
            for c_, (M1, W1, rn) in enumerate(((big["M1r"], W1re, "W1re"), (big["M1i"], W1im, "W1im"))):
                for quad in range(4):
                    bank = 6 + quad % 2

                    def mmW(e, quad=quad, bank=bank, M1=M1):
                        last = None
                        for pl in range(4):
                            Pp = quad * 4 + pl
                            last = e.matmul(PS[bank][:, pl * 128:(pl + 1) * 128], lhsT=M1[:, Pp, :, :].rearrange("p s i -> p (s i)"), rhs=identf,
                                            start=True, stop=True)
                        return last
                    P.op("pe", mmW, reads=[r("Lr"), r("Li"), r("cst")], writes=[psr(bank)])
                    w1v = W1[:].rearrange("p (P e) c -> p P e c", e=2)
                    psv = PS[bank][:, :].rearrange("p (P c) -> p P c", P=4)
                    dv(lambda e, w1v=w1v, psv=psv, quad=quad: e.tensor_copy(out=w1v[:, quad * 4:quad * 4 + 4, 0, 0:64], in_=psv[:, :, 0:64]),
                       [("ps", bank)], [rn])
                    dv(lambda e, w1v=w1v, psv=psv, quad=quad: e.tensor_copy(out=w1v[:, quad * 4:quad * 4 + 4, 1, 64:128], in_=psv[:, :, 64:128]),
                       [("ps", bank)], [rn])
            cplx_mul(big["Lr"], big["Li"], "Lr", "Li", QB[0], QB[1], ["QB0", "QB1"], "AMR", "AMI", 1)
            dv(lambda e: e.activation(out=W2rb[:], in_=big["W2r"][:].rearrange("p P s o -> p P (s o)"), func=AF.Copy), ["W2r"], ["W2rb"], "act")
            _srcname = os.environ.get("KSRC", "W2i")
            _dst = {"W2ib": W2ib, "Tm": Tm[:, 0:16, :], "W1im": W1im[:, 0:16, :]}[os.environ.get("KDST", "W2ib")]
            dv(lambda e: e.activation(out=_dst[:] if os.environ.get("KDST", "W2ib") == "W2ib" else _dst, in_=big[_srcname][:].rearrange("p P s o -> p P (s o)"), func=AF.Copy), [_srcname], ["W2ib"], "act")
            def t_copies(quad):
                lmb = Lm[quad % 2]
                for gl in range(4):
                    g_ = quad * 4 + gl; Pp = g_ // 2; ee = g_ % 2
                    dv(lambda e, lmb=lmb, gl=gl, Pp=Pp, ee=ee: e.tensor_scalar(out=lmb[:, gl, 0, :], in0=big["Lr"][:, Pp, :, :].rearrange("p s i -> p (s i)"),
                                                                     scalar1=cst[:, C_PM + ee:C_PM + ee + 1], scalar2=None, op0=ALU.mult),
                       ["Lr", "cst"], [("Lm", quad % 2)])
                    dv(lambda e, lmb=lmb, gl=gl, Pp=Pp, ee=ee: e.tensor_scalar(out=lmb[:, gl, 1, :], in0=big["Li"][:, Pp, :, :].rearrange("p s i -> p (s i)"),
                                                                     scalar1=cst[:, C_PM + ee:C_PM + ee + 1], scalar2=None, op0=ALU.mult),
                       ["Li", "cst"], [("Lm", quad % 2)])

            def t_mm(quad):
                bank = 2 + quad % 4
                lmb = Lm[quad % 2]

                def mmT(e, quad=quad, bank=bank, lmb=lmb):
                    last = None
                    for gl in range(4):
                        g_ = quad * 4 + gl; Pp = g_ // 2
                        e.matmul(PS[bank][:, gl * 128:(gl + 1) * 128], lhsT=lmb[:, gl, 0, :],
                                 rhs=big["W2r"][:, Pp, :, :].rearrange("p s o -> p (s o)"), start=True, stop=False)
                        last = e.matmul(PS[bank][:, gl * 128:(gl + 1) * 128], lhsT=lmb[:, gl, 1, :],
                                        rhs=big["W2i"][:, Pp, :, :].rearrange("p s o -> p (s o)"), start=False, stop=True)
                    return last
                P.op("pe", mmT, reads=[r("Lm", quad % 2), r("W2r"), r("W2i")], writes=[psr(bank)])

            def t_evac(quad):
                bank = 2 + quad % 4
                tT = tmpT[quad % 2]
                dv(lambda e, bank=bank, tT=tT: e.tensor_tensor(out=tT[:], in0=PS[bank][:, :].rearrange("p (g c) -> p g c", g=4),
                                                               in1=cst[:, C_TM:C_TM + 128].unsqueeze(1).to_broadcast([128, 4, 128]), op=ALU.mult),
                   [("ps", bank), "cst"], [("tmpT", quad % 2)])
                for gl in range(4):
                    g_ = quad * 4 + gl
                    dv(lambda e, g_=g_, gl=gl, tT=tT: e.scalar_tensor_tensor(out=Tm[:, g_, :], in0=identf, scalar=dcol[:, g_:g_ + 1], in1=tT[:, gl, :],
                                                                             op0=ALU.mult, op1=ALU.add), [("tmpT", quad % 2), "dcol", "cst"], [("Tm", g_)])
            t_copies(0); t_mm(0)
            for quad in range(8):
                if quad + 1 < 8:
                    t_copies(quad + 1); t_mm(quad + 1)
                t_evac(quad)
            if "setup" in debug:
                _dl = [("Tm", Tm, 4096), ("W1re", W1re, 4096), ("W1im", W1im, 4096), ("W2rb", W2rb, 2048), ("W2ib", W2ib, 2048)]
                if os.environ.get("KNODUMP"):
                    _dl = [x for x in _dl if x[0] not in os.environ["KNODUMP"].split(",")]
                for nm, t_, n_ in _dl:
                    dbg[nm] = nc.dram_tensor("dbg_" + nm, [128, n_], BF16, kind="ExternalOutput").ap()
                    P.dma("sp", [lambda e, nm=nm, t_=t_: e.dma_start(out=dbg[nm][:, :], in_=t_[:].rearrange("p a b -> p (a b)"))],
                          reads=[P.res[k] for k in list(P.res) if k[0] == nm], out=True)
                dbg["s5c"] = nc.dram_tensor("dbg_s5c", [128, 64], F32, kind="ExternalOutput").ap()
                P.dma("sp", [lambda e: e.dma_start(out=dbg["s5c"][:, :], in_=s5c[:].rearrange("p a b -> p (a b)"))], reads=[r("s5c")], out=True)
            P.emit()
        if debug.endswith("setup"):
            mid.close()
            return nc

        with ExitStack() as esd:
            U = sb(esd, "U", (128, 32, 256), BF16); Us = sb(esd, "Us", (128, 32, 16), BF16)
            Hp = [sb(esd, "Hp%d" % i, (128, 16, 256), BF16) for i in range(2)]
            Hs = [sb(esd, "Hs%d" % i, (128, 16, 16), BF16) for i in range(2)]
            Hf = sb(esd, "Hf", (128, 2, 16))
            P.dma("sp", [(lambda c_, Pp: (lambda e: _ncd(e, out=H0T[c_][:, Pp, :],
                                                         in_=D["h0r" if c_ == 0 else "h0i"][:, Pp * 128:(Pp + 1) * 128].rearrange("q p -> p q"))))(c_, Pp)
                         for c_ in (1,) for Pp in range(16)], writes=[r("H0T", 1)])
            for c_ in range(2):
                dv(lambda e, c_=c_: e.memset(Hp[c_][:, :, 0:1], 0.0), [], [("Hp0", c_)], "pool")
            for bt in range(2):
                for oc in range(4):
                    bank = (bt * 4 + oc) % 2
                    P.op("pe", lambda e, bt=bt, oc=oc, bank=bank: pe_transposes(
                        e, [PSB[bank][:, gl * 128:(gl + 1) * 128] for gl in range(8)],
                        [P8[:, bt, (oc * 8 + gl) * 128:(oc * 8 + gl + 1) * 128] for gl in range(8)]),
                        reads=[r("P8", bt, s_) for s_ in range(8)] + [r("identb")], writes=[psr(bank)])
                    if oc % 2 == 0:
                        P.op("act", lambda e, bt=bt, oc=oc, bank=bank: e.activation(out=U[:, oc * 8:(oc + 1) * 8, bt * 128:(bt + 1) * 128],
                                                                                     in_=PSB[bank][:, 0:1024].rearrange("p (g c) -> p g c", g=8), func=AF.Copy),
                             reads=[psr(bank)], writes=[r("U", bt, oc)])
                    else:
                        P.op("dve", lambda e, bt=bt, oc=oc, bank=bank: e.tensor_copy(out=U[:, oc * 8:(oc + 1) * 8, bt * 128:(bt + 1) * 128],
                                                                                      in_=PSB[bank][:, 0:1024].rearrange("p (g c) -> p g c", g=8)),
                             reads=[psr(bank)], writes=[r("U", bt, oc)])

            def tr_s(e):
                last = None
                for g_ in range(32):
                    last = e.transpose(PSB[0][:, g_ * 16:(g_ + 1) * 16], P8[0:16, 2, g_ * 128:(g_ + 1) * 128], identb[0:16, 0:16])
                return last
            P.op("pe", tr_s, reads=[r("P8", 2, s_) for s_ in range(8)] + [r("identb")], writes=[psr(0)])
            dv(lambda e: e.tensor_copy(out=Us[:].rearrange("p g q -> p (g q)"), in_=PSB[0][:, 0:512]), [("ps", 0)], ["Us"])

            for c_, W1 in enumerate((W1re, W1im)):
                def mmxs(e, c_=c_, W1=W1):
                    last = None
                    for Pp in range(16):
                        e.matmul(PS[6 + c_][:, Pp * 16:(Pp + 1) * 16], lhsT=W1[:, 2 * Pp, :], rhs=Us[:, 2 * Pp, :], start=True, stop=False)
                        last = e.matmul(PS[6 + c_][:, Pp * 16:(Pp + 1) * 16], lhsT=W1[:, 2 * Pp + 1, :], rhs=Us[:, 2 * Pp + 1, :], start=False, stop=True)
                    return last
                P.op("pe", mmxs, reads=[r("Us"), r("W1re"), r("W1im")], writes=[psr(6 + c_)])
            T1 = scrF[:, 0:2048].rearrange("p (a b) -> p a b", a=8); T2 = scrF[:, 2048:4096].rearrange("p (a b) -> p a b", a=8)
            PH = scrF[:, 4096:6144].rearrange("p (a b) -> p a b", a=8); KK = scrF[:, 6144:8192].rearrange("p (a b) -> p a b", a=8)
            with ExitStack() as esh:
                Xr = sb(esh, "Xr", (128, 8, 256)); Xi = sb(esh, "Xi", (128, 8, 256))
                CS = sb(esh, "CS", (128, 8, 256)); SN = sb(esh, "SN", (128, 8, 256))
                X = (Xr, Xi)
                P8f2 = P8[:].rearrange("p a c -> p (a c)")
                CS1 = P8f2[:, 0:4096].bitcast(F32).rearrange("p (a b) -> p a b", a=8)
                SN1 = P8f2[:, 4096:8192].bitcast(F32).rearrange("p (a b) -> p a b", a=8)
                tabs_cs = (CS[:], CS1); tabs_sn = (SN[:], SN1)
                dv(lambda e: e.memset(KK[:, 0, 0:1], 0.0), [], ["KK"] + [("P8", bt, s_) for bt in range(3) for s_ in range(8)])
                for hf in range(2):
                    P0 = 8 * hf
                    CSh = tabs_cs[hf]; SNh = tabs_sn[hf]; rcs = ("CS", hf); rsn = ("SN", hf)
                    dv(lambda e, P0=P0: e.tensor_tensor(out=PH, in0=cst[:, C_IO:C_IO + 256].unsqueeze(1).to_broadcast([128, 8, 256]),
                                                        in1=s5c[:, 1, P0:P0 + 8].unsqueeze(2).to_broadcast([128, 8, 256]), op=ALU.mult),
                       ["cst", "s5c"], ["PH"], "pool")
                    range_reduce(SNh, PH, 0.0, rsn, "PH", T1, KK, "T1", "KK")
                    dv(lambda e, CSh=CSh, SNh=SNh: e.activation(out=CSh, in_=SNh, func=AF.Abs), [rsn], [rcs], "act")
                    dv(lambda e, CSh=CSh: e.activation(out=CSh, in_=CSh, func=AF.Sin, scale=-1.0, bias=cst[:, C_HPI:C_HPI + 1]), [rcs, "cst"], [rcs], "act")
                    dv(lambda e, SNh=SNh: e.activation(out=SNh, in_=SNh, func=AF.Sin), [rsn, rcs], [rsn], "act")
                for hf in range(2):
                    P0 = 8 * hf
                    CSh = tabs_cs[hf]; SNh = tabs_sn[hf]; rcs = ("CS", hf); rsn = ("SN", hf)
                    for bt in range(2):
                        for quad in range(2):
                            for c_, W1 in enumerate((W1re, W1im)):
                                bank = 2 + ((bt * 2 + quad) * 2 + c_) % 4

                                def mmx(e, bt=bt, quad=quad, W1=W1, bank=bank, P0=P0):
                                    last = None
                                    for pl in range(4):
                                        Pp = P0 + 4 * quad + pl
                                        e.matmul(PS[bank][:, pl * 128:(pl + 1) * 128], lhsT=W1[:, 2 * Pp, :], rhs=U[:, 2 * Pp, bt * 128:(bt + 1) * 128],
                                                 start=True, stop=False)
                                        last = e.matmul(PS[bank][:, pl * 128:(pl + 1) * 128], lhsT=W1[:, 2 * Pp + 1, :],
                                                        rhs=U[:, 2 * Pp + 1, bt * 128:(bt + 1) * 128], start=False, stop=True)
                                    return last
                                P.op("pe", mmx, reads=[r("U", bt, oc) for oc in range(4)] + [r("W1re"), r("W1im")], writes=[psr(bank)])
                                Xc = X[c_]
                                if c_ == 0:
                                    P.op("act", lambda e, Xc=Xc, quad=quad, bt=bt, bank=bank: e.activation(
                                        out=Xc[:, 4 * quad:4 * quad + 4, bt * 128:(bt + 1) * 128], in_=PS[bank][:, :].rearrange("p (a b) -> p a b", a=4), func=AF.Copy),
                                        reads=[psr(bank)], writes=[r("X", c_)])
                                else:
                                    P.op("dve", lambda e, Xc=Xc, quad=quad, bt=bt, bank=bank: e.tensor_copy(
                                        out=Xc[:, 4 * quad:4 * quad + 4, bt * 128:(bt + 1) * 128], in_=PS[bank][:, :].rearrange("p (a b) -> p a b", a=4)),
                                        reads=[psr(bank)], writes=[r("X", c_)])
                    dv(lambda e, CSh=CSh, SNh=SNh: e.tensor_tensor(out=T1, in0=CSh, in1=Xr[:], op=ALU.mult), [rcs, ("X", 0)], ["T1"])
                    dv(lambda e, CSh=CSh, SNh=SNh: e.tensor_tensor(out=T2, in0=SNh, in1=Xi[:], op=ALU.mult), [rsn, ("X", 1)], ["T2"], "pool")
                    dv(lambda e: e.tensor_tensor(out=T1, in0=T1, in1=T2, op=ALU.add), ["T1", "T2"], ["T1"])
                    dv(lambda e, CSh=CSh, SNh=SNh: e.tensor_tensor(out=T2, in0=CSh, in1=Xi[:], op=ALU.mult), [rcs, ("X", 1)], ["T2"], "pool")
                    dv(lambda e, CSh=CSh, SNh=SNh: e.tensor_tensor(out=Xi[:], in0=SNh, in1=Xr[:], op=ALU.mult), [rsn, ("X", 0)], [("X", 1)])
                    dv(lambda e: e.tensor_tensor(out=T2, in0=T2, in1=Xi[:], op=ALU.subtract), ["T2", ("X", 1)], ["T2"])
                    for pl in range(8):
                        Pp = P0 + pl
                        dv(lambda e, pl=pl, Pp=Pp: e.tensor_tensor_scan(out=Xr[:, pl, :], data0=s5c[:, 0, Pp:Pp + 1].to_broadcast([128, 256]),
                                                                        data1=T1[:, pl, :], initial=0.0, op0=ALU.mult, op1=ALU.add),
                           ["T1", "s5c"], [("X", 0)])
                        dv(lambda e, pl=pl, Pp=Pp: e.tensor_tensor_scan(out=Xi[:, pl, :], data0=s5c[:, 0, Pp:Pp + 1].to_broadcast([128, 256]),
                                                                        data1=T2[:, pl, :], initial=0.0, op0=ALU.mult, op1=ALU.add),
                           ["T2", "s5c"], [("X", 1)])
                    dv(lambda e, CSh=CSh, SNh=SNh: e.tensor_tensor(out=T1, in0=CSh, in1=Xr[:], op=ALU.mult), [rcs, ("X", 0)], ["T1"])
                    dv(lambda e, CSh=CSh, SNh=SNh: e.tensor_tensor(out=T2, in0=SNh, in1=Xi[:], op=ALU.mult), [rsn, ("X", 1)], ["T2"], "pool")
                    dv(lambda e, P0=P0: e.tensor_tensor(out=Hp[0][:, P0:P0 + 8, 1:256], in0=T1[:, :, 0:255], in1=T2[:, :, 0:255], op=ALU.subtract),
                       ["T1", "T2"], [("Hp", 0, hf)])
                    dv(lambda e, P0=P0: e.tensor_tensor(out=Hf[:, 0, P0:P0 + 8], in0=T1[:, :, 255], in1=T2[:, :, 255], op=ALU.subtract),
                       ["T1", "T2"], [("Hf", 0, hf)])
                    dv(lambda e, CSh=CSh, SNh=SNh: e.tensor_tensor(out=T1, in0=CSh, in1=Xi[:], op=ALU.mult), [rcs, ("X", 1), ("Hp", 0, hf), ("Hf", 0, hf)], ["T1"])
                    dv(lambda e, CSh=CSh, SNh=SNh: e.tensor_tensor(out=T2, in0=SNh, in1=Xr[:], op=ALU.mult), [rsn, ("X", 0), ("Hp", 0, hf), ("Hf", 0, hf)], ["T2"], "pool")
                    dv(lambda e, P0=P0: e.tensor_tensor(out=Hp[1][:, P0:P0 + 8, 1:256], in0=T1[:, :, 0:255], in1=T2[:, :, 0:255], op=ALU.add),
                       ["T1", "T2"], [("Hp", 1, hf)])
                    dv(lambda e, P0=P0: e.tensor_tensor(out=Hf[:, 1, P0:P0 + 8], in0=T1[:, :, 255], in1=T2[:, :, 255], op=ALU.add),
                       ["T1", "T2"], [("Hf", 1, hf)])
                sft0 = scrF[:, 6144:6400].rearrange("p (a b) -> p a b", a=16); sft1 = scrF[:, 6400:6656].rearrange("p (a b) -> p a b", a=16)
                Sf0 = scrF[:, 4096:4352].rearrange("p (a b) -> p a b", a=16); Sf1 = scrF[:, 4352:4608].rearrange("p (a b) -> p a b", a=16)
                Sf = (Sf0, Sf1)
                a8rb = s5c[:, 2, :].unsqueeze(2).to_broadcast([128, 16, 16]); a8ib = s5c[:, 3, :].unsqueeze(2).to_broadcast([128, 16, 16])
                dv(lambda e: e.tensor_tensor(out=sft0, in0=H0T[0][:], in1=a8rb, op=ALU.mult), [("H0T", 0), ("H0T", 1), "s5c"], ["KK"])
                dv(lambda e: e.tensor_tensor(out=sft1, in0=H0T[1][:], in1=a8ib, op=ALU.mult), [("H0T", 0), ("H0T", 1), "s5c"], ["KK"])
                dv(lambda e: e.tensor_tensor(out=sft0, in0=sft0, in1=sft1, op=ALU.subtract), ["KK"], ["KK"])
                dv(lambda e: e.tensor_tensor(out=Sf0, in0=sft0, in1=PS[6][:, 0:256].rearrange("p (a b) -> p a b", a=16), op=ALU.add),
                   ["KK", ("ps", 6)], ["PH"])
                dv(lambda e: e.tensor_tensor(out=sft0, in0=H0T[1][:], in1=a8rb, op=ALU.mult), [("H0T", 0), ("H0T", 1), "s5c", "PH"], ["KK"])
                dv(lambda e: e.tensor_tensor(out=sft1, in0=H0T[0][:], in1=a8ib, op=ALU.mult), [("H0T", 0), ("H0T", 1), "s5c", "PH"], ["KK"])
                dv(lambda e: e.tensor_tensor(out=sft0, in0=sft0, in1=sft1, op=ALU.add), ["KK"], ["KK"])
                dv(lambda e: e.tensor_tensor(out=Sf1, in0=sft0, in1=PS[7][:, 0:256].rearrange("p (a b) -> p a b", a=16), op=ALU.add),
                   ["KK", ("ps", 7)], ["PH"])

                P.dma("sp", [lambda e: ncdma(e, out=D["pr"].rearrange("(P e) n -> (e n) P", e=2), in_=Hf[:, 0, :]),
                             lambda e: ncdma(e, out=D["pi"].rearrange("(P e) n -> (e n) P", e=2), in_=Hf[:, 1, :])],
                      reads=[r("Hf", c_, hf) for c_ in range(2) for hf in range(2)], out=True)
                Sout = scrF[0:16, 0:2048]
                for c_ in range(2):
                    def mmso(e, c_=c_):
                        last = None
                        for Pp in range(16):
                            last = e.matmul(PS[Pp // 4][0:16, (Pp % 4) * 128:(Pp % 4 + 1) * 128], lhsT=Sf[c_][:, Pp, :], rhs=identf, start=True, stop=True)
                        return last
                    P.op("pe", mmso, reads=[r("PH"), r("cst")], writes=[psr(0), psr(1), psr(2), psr(3)])
                    for bk in range(4):
                        dv(lambda e, bk=bk: e.tensor_copy(out=Sout[:, bk * 512:(bk + 1) * 512], in_=PS[bk][0:16, :]), [("ps", bk)], ["Sout", "T1"])
                    dn = "sr" if c_ == 0 else "si"
                    P.dma("sp", [lambda e, dn=dn: e.dma_start(out=D[dn][:, :], in_=Sout)], reads=[r("Sout")], out=True)
            if "da" in debug:
                for nm, t_ in (("U", U), ("Hp0", Hp[0]), ("Hp1", Hp[1])):
                    dbg[nm] = nc.dram_tensor("dbg_" + nm, list(t_.shape), BF16, kind="ExternalOutput").ap()
                    P.dma("sp", [lambda e, nm=nm, t_=t_: e.dma_start(out=dbg[nm][:, :, :], in_=t_[:])],
                          reads=[P.res[k] for k in list(P.res) if k[0] in ("U", "Hp", "Hp0")], out=True)

            P.emit()
            if debug.endswith("da"):
                esd.close(); mid.close()
                return nc
            with ExitStack() as esb:
                P8f = P8[:].rearrange("p a c -> p (a c)")
                G8_bufs = (scrF[:, 0:4096], P8f[:, 0:8192].bitcast(F32))
                gT = scrB[:, 8192:12288].rearrange("p (j c) -> p j c", j=4)
                G8b0 = sb(esb, "G8b", (128, 4096), BF16); O8b = sb(esb, "O8b", (128, 4096), BF16)
                G8b_bufs = (G8b0[:], P8f[:, 8192:12288])
                tmpz = [sb(esb, "tmpz%d" % i, (128, 512)) for i in range(2)]
                junk2 = sb(esb, "junk2", (128, 512), BF16)
                for bt in range(3):
                    if bt == 2:
                        for c_ in range(2):
                            dv(lambda e, c_=c_: e.tensor_copy(out=Hs[c_][:], in_=H0T[c_][:]), [("H0T", c_)], [("Hs", c_)], "pool")
                    bf_ = bt % 2
                    G8 = G8_bufs[bf_]; G8b = G8b_bufs[bf_]
                    G8v = G8.rearrange("p (s g o) -> p g s o", s=8, g=32)
                    G8bv = G8b.rearrange("p (s g o) -> p g s o", s=8, g=32)
                    npt = 128 if bt < 2 else 16
                    for gq in range(8):
                        bank = 4 + gq % 2

                        def mmy(e, bt=bt, gq=gq, bank=bank, npt=npt):
                            last = None
                            for gl in range(4):
                                g_ = 4 * gq + gl; Pp = g_ // 2; ee = g_ % 2; lo, hi = ee * 64, ee * 64 + 64
                                o_ = PS[bank][0:npt, gl * 128:(gl + 1) * 128]
                                if bt < 2:
                                    u_ = U[:, g_, bt * 128:(bt + 1) * 128]; hr_ = Hp[0][lo:hi, Pp, bt * 128:(bt + 1) * 128]; hi_ = Hp[1][lo:hi, Pp, bt * 128:(bt + 1) * 128]
                                else:
                                    u_ = Us[:, g_, :]; hr_ = Hs[0][lo:hi, Pp, :]; hi_ = Hs[1][lo:hi, Pp, :]
                                e.matmul(o_, lhsT=u_, rhs=Tm[:, g_, :], start=True, stop=False)
                                e.matmul(o_, lhsT=hr_, rhs=W2rb[lo:hi, Pp, :], start=False, stop=False)
                                last = e.matmul(o_, lhsT=hi_, rhs=W2ib[lo:hi, Pp, :], start=False, stop=True)
                            return last
                        rd = ([r("U", bt, oc) for oc in range(4)] + [r("Hp", c_, hf) for c_ in range(2) for hf in range(2)] + [r("Hp0", 0), r("Hp0", 1)]) if bt < 2 \
                            else [r("Us"), r("Hs", 0), r("Hs", 1)]
                        P.op("pe", mmy, reads=rd + [r("Tm", g_) for g_ in range(4 * gq, 4 * gq + 4)] + [r("W2rb"), r("W2ib")], writes=[psr(bank)])
                        P.op("act", lambda e, gq=gq, bank=bank, npt=npt, G8v=G8v: e.activation(
                            out=G8v[0:npt, 4 * gq:4 * gq + 4, :, :], in_=PS[bank][0:npt, :].rearrange("p (g s o) -> p g s o", g=4, s=8), func=AF.Gelu_apprx_tanh),
                            reads=[psr(bank)], writes=[r("G8", bf_, gq)])
                        P.op("dve", lambda e, gq=gq, npt=npt, G8bv=G8bv, G8v=G8v: e.tensor_copy(
                            out=G8bv[0:npt, 4 * gq:4 * gq + 4, :, :], in_=G8v[0:npt, 4 * gq:4 * gq + 4, :, :]),
                            reads=[r("G8", bf_, gq)], writes=[r("G8b", bf_, gq)])
                    def phaseA(s_, bt=bt, npt=npt, bf_=bf_, G8=G8, G8b=G8b):
                        bk = s_ % 2; idx = bt * 8 + s_
                        P.op("pe", lambda e, s_=s_, bk=bk, npt=npt: pe_transposes(
                            e, [PSB[bk][:, j * npt:(j + 1) * npt] for j in range(4)],
                            [G8b[0:npt, s_ * 512 + j * 128:s_ * 512 + (j + 1) * 128] for j in range(4)], npart=npt),
                            reads=[r("G8b", bf_, gq) for gq in range(8)] + [r("identb")], writes=[psr(bk)])
                        dv(lambda e, s_=s_, bk=bk, npt=npt: e.tensor_copy(out=gT[:, :, s_ * npt:(s_ + 1) * npt],
                                                                          in_=PSB[bk][:, 0:4 * npt].rearrange("p (j c) -> p j c", j=4)),
                           [("ps", bk)], [("gT", s_)])
                        zb = 6 + s_ % 2

                        def mmz(e, s_=s_, zb=zb, npt=npt):
                            for j in range(4):
                                e.matmul(PS[zb][0:npt, :], lhsT=gT[:, j, s_ * npt:(s_ + 1) * npt], rhs=wglu[:, j, :], start=(j == 0), stop=False)
                            return e.matmul(PS[zb][0:npt, :], lhsT=onesb[0:1, 0:npt], rhs=bglub[0:1, :], start=False, stop=True)
                        P.op("pe", mmz, reads=[r("gT", s_), r("wglu"), r("onesb")], writes=[psr(zb)])

                    def phaseB(s_, bt=bt, npt=npt, bf_=bf_, G8=G8, G8b=G8b):
                        bk = s_ % 2; idx = bt * 8 + s_; zb = 6 + s_ % 2
                        tz = tmpz[s_ % 2]
                        P.op("act", lambda e, tz=tz, zb=zb, npt=npt: e.activation(out=tz[0:npt, :], in_=PS[zb][0:npt, :], func=AF.Tanh, scale=0.5),
                             reads=[psr(zb)], writes=[r("tmpz", s_ % 2)])
                        g8s = G8[0:npt, s_ * 512:(s_ + 1) * 512]
                        dv(lambda e, tz=tz, g8s=g8s, npt=npt: e.scalar_tensor_tensor(out=g8s, in0=tz[0:npt, :], scalar=1.0, in1=g8s, op0=ALU.add, op1=ALU.mult),
                           [("tmpz", s_ % 2)] + [("G8", bf_, gq) for gq in range(8)], [("G8s", bf_, s_)])
                        P.op("act", lambda e, g8s=g8s, idx=idx, npt=npt: e.activation(out=junk2[0:npt, :], in_=g8s, func=AF.Square, accum_out=ssq5[0:npt, idx:idx + 1]),
                             reads=[r("G8s", bf_, s_)], writes=[r("junk2"), r("ssq5")])
                        dv(lambda e, g8s=g8s, s_=s_, npt=npt: e.tensor_tensor(out=O8b[0:npt, s_ * 512:(s_ + 1) * 512], in0=g8s, in1=gs5bc[0:npt, :], op=ALU.mult),
                           [("G8s", bf_, s_), "gs5bc"], [("O8b", s_)], "pool")
                        bk2 = 2 + s_ % 2
                        P.op("pe", lambda e, s_=s_, bk2=bk2, npt=npt: pe_transposes(
                            e, [PSB[bk2][:, j * npt:(j + 1) * npt] for j in range(4)],
                            [O8b[0:npt, s_ * 512 + j * 128:s_ * 512 + (j + 1) * 128] for j in range(4)], npart=npt),
                            reads=[r("O8b", s_), r("identb")], writes=[psr(bk2)])

                    def phaseC(s_, bt=bt, npt=npt):
                        bk2 = 2 + s_ % 2
                        if bt < 2:
                            mo = mixT[:, 0:4, bt * 1024 + s_:bt * 1024 + 1024:8]
                        else:
                            mo = mixT[:, 0:4, 2048 + s_:2176:8]
                        P.op("act", lambda e, mo=mo, bk2=bk2, npt=npt: e.activation(out=mo, in_=PSB[bk2][:, 0:4 * npt].rearrange("p (j c) -> p j c", j=4), func=AF.Copy),
                             reads=[psr(bk2)], writes=[r("mixs5", bt, s_)])

                    phaseA(0)
                    for s_ in range(8):
                        if s_ + 1 < 8:
                            phaseA(s_ + 1)
                        phaseB(s_)
                        if s_ >= 1:
                            phaseC(s_ - 1)
                    phaseC(7)
                    for gq in range(8):
                        dst = r("G8", bf_, gq)
                        for s_ in range(8):
                            src = r("G8s", bf_, s_)
                            for tok in list(src.rs.values()) + ([src.w] if src.w is not None else []):
                                k = id(tok[0])
                                if k not in dst.rs or dst.rs[k][1] < tok[1]:
                                    dst.rs[k] = tok
                if "db" in debug:
                    dbg["mixs5"] = nc.dram_tensor("dbg_mixs5", [128, 4, TOK], BF16, kind="ExternalOutput").ap()
                    P.dma("sp", [lambda e: e.dma_start(out=dbg["mixs5"][:, :, :], in_=mixT[:, 0:4, :])],
                          reads=[r("mixs5", bt, s_) for bt in range(3) for s_ in range(8)], out=True)
                    dbg["ssq5"] = nc.dram_tensor("dbg_ssq5", [128, 24], F32, kind="ExternalOutput").ap()
                    P.dma("sp", [lambda e: e.dma_start(out=dbg["ssq5"][:, :], in_=ssq5[:])], reads=[r("ssq5")], out=True)
                P.emit()
        mid.close()
        if debug.endswith("d"):
            return nc

        X1 = sb(top, "X1", (128, NT, 1024))
        ssq2 = sb(top, "ssq2", (128, NT)); rstd2 = sb(top, "rstd2", (128, NT)); srtE = sb(top, "srtE", (128, NT))
        ssqf = sb(top, "ssqf", (128, NT)); rstdf = sb(top, "rstdf", (128, NT))
        wd = sb(top, "wd", (128, 8, 1024), BF16)
        wg = [sb(top, "wg%d" % i, (128, 8, 128), BF16) for i in range(3)]
        wu = [sb(top, "wu%d" % i, (128, 8, 128), BF16) for i in range(3)]
        def load_f(f):
            sl = f % 3
            P.dma("pool", [lambda e, f=f, sl=sl: e.dma_start(out=wg[sl][:], in_=D["w_gate"][:, f * 128:(f + 1) * 128].rearrange("(kt p) n -> p kt n", p=128)),
                           lambda e, f=f, sl=sl: e.dma_start(out=wu[sl][:], in_=D["w_up"][:, f * 128:(f + 1) * 128].rearrange("(kt p) n -> p kt n", p=128))],
                  writes=[r("wgu", sl)])

        def load_wd(f, fl):
            P.dma("pool", [lambda e, f=f, fl=fl: e.dma_start(out=wd[:, fl, :], in_=D["w_down"][f * 128:(f + 1) * 128, :])], writes=[r("wd", fl)])

        with ExitStack() as es:
            wout = sb(es, "wout", (128, 8, 1024), BF16)
            g2bc = sb(es, "g2bc", (128, 1024))
            xt = [sb(es, "ext%d" % i, (128, 1024)) for i in range(2)]
            xb = [sb(es, "exb%d" % i, (128, 1024), BF16) for i in range(2)]
            wov = D["w_out"].rearrange("(kt p) n -> p kt n", p=128)
            P.dma("pool", [lambda e: e.dma_start(out=wout[:, 4:8, :], in_=wov[:, 4:8, :])], writes=[r("wout", 1)])
            P.dma("pool", [lambda e: e.dma_start(out=wout[:, 0:4, :], in_=wov[:, 0:4, :])], writes=[r("wout", 0)])
            P.dma("sp", [lambda e: e.dma_start(out=g2bc[:], in_=D["norm2"][0:1, :].partition_broadcast(128))], writes=[r("g2bc")])
            P.dma("sp", [lambda e: ncdma(e, out=scr2[0:2048].rearrange("(a b s) -> b a s", a=2, s=8), in_=ssq5[:, 0:16].rearrange("p (a s) -> p a s", a=2)),
                         lambda e: ncdma(e, out=scr2[2048:2176].rearrange("(q s) -> q s", s=8), in_=ssq5[0:16, 16:24])],
                  reads=[r("ssq5")], writes=[r("scr2")])
            P.dma("sp", [lambda e: ncdma(e, out=ssq5n[:], in_=scr2.rearrange("(t p) -> p t", p=128))], reads=[r("scr2")], writes=[r("ssq5n")])
            dv(lambda e: e.activation(out=srtE[:], in_=ssq5n[:], func=AF.Sqrt, bias=eps4c, scale=1.0 / 512), ["ssq5n", "cst"], ["srtE"], "act")
            dv(lambda e: e.reciprocal(out=rstd5[:], in_=srtE[:]), ["srtE"], ["rstd5"])
            dv(lambda e: e.activation(out=srtE[:], in_=ssqcm[:], func=AF.Sqrt, bias=epsc, scale=1.0 / 512),
               [("ssqcm", t) for t in range(NT)] + ["cst", "rstd5"], ["srtE"], "act")
            dv(lambda e: e.reciprocal(out=rstdcm[:], in_=srtE[:]), ["srtE"], ["rstdcm"])
            for t in range(NT):
                xs_ = xt[t % 2]; rx = r("ext", t % 2); b0 = 4 * (t % 2)
                P.dma("sp", [lambda e, xs_=xs_, t=t: e.dma_start(out=xs_[:], in_=xsrc(t))], writes=[rx])
                for part in range(2):
                    for half in range(2):
                        bank = b0 + 2 * part + half
                        k0 = 4 if part == 0 else 0

                        def mmo(e, t=t, bank=bank, k0=k0, half=half):
                            last = None
                            for j in range(4):
                                last = e.matmul(PS[bank][:, :], lhsT=mixT[:, k0 + j, t * 128:(t + 1) * 128], rhs=wout[:, k0 + j, half * 512:(half + 1) * 512],
                                                start=(j == 0), stop=(j == 3))
                            return last
                        rd = [r("mixcm", t)] if part == 0 else [r("mixs5", bt, s_) for bt in range(3) for s_ in range(8)]
                        P.op("pe", mmo, reads=rd + [r("wout", 1 - part)], writes=[psr(bank)])
                for half in range(2):
                    hs = slice(half * 512, (half + 1) * 512)
                    dv(lambda e, t=t, hs=hs, xs_=xs_, bank=b0 + half: e.scalar_tensor_tensor(out=X1[:, t, hs], in0=PS[bank][:, :], scalar=rstdcm[:, t:t + 1], in1=xs_[:, hs],
                                                                                     op0=ALU.mult, op1=ALU.add),
                       [("ps", b0 + half), "rstdcm", ("ext", t % 2)], [("X1", t)])
                    dv(lambda e, t=t, hs=hs, bank=b0 + 2 + half: e.scalar_tensor_tensor(out=X1[:, t, hs], in0=PS[bank][:, :], scalar=rstd5[:, t:t + 1], in1=X1[:, t, hs],
                                                                                op0=ALU.mult, op1=ALU.add),
                       [("ps", b0 + 2 + half), "rstd5", ("X1", t)], [("X1", t)])
                dv(lambda e, t=t: e.activation(out=xb[t % 2][:], in_=X1[:, t, :], func=AF.Square, accum_out=ssq2[:, t:t + 1]), [("X1", t)], [("exb", t % 2), ("ssq2", t)], "act")
            for f in range(3):
                load_f(f)
            for fl in range(8):
                load_wd(fl, fl)
            dv(lambda e: e.activation(out=srtE[:], in_=ssq2[:], func=AF.Sqrt, bias=epsc, scale=1.0 / 1024),
               [("ssq2", t) for t in range(NT)] + ["cst", "rstdcm"], ["srtE"], "act")
            dv(lambda e: e.reciprocal(out=rstd2[:], in_=srtE[:]), ["srtE"], ["rstd2"])
            for t in range(NT):
                xbt = xb[t % 2]; pb = t % 2
                dv(lambda e, t=t, xbt=xbt: e.scalar_tensor_tensor(out=xbt[:], in0=X1[:, t, :], scalar=rstd2[:, t:t + 1], in1=g2bc[:], op0=ALU.mult, op1=ALU.mult),
                   [("X1", t), "rstd2", "g2bc"], [("exb", t % 2)])
                P.op("pe", lambda e, xbt=xbt, pb=pb: pe_transposes(e, [PSB[pb][:, k * 128:(k + 1) * 128] for k in range(8)],
                                                               [xbt[:, k * 128:(k + 1) * 128] for k in range(8)]),
                     reads=[r("exb", t % 2), r("identb")], writes=[psr(pb)])
                P.op("act", lambda e, t=t, pb=pb: e.activation(out=actA[:, :, t * 128:(t + 1) * 128], in_=PSB[pb][:, 0:1024].rearrange("p (k c) -> p k c", k=8), func=AF.Copy),
                     reads=[psr(pb)], writes=[r("x2T", t)])
            if "e" in debug.split("-"):
                dbg["X1"] = nc.dram_tensor("dbg_X1", [128, NT, 1024], F32, kind="ExternalOutput").ap()
                P.dma("sp", [lambda e: e.dma_start(out=dbg["X1"][:, :, :], in_=X1[:])], reads=[r("X1", t) for t in range(NT)], out=True)
                dbg["x2T"] = nc.dram_tensor("dbg_x2T", [128, 8, TOK], BF16, kind="ExternalOutput").ap()
                P.dma("sp", [lambda e: e.dma_start(out=dbg["x2T"][:, :, :], in_=actA[:])], reads=[r("x2T", t) for t in range(NT)], out=True)
            P.emit()
        if debug.endswith("-e"):
            return nc

        with ExitStack() as es:
            hF = mixT
            sg = [sb(es, "sg%d" % i, (128, 512)) for i in range(2)]
            gfbc = sb(es, "gfbc", (128, 1024))
            yo = [sb(es, "yo%d" % i, (128, 1024)) for i in range(2)]
            gjunk = sb(es, "gjunk", (128, 1024), BF16)
            P.dma("sp", [lambda e: e.dma_start(out=gfbc[:], in_=D["norm_f"][0:1, :].partition_broadcast(128))], writes=[r("gfbc")])
            fgroups = [(0, 8), (8, 16), (16, 22)]
            tgs = [(0, 512), (512, 1024), (1024, 1536), (1536, 2048), (2048, 2176)]

            cnt = 0
            for gi, (f0, f1) in enumerate(fgroups):
                nf = f1 - f0
                if gi > 0:
                    for fl in range(nf):
                        load_wd(f0 + fl, fl)
                for f in range(f0, f1):
                    sl = f % 3; fl = f - f0
                    for ti, (c0, c1) in enumerate(tgs):
                        n = c1 - c0; bg = cnt % 2; bu = 2 + cnt % 2; sgt = sg[cnt % 2]; rsg = r("sg", cnt % 2); cnt += 1
                        tiles = list(range(c0 // 128, (c1 + 127) // 128))

                        def mmgu(e, W, bank, sl=sl, c0=c0, c1=c1, n=n):
                            last = None
                            for k in range(8):
                                last = e.matmul(PS[bank][:, 0:n], lhsT=W[sl][:, k, :], rhs=actA[:, k, c0:c1], start=(k == 0), stop=(k == 7))
                            return last
                        P.op("pe", lambda e, bg=bg, mmgu=mmgu: mmgu(e, wg, bg), reads=[r("x2T", t) for t in tiles] + [r("wgu", sl)], writes=[psr(bg)])
                        P.op("pe", lambda e, bu=bu, mmgu=mmgu: mmgu(e, wu, bu), reads=[r("x2T", t) for t in tiles] + [r("wgu", sl)], writes=[psr(bu)])
                        P.op("act", lambda e, sgt=sgt, bg=bg, n=n: e.activation(out=sgt[:, 0:n], in_=PS[bg][:, 0:n], func=AF.Silu), reads=[psr(bg)], writes=[rsg])
                        P.op("dve", lambda e, sgt=sgt, bu=bu, n=n, fl=fl, c0=c0, c1=c1: e.tensor_tensor(out=hF[:, fl, c0:c1], in0=sgt[:, 0:n], in1=PS[bu][:, 0:n], op=ALU.mult),
                             reads=[rsg, psr(bu)], writes=[r("hF", fl, ti)])
                    if f + 3 < NFF:
                        load_f(f + 3)
                for t in range(NT):
                    ti = min(t // 4, 4)
                    for half in range(2):
                        bank = 4 + (t * 2 + half) % 4

                        def mmd(e, t=t, half=half, bank=bank, nf=nf):
                            last = None
                            for fl in range(nf):
                                last = e.matmul(PS[bank][:, :], lhsT=hF[:, fl, t * 128:(t + 1) * 128], rhs=wd[:, fl, half * 512:(half + 1) * 512],
                                                start=(fl == 0), stop=(fl == nf - 1))
                            return last
                        P.op("pe", mmd, reads=[r("hF", fl, ti) for fl in range(nf)] + [r("wd", fl) for fl in range(nf)], writes=[psr(bank)])
                        hs = slice(half * 512, (half + 1) * 512)
                        dv(lambda e, t=t, hs=hs, bank=bank: e.tensor_tensor(out=X1[:, t, hs], in0=X1[:, t, hs], in1=PS[bank][:, :], op=ALU.add),
                           [("ps", bank), ("X1", t)], [("X1", t)])
                    if gi == len(fgroups) - 1:
                        yot = yo[t % 2]
                        dv(lambda e, t=t: e.activation(out=gjunk[:], in_=X1[:, t, :], func=AF.Square, accum_out=ssqf[:, t:t + 1]), [("X1", t)], ["gjunk", ("ssqf", t)], "act")
                        dv(lambda e, t=t: e.activation(out=srtE[:, t:t + 1], in_=ssqf[:, t:t + 1], func=AF.Sqrt, bias=epsc, scale=1.0 / 1024),
                           [("ssqf", t), "cst"], [("srtf", t)], "act")
                        dv(lambda e, t=t: e.reciprocal(out=rstdf[:, t:t + 1], in_=srtE[:, t:t + 1]), [("srtf", t)], [("rstdf", t)])
                        dv(lambda e, t=t, yot=yot: e.scalar_tensor_tensor(out=yot[:], in0=X1[:, t, :], scalar=rstdf[:, t:t + 1], in1=gfbc[:], op0=ALU.mult, op1=ALU.mult),
                           [("X1", t), ("rstdf", t), "gfbc"], [("yo", t % 2)])
                        P.dma("sp", [lambda e, t=t, yot=yot: e.dma_start(out=ydst(t), in_=yot[:])], reads=[r("yo", t % 2)], out=True)
            P.emit()


    return nc


_CACHE = {}


def _prep_inputs(inputs, c):
    f = lambda a: np.ascontiguousarray(np.asarray(a, dtype=np.float32))
    m = {
        "xp": f(inputs["x_prompt"][c]),
        "xs": f(inputs["x_sample"][16 * c:16 * c + 16]).reshape(128, 1024),
        "h0r": f(inputs["state_s5_re"][0, 16 * c:16 * c + 16]).reshape(16, 2048),
        "h0i": f(inputs["state_s5_im"][0, 16 * c:16 * c + 16]).reshape(16, 2048),
        "norm1": f(inputs["norm1"]).reshape(1, 1024),
        "w_in": f(inputs["w_in"][0]),
        "lam_re": f(inputs["lam_re"][0]), "lam_im": f(inputs["lam_im"][0]),
        "log_dt": f(inputs["log_dt"]).reshape(1, 32),
        "b_re": f(inputs["b_re"][0]).reshape(2048, 16), "b_im": f(inputs["b_im"][0]).reshape(2048, 16),
        "c_re": f(inputs["c_re"][0]).reshape(512, 64), "c_im": f(inputs["c_im"][0]).reshape(512, 64),
        "d_skip": f(inputs["d_skip"]).reshape(1, 512), "w_glu": f(inputs["w_glu"][0]), "b_glu": f(inputs["b_glu"]).reshape(1, 512),
        "cm_ln_g": f(inputs["cm_ln_g"]).reshape(1, 512), "cm_ln_b": f(inputs["cm_ln_b"]).reshape(1, 512),
        "w_s": f(inputs["w_s"][0]).reshape(1024, 128), "b_s": f(inputs["b_s"][0]),
        "g_s5": f(inputs["g_s5"]).reshape(1, 512), "g_cm": f(inputs["g_cm"]).reshape(1, 512),
        "w_out": f(inputs["w_out"][0]), "norm2": f(inputs["norm2"]).reshape(1, 1024),
        "w_gate": f(inputs["w_gate"][0]), "w_up": f(inputs["w_up"][0]), "w_down": f(inputs["w_down"][0]),
        "norm_f": f(inputs["norm_f"]).reshape(1, 1024),
        "consts": make_consts(),
    }
    return m


def kernel(**inputs):
    nc = build(KDEBUG)
    in_maps = [_prep_inputs(inputs, c) for c in range(8)]
    res = run_bass_kernel_spmd(nc, in_maps, core_ids=list(range(8)))
    R = res.results
    yp = np.stack([R[c]["yp"] for c in range(8)]).astype(np.float32)
    ys = np.concatenate([R[c]["ys"].reshape(16, 8, 1024) for c in range(8)]).astype(np.float32)
    pr = np.stack([R[c]["pr"] for c in range(8)])[None].astype(np.float32)
    pi = np.stack([R[c]["pi"] for c in range(8)])[None].astype(np.float32)
    pv = np.stack([R[c]["pv"] for c in range(8)])[None].astype(np.float32)
    sr = np.concatenate([R[c]["sr"].reshape(16, 32, 64) for c in range(8)])[None].astype(np.float32)
    si = np.concatenate([R[c]["si"].reshape(16, 32, 64) for c in range(8)])[None].astype(np.float32)
    sv = np.concatenate([R[c]["sv"].reshape(16, 8, 512) for c in range(8)])[None].astype(np.float32)
    return (yp, ys, pr, pi, pv, sr, si, sv)
```

```python
import os
import numpy as np
from contextlib import ExitStack
import concourse.bass as bass
import concourse.mybir as mybir
from concourse.bass_utils import run_bass_kernel_spmd

F32 = mybir.dt.float32
BF16 = mybir.dt.bfloat16
AF = mybir.ActivationFunctionType
ALU = mybir.AluOpType

NT = 17
TOK = 2176
NFF = 22
EPS = 1e-6
PI = float(np.pi)
TWO_PI = 2.0 * PI
MAGIC = 12582912.0
CW_C1 = 6.28125
CW_C2 = TWO_PI - 6.28125
PI_LO = 3.1415925

C_ID = 0
C_TM = 128
C_CM = 256
C_IO = 384
C_EPS = 640
C_EPS4 = 641
C_TAU = 642
C_ONE = 659
C_PM = 723
C_HPI = 725
CW = 726

KDEBUG = os.environ.get("KDEBUG", "")


def make_consts():
    c = np.zeros((128, CW), np.float32)
    c[:, C_ID:C_ID + 128] = np.eye(128, dtype=np.float32)
    p = np.arange(128)
    c[:, C_TM:C_TM + 128] = (p[None, :] // 16 >= p[:, None] // 16).astype(np.float32)
    c[:, C_CM:C_CM + 128] = (p[None, :] >= p[:, None]).astype(np.float32)
    c[:, C_IO:C_IO + 256] = np.arange(256, dtype=np.float32)[None, :]
    c[:, C_EPS] = EPS
    c[:, C_EPS4] = 4 * EPS
    c[:, C_TAU:C_TAU + 17] = np.array(list(range(9)) + list(range(7, -1, -1)), np.float32)[None, :]
    c[:, C_ONE:C_ONE + 64] = 1.0
    c[0:64, C_PM] = 1.0
    c[64:128, C_PM + 1] = 1.0
    c[:, C_HPI] = PI / 2
    return c


class Res:
    __slots__ = ("name", "w", "rs")

    def __init__(self, name):
        self.name = name
        self.w = None
        self.rs = {}


class Eng:
    def __init__(self, name, sem):
        self.name = name
        self.sem = sem
        self.cnt = 0
        self.waited = {}
        self.ops = []


class DSem:
    def __init__(self, sem):
        self.sem = sem
        self.cnt = 0
        self.last = None


class _Single:
    def __init__(self, fn):
        self.fn = fn

    def __call__(self, e):
        return self.fn(e)


class Prog:
    def __init__(self, nc, es, ndma=40):
        self.nc = nc
        self.E = {n: Eng(n, es.enter_context(nc.semaphore("sem_" + n))) for n in ("pe", "act", "dve", "pool", "sp")}
        self.dsems_hw = [DSem(es.enter_context(nc.semaphore("dq%d" % i))) for i in range(ndma)]
        self.dsems_sw = [DSem(es.enter_context(nc.semaphore("dw%d" % i))) for i in range(16)]
        self.dsems = self.dsems_hw + self.dsems_sw
        self.rr = 0
        self.rr_sw = 0
        self.res = {}
        self.out_toks = []
        self.limit = None
        self.nops = 0
        self.lazy = []
        self.last_ds = None

    def r(self, *key):
        x = self.res.get(key)
        if x is None:
            x = Res(key)
            self.res[key] = x
        return x

    def _deps(self, eng, reads, writes, is_dma):
        need = {}

        def add(tok, kind):
            if tok is None:
                return
            sem, val, teng, tdma = tok
            if not tdma and not is_dma and teng == eng:
                if eng == "pe":
                    return
                if kind != "raw" and eng in os.environ.get("KRAWONLY", "").split(","):
                    return
            k = id(sem)
            cur = need.get(k)
            if cur is None or cur[1] < val:
                need[k] = (sem, val)

        for r in reads:
            add(r.w, "raw")
        for w in writes:
            add(w.w, "waw")
            for t in w.rs.values():
                add(t, "war")
        return need

    def _finish(self, E, need, fn, inc, tok, reads, writes):
        waits = []
        for k, (sem, val) in need.items():
            if E.waited.get(k, 0) < val:
                E.waited[k] = val
                waits.append((sem, val))
        E.ops.append((waits, fn, inc))
        k = id(tok[0])
        for r in reads:
            cur = r.rs.get(k)
            if cur is None or cur[1] < tok[1]:
                r.rs[k] = tok
        for w in writes:
            w.w = tok
            w.rs = {}
        return tok

    def op(self, eng, fn, reads=(), writes=()):
        if eng != "pe":
            fn = _Single(fn)
        self.nops += 1
        if self.limit is not None and self.nops > self.limit:
            return None
        E = self.E[eng]
        need = self._deps(eng, reads, writes, False)
        E.cnt += 1
        tok = (E.sem, E.cnt, eng, False)
        return self._finish(E, need, fn, (E.sem, 1), tok, reads, writes)

    def dma(self, q, fns, reads=(), writes=(), out=False):
        if not isinstance(fns, (list, tuple)):
            fns = [fns]
        self.nops += 1
        if self.limit is not None and self.nops > self.limit and not out:
            return None
        E = self.E[q]
        need = self._deps(q, reads, writes, True)
        if q == "pool":
            ds = self.dsems_sw[self.rr_sw % len(self.dsems_sw)]
            self.rr_sw += 1
        else:
            ds = self.dsems_hw[self.rr % len(self.dsems_hw)]
            self.rr += 1
        if ds.last is not None:
            k = id(ds.sem)
            cur = need.get(k)
            if cur is None or cur[1] < ds.last[1]:
                need[k] = (ds.sem, ds.last[1])
        ds.cnt += 16 * len(fns)
        tok = (ds.sem, ds.cnt, q, True)
        ds.last = tok
        self.last_ds = ds
        if ds in self.lazy:
            self.lazy.remove(ds)

        def fn(e, fns=fns, sem=ds.sem):
            last = None
            for f in fns:
                last = f(e)
                last.then_inc(sem, 16)
            return None

        self._finish(E, need, fn, None, tok, reads, writes)
        if out:
            self.out_toks.append(tok)
        return tok

    def wait_all_dma(self, q="sp"):
        E = self.E[q]
        waits = []
        for ds in self.dsems:
            if ds in self.lazy:
                continue
            if ds.last is not None and E.waited.get(id(ds.sem), 0) < ds.last[1]:
                E.waited[id(ds.sem)] = ds.last[1]
                waits.append((ds.sem, ds.last[1]))
        if waits:
            E.ops.append((waits, None, None))

    def emit(self):
        self.wait_all_dma("sp")
        with self.nc.Block() as block:
            regs = (("pe", block.tensor), ("act", block.scalar), ("dve", block.vector),
                    ("pool", block.gpsimd), ("sp", block.sync))
            for name, reg in regs:
                E = self.E[name]
                ops = E.ops
                E.ops = []
                if not ops:
                    continue

                def f(e, ops=ops):
                    for waits, fn, inc in ops:
                        attach = None
                        if isinstance(fn, _Single) and waits:
                            attach = waits[-1]
                            waits = waits[:-1]
                        for sem, val in waits:
                            e.wait_ge(sem, val)
                        if fn is None:
                            continue
                        ins = fn(e)
                        if attach is not None:
                            ins._wait_ge(attach[0], attach[1])
                        if inc is not None:
                            ins.then_inc(inc[0], inc[1])

                reg(f)


def build(debug=""):
    nc = bass.Bass("TRN2", target_bir_lowering=False)
    D = {}

    def din(name, shape):
        D[name] = nc.dram_tensor(name, list(shape), F32, kind="ExternalInput").ap()

    def dout(name, shape):
        D[name] = nc.dram_tensor(name, list(shape), F32, kind="ExternalOutput").ap()

    din("xp", (2048, 1024)); din("xs", (128, 1024)); din("h0r", (16, 2048)); din("h0i", (16, 2048))
    din("norm1", (1, 1024)); din("w_in", (1024, 1536)); din("lam_re", (32, 64)); din("lam_im", (32, 64))
    din("log_dt", (1, 32)); din("b_re", (2048, 16)); din("b_im", (2048, 16)); din("c_re", (512, 64)); din("c_im", (512, 64))
    din("d_skip", (1, 512)); din("w_glu", (512, 512)); din("b_glu", (1, 512)); din("cm_ln_g", (1, 512)); din("cm_ln_b", (1, 512))
    din("w_s", (1024, 128)); din("b_s", (8, 128)); din("g_s5", (1, 512)); din("g_cm", (1, 512)); din("w_out", (1024, 1024))
    din("norm2", (1, 1024)); din("w_gate", (1024, 2816)); din("w_up", (1024, 2816)); din("w_down", (2816, 1024)); din("norm_f", (1, 1024))
    din("consts", (128, CW))
    dout("yp", (2048, 1024)); dout("ys", (128, 1024)); dout("pr", (32, 64)); dout("pi", (32, 64)); dout("pv", (128, 512))
    dout("sr", (16, 2048)); dout("si", (16, 2048)); dout("sv", (128, 512))
    scr1 = nc.dram_tensor("scr1", [TOK], F32, kind="Internal").ap()
    scr2 = nc.dram_tensor("scr2", [TOK], F32, kind="Internal").ap()
    dbg = {}

    def xsrc(t):
        return D["xp"][t * 128:(t + 1) * 128, :] if t < 16 else D["xs"][:, :]

    def ydst(t):
        return D["yp"][t * 128:(t + 1) * 128, :] if t < 16 else D["ys"][:, :]

    with ExitStack() as top:
        P = Prog(nc, top)
        r = P.r

        def sb(es, name, shape, dt=F32):
            return es.enter_context(nc.sbuf_tensor(name, list(shape), dt))

        PS = [top.enter_context(nc.psum_tensor("ps%d" % i, [128, 512], F32)) for i in range(8)]
        PSB = [p[:].bitcast(BF16) for p in PS]

        def psr(i):
            return r("ps", i)

        cst = sb(top, "cst", (128, CW))
        identb = sb(top, "identb", (128, 128), BF16)
        onesb = sb(top, "onesb", (1, 128), BF16)
        actA = sb(top, "actA", (128, 8, TOK), BF16)
        mixT = sb(top, "mixT", (128, 8, TOK), BF16)
        mixs5 = mixT[:, 0:4, :]
        mixcm = mixT[:, 4:8, :]
        ssq1 = sb(top, "ssq1", (128, NT)); rstd1 = sb(top, "rstd1", (128, NT))
        ssqcm = sb(top, "ssqcm", (128, NT)); rstdcm = sb(top, "rstdcm", (128, NT))
        ssq5 = sb(top, "ssq5", (128, 24)); ssq5n = sb(top, "ssq5n", (128, NT)); rstd5 = sb(top, "rstd5", (128, NT))
        H0T = [sb(top, "H0T%d" % i, (128, 16, 16)) for i in range(2)]
        lr = sb(top, "lr", (128, 16)); li = sb(top, "li", (128, 16)); ldt = sb(top, "ldt", (128, 16))
        Br = sb(top, "Br", (128, 16, 16)); Bi = sb(top, "Bi", (128, 16, 16))
        Cn = [sb(top, "Cn%d" % i, (128, 4, 64)) for i in range(2)]
        dcol = sb(top, "dcol", (128, 32))
        mid = ExitStack()
        P8 = sb(mid, "P8", (128, 3, 4096), BF16)
        identf = cst[:, C_ID:C_ID + 128]
        epsc = cst[:, C_EPS:C_EPS + 1]
        eps4c = cst[:, C_EPS4:C_EPS4 + 1]

        P.dma("sp", lambda e: e.dma_start(out=cst[:], in_=D["consts"][:, :]), writes=[r("cst")])
        def _ncd(e, **kw):
            with nc.allow_non_contiguous_dma(reason="small strided param load"):
                return e.dma_start(**kw)
        def load_s5_params():
            P.dma("sp", [lambda e: _ncd(e, out=lr[:], in_=D["lam_re"].rearrange("(P e) n -> (e n) P", e=2)),
                          lambda e: _ncd(e, out=li[:], in_=D["lam_im"].rearrange("(P e) n -> (e n) P", e=2)),
                          lambda e: _ncd(e, out=ldt[0:64, :], in_=D["log_dt"][0:1, 0:32:2].partition_broadcast(64)),
                          lambda e: _ncd(e, out=ldt[64:128, :], in_=D["log_dt"][0:1, 1:32:2].partition_broadcast(64))],
                  writes=[r("lam")])
            P.dma("sp", [lambda e: _ncd(e, out=Br[:], in_=D["b_re"].rearrange("(P e n) i -> (e n) P i", e=2, n=64)),
                          lambda e: _ncd(e, out=Bi[:], in_=D["b_im"].rearrange("(P e n) i -> (e n) P i", e=2, n=64)),
                          lambda e: _ncd(e, out=Cn[0][:], in_=D["c_re"].rearrange("(t r) n -> r t n", r=128)),
                          lambda e: _ncd(e, out=Cn[1][:], in_=D["c_im"].rearrange("(t r) n -> r t n", r=128))],
                  writes=[r("BC")])
            P.dma("sp", [(lambda s_: (lambda e: _ncd(e, out=dcol[16 * s_:16 * s_ + 16, :], in_=D["d_skip"][0:1, :].rearrange("o (g i) -> (o i) g", i=16))))(s_)
                          for s_ in range(8)], writes=[r("dcol")])
        P.op("dve", lambda e: e.tensor_copy(out=identb[:], in_=identf), reads=[r("cst")], writes=[r("identb")])
        P.op("dve", lambda e: e.memset(onesb[:], 1.0), writes=[r("onesb")])
        P.op("dve", lambda e: e.memset(ssq5[:], 0.0), writes=[r("ssq5")])

        def pe_transposes(e, out_bf, srcs, npart=128):
            last = None
            for i, s in enumerate(srcs):
                last = e.transpose(out_bf[i], s, identb[0:npart, 0:npart])
            return last

        with ExitStack() as es:
            win = sb(es, "win", (128, 8, 1536), BF16)
            srt1 = sb(es, "srt1", (128, NT)); rstd8 = sb(es, "rstd8", (128, 24))
            g1bc = sb(es, "g1bc", (128, 1024)); lngbc = sb(es, "lngbc", (128, 512)); lnbbc = sb(es, "lnbbc", (128, 512))
            gcmbc = sb(es, "gcmbc", (128, 512))
            biasP = sb(es, "biasP", (128, 512)); biasS = sb(es, "biasS", (128, 512))
            bsT = sb(es, "bsT", (128, 8)); bsTs = sb(es, "bsTs", (128, 8))
            WT = sb(es, "WT", (128, 8, 128), BF16); WTs = sb(es, "WTs", (128, 8, 128), BF16)
            xt = [sb(es, "xt%d" % i, (128, 1024)) for i in range(2)]
            hb = [sb(es, "hb%d" % i, (128, 1024), BF16) for i in range(2)]
            junk = sb(es, "junk", (128, 1024), BF16)
            wsn = junk[:].rearrange("p (h s) -> p h s", h=8)
            NSL = 8
            ut = sb(es, "ut", (128, NSL, 512)); vt = sb(es, "vt", (128, NSL, 512))
            vbf = sb(es, "vbf", (128, NSL, 512), BF16)
            tmpc = [sb(es, "tmpc0", (128, 512))] * 2
            vout = tmpc
            ob = [sb(es, "ob%d" % i, (128, 512), BF16) for i in range(4)]
            st6 = sb(es, "st6", (128, NT, 6)); mv = sb(es, "mv", (128, NT, 2))
            sdln = sb(es, "sdln", (128, NT)); rsln = sb(es, "rsln", (128, NT))

            P.dma("pool", [lambda e: e.dma_start(out=wsn, in_=D["w_s"].rearrange("(h t) s -> t h s", h=8))], writes=[r("junk")])
            wv = D["w_in"].rearrange("(kt p) n -> p kt n", p=128)
            for k_ in range(8):
                P.dma("pool", [lambda e, k_=k_: e.dma_start(out=win[:, k_, 512:1536], in_=wv[:, k_, 512:1536])], writes=[r("win_uv", k_)])
            P.dma("pool", [lambda e: e.dma_start(out=win[:, :, 0:512], in_=wv[:, :, 0:512])], writes=[r("win_s5")])
            P.dma("sp", [lambda e: e.dma_start(out=g1bc[:], in_=D["norm1"][0:1, :].partition_broadcast(128))], writes=[r("bc1")])
            for t_ in range(2):
                P.dma("sp", [lambda e, t_=t_: e.dma_start(out=xt[t_ % 2][:], in_=xsrc(t_))], writes=[r("xt", t_ % 2)])
            P.dma("sp", [lambda e: e.dma_start(out=lngbc[:], in_=D["cm_ln_g"][0:1, :].partition_broadcast(128)),
                         lambda e: e.dma_start(out=lnbbc[:], in_=D["cm_ln_b"][0:1, :].partition_broadcast(128)),
                         lambda e: e.dma_start(out=gcmbc[:], in_=D["g_cm"][0:1, :].partition_broadcast(128))],
                  writes=[r("bc")])

            def ld_bs(e):
                with nc.allow_non_contiguous_dma(reason="tiny bias transposes"):
                    last = e.dma_start(out=bsT[:], in_=D["b_s"].rearrange("h t -> t h"))
                return last
            P.dma("sp", [ld_bs], writes=[r("bsT")])

            def ld_bss(q):
                def f(e):
                    with nc.allow_non_contiguous_dma(reason="tiny bias transposes"):
                        return e.dma_start(out=bsTs[8 * q:8 * q + 8, :], in_=D["b_s"][:, 0:8].rearrange("h t -> t h"))
                return f
            P.dma("act", [ld_bss(q) for q in range(16)], writes=[r("bsTs")])
            P.op("dve", lambda e: e.tensor_copy(out=biasP[:].rearrange("p (h d) -> p h d", h=8), in_=bsT[:].unsqueeze(2).to_broadcast([128, 8, 64])),
                 reads=[r("bsT")], writes=[r("biasP")])
            P.op("dve", lambda e: e.tensor_copy(out=biasS[:].rearrange("p (h d) -> p h d", h=8), in_=bsTs[:].unsqueeze(2).to_broadcast([128, 8, 64])),
                 reads=[r("bsTs")], writes=[r("biasS")])
            P.op("pe", lambda e: pe_transposes(e, [PSB[0][:, h * 128:(h + 1) * 128] for h in range(8)], [wsn[:, h, :] for h in range(8)]),
                 reads=[r("junk"), r("identb")], writes=[psr(0)])
            P.op("dve", lambda e: e.tensor_tensor(out=WT[:], in0=PSB[0][:, 0:1024].rearrange("p (h t) -> p h t", h=8),
                                                  in1=cst[:, C_CM:C_CM + 128].unsqueeze(1).to_broadcast([128, 8, 128]), op=ALU.mult),
                 reads=[psr(0), r("cst")], writes=[r("WT")])
            def build_WTs():
                P.op("dve", lambda e: e.memset(WTs[:], 0.0), writes=[r("WTs")])
                P.dma("sp", [(lambda q: (lambda e: e.dma_start(out=WTs[8 * q:8 * q + 8, :, 8 * q:8 * q + 8], in_=WT[0:8, :, 0:8])))(q) for q in range(16)],
                      reads=[r("WT")], writes=[r("WTs")])

            groups = [[0, 1, 2, 3], [4, 5, 6, 7], [8, 9, 10, 11], [12, 13, 14, 15], [16]]
            slot_of = {}
            for gi, g in enumerate(groups):
                for j, t in enumerate(g):
                    slot_of[t] = (gi % 2) * 4 + j

            def stage_A(g):
                for t in g:
                    xs_ = xt[t % 2]; rx = r("xt", t % 2); hbt = hb[t % 2]; rhb = r("hb", t % 2); pb = t % 2
                    if t >= 2:
                        P.dma("sp", [lambda e, xs_=xs_, t=t: e.dma_start(out=xs_[:], in_=xsrc(t))], writes=[rx])
                    P.op("act", lambda e, xs_=xs_, t=t: e.activation(out=junk[:], in_=xs_[:], func=AF.Square, accum_out=ssq1[:, t:t + 1]),
                         reads=[rx], writes=[r("junk"), r("ssq1", t)])
                    P.op("dve", lambda e, xs_=xs_, hbt=hbt: e.tensor_tensor(out=hbt[:], in0=xs_[:], in1=g1bc[:], op=ALU.mult),
                         reads=[rx, r("bc1")], writes=[rhb])
                    P.op("pe", lambda e, hbt=hbt, pb=pb: pe_transposes(e, [PSB[pb][:, k * 128:(k + 1) * 128] for k in range(8)],
                                                                   [hbt[:, k * 128:(k + 1) * 128] for k in range(8)]),
                         reads=[rhb, r("identb")], writes=[psr(pb)])
                    P.op("dve", lambda e, t=t, pb=pb: e.tensor_copy(out=actA[:, :, t * 128:(t + 1) * 128],
                                                                  in_=PSB[pb][:, 0:1024].rearrange("p (k c) -> p k c", k=8)),
                         reads=[psr(pb)], writes=[r("hT", t)])
                c0, c1 = g[0], g[-1] + 1
                P.op("act", lambda e: e.activation(out=srt1[:, c0:c1], in_=ssq1[:, c0:c1], func=AF.Sqrt, bias=epsc, scale=1.0 / 1024),
                     reads=[r("ssq1", t) for t in g] + [r("cst")], writes=[r("srt1", c0)])
                P.op("dve", lambda e: e.reciprocal(out=rstd1[:, c0:c1], in_=srt1[:, c0:c1]), reads=[r("srt1", c0)], writes=[r("rstd1", t) for t in g])

            def stage_B1(g):
                for t in g:
                    sl = slot_of[t]; bu = 2 + 2 * (t % 2); bv = bu + 1

                    def mm(e, t=t, bank=bu, c0=512):
                        last = None
                        for k in range(8):
                            last = e.matmul(PS[bank][:, :], lhsT=actA[:, k, t * 128:(t + 1) * 128], rhs=win[:, k, c0:c0 + 512],
                                            start=(k == 0), stop=(k == 7))
                        return last
                    P.op("pe", lambda e, t=t, bu=bu: mm(e, t, bu, 512), reads=[r("hT", t)] + [r("win_uv", k_) for k_ in range(8)], writes=[psr(bu)])
                    P.op("pe", lambda e, t=t, bv=bv: mm(e, t, bv, 1024), reads=[r("hT", t)] + [r("win_uv", k_) for k_ in range(8)], writes=[psr(bv)])
                    P.op("act", lambda e, t=t, sl=sl, bu=bu: e.activation(out=ut[:, sl, :], in_=PS[bu][:, :], func=AF.Gelu_apprx_tanh, scale=rstd1[:, t:t + 1]),
                         reads=[psr(bu), r("rstd1", t)], writes=[r("ut", sl)])
                    P.op("act", lambda e, t=t, sl=sl, bv=bv: e.activation(out=vt[:, sl, :], in_=PS[bv][:, :], func=AF.Gelu_apprx_tanh, scale=rstd1[:, t:t + 1]),
                         reads=[psr(bv), r("rstd1", t)], writes=[r("vt", sl)])
                    P.op("dve", lambda e, t=t, sl=sl: e.bn_stats(out=st6[:, t, :], in_=vt[:, sl, :]), reads=[r("vt", sl)], writes=[r("st6", t)])
                    P.op("dve", lambda e, t=t: e.bn_aggr(out=mv[:, t, :], in_=st6[:, t, :]), reads=[r("st6", t)], writes=[r("mv", t)])

            def stage_LN(g):
                c0, c1 = g[0], g[-1] + 1
                P.op("act", lambda e: e.activation(out=sdln[:, c0:c1], in_=mv[:, c0:c1, 1], func=AF.Sqrt, bias=epsc, scale=1.0),
                     reads=[r("mv", t) for t in g] + [r("cst")], writes=[r("sdln", c0)])
                P.op("dve", lambda e: e.reciprocal(out=rsln[:, c0:c1], in_=sdln[:, c0:c1]), reads=[r("sdln", c0)], writes=[r("rsln", c0)])
                for t in g:
                    sl = slot_of[t]
                    P.op("dve", lambda e, t=t, sl=sl: e.tensor_scalar(out=vt[:, sl, :], in0=vt[:, sl, :], scalar1=mv[:, t, 0:1], scalar2=rsln[:, t:t + 1],
                                                                    op0=ALU.subtract, op1=ALU.mult),
                         reads=[r("vt", sl), r("mv", t), r("rsln", c0)], writes=[r("vt", sl)])
                    lne = "dve" if t % 2 == 0 else "pool"
                    P.op(lne, lambda e, sl=sl: e.tensor_tensor(out=vt[:, sl, :], in0=vt[:, sl, :], in1=lngbc[:], op=ALU.mult),
                         reads=[r("vt", sl), r("bc")], writes=[r("vt", sl)])
                    P.op(lne, lambda e, sl=sl: e.tensor_tensor(out=vbf[:, sl, :], in0=vt[:, sl, :], in1=lnbbc[:], op=ALU.add),
                         reads=[r("vt", sl), r("bc")], writes=[r("vbf", sl)])
                    if t >= 15:
                        vo = vout[t - 15]; dn = "pv" if t == 15 else "sv"
                        P.op("pool", lambda e, sl=sl, vo=vo: e.tensor_tensor(out=vo[:], in0=vt[:, sl, :], in1=lnbbc[:], op=ALU.add),
                             reads=[r("vt", sl), r("bc")], writes=[r("tmpc")])
                        P.dma("sp", [lambda e, vo=vo, dn=dn: e.dma_start(out=D[dn][:, :], in_=vo[:])], reads=[r("tmpc")], out=True)

            def stage_C1(g):
                for j_, t in enumerate(g):
                    sl = slot_of[t]; Wm = WT if t < 16 else WTs; bias = biasP if t < 16 else biasS
                    rW = r("WT") if t < 16 else r("WTs"); rb = r("biasP") if t < 16 else r("biasS")
                    tc_ = tmpc[0]; rtc = r("tmpc"); obt = ob[j_]; rob = r("ob", j_); mb = 6 + t % 2

                    def mm(e, sl=sl, Wm=Wm, mb=mb):
                        last = None
                        for h in range(8):
                            last = e.matmul(PS[mb][:, h * 64:(h + 1) * 64], lhsT=Wm[:, h, :], rhs=vbf[:, sl, h * 64:(h + 1) * 64], start=True, stop=True)
                        return last
                    P.op("pe", mm, reads=[r("vbf", sl), rW], writes=[psr(mb)])
                    P.op("dve", lambda e, tc_=tc_, bias=bias, mb=mb: e.tensor_tensor(out=tc_[:], in0=PS[mb][:, :], in1=bias[:], op=ALU.add),
                         reads=[psr(mb), rb], writes=[rtc])
                    P.op("dve", lambda e, tc_=tc_, sl=sl: e.tensor_tensor(out=ut[:, sl, :], in0=tc_[:], in1=ut[:, sl, :], op=ALU.mult),
                         reads=[rtc, r("ut", sl)], writes=[r("ut", sl)])
                    P.op("act", lambda e, t=t, sl=sl: e.activation(out=junk[:, 0:512], in_=ut[:, sl, :], func=AF.Square, accum_out=ssqcm[:, t:t + 1]),
                         reads=[r("ut", sl)], writes=[r("junk"), r("ssqcm", t)])
                    P.op("pool", lambda e, sl=sl, obt=obt: e.tensor_tensor(out=obt[:], in0=ut[:, sl, :], in1=gcmbc[:], op=ALU.mult),
                         reads=[r("ut", sl), r("bc")], writes=[rob])

            def stage_C2(g):
                for j_, t in enumerate(g):
                    obt = ob[j_]; rob = r("ob", j_); tb = t % 2
                    P.op("pe", lambda e, obt=obt, tb=tb: pe_transposes(e, [PSB[tb][:, j * 128:(j + 1) * 128] for j in range(4)],
                                                                     [obt[:, j * 128:(j + 1) * 128] for j in range(4)]),
                         reads=[rob, r("identb")], writes=[psr(tb)])
                    P.op("act", lambda e, t=t, tb=tb: e.activation(out=mixcm[:, :, t * 128:(t + 1) * 128], in_=PSB[tb][:, 0:512].rearrange("p (j c) -> p j c", j=4),
                                                                   func=AF.Copy),
                         reads=[psr(tb)], writes=[r("mixcm", t)])

            stage_A(groups[0])
            NG = len(groups)
            for gi, g in enumerate(groups):
                if gi + 1 < NG:
                    stage_A(groups[gi + 1])
                if gi == 1:
                    build_WTs()
                if gi == NG - 2:
                    load_s5_params()
                stage_B1(g)
                if gi >= 2:
                    stage_C2(groups[gi - 2])
                if gi >= 1:
                    stage_C1(groups[gi - 1])
                stage_LN(g)
            if NG >= 2:
                stage_C2(groups[NG - 2])
            stage_C1(groups[NG - 1])
            stage_C2(groups[NG - 1])

            def st_rs(e):
                with nc.allow_non_contiguous_dma(reason="tiny stat relayout"):
                    return e.dma_start(out=scr1.rearrange("(t p) -> p t", p=128), in_=rstd1[:])
            P.dma("act", [st_rs], reads=[r("rstd1", t) for t in range(NT)], writes=[r("scr1")])

            P.dma("act", [lambda e: e.dma_start(out=rstd8[:, 0:16].rearrange("p (a s) -> p a s", a=2),
                                               in_=scr1[0:2048].rearrange("(a b s) -> b a s", a=2, s=8)),
                         lambda e: e.dma_start(out=rstd8[0:16, 16:24], in_=scr1[2048:2176].rearrange("(q s) -> q s", s=8))],
                  reads=[r("scr1")], writes=[r("rstd8")])
            P8v = P8[:].rearrange("p a (g s i) -> p a g s i", g=32, s=8)
            for bt in range(3):
                npart = 128 if bt < 2 else 16
                for s in range(8):
                    idx = bt * 8 + s; bank = 2 + (idx % 4)
                    if bt < 2:
                        tiles = range(bt * 8, bt * 8 + 8)
                        base = bt * 1024 + s; end = bt * 1024 + 1024
                    else:
                        tiles = [16]
                        base = 2048 + s; end = 2176

                    def mm(e, bank=bank, base=base, end=end, npart=npart):
                        last = None
                        for k in range(8):
                            last = e.matmul(PS[bank][0:npart, :], lhsT=actA[:, k, base:end:8], rhs=win[:, k, 0:512], start=(k == 0), stop=(k == 7))
                        return last
                    P.op("pe", mm, reads=[r("hT", t) for t in tiles] + [r("win_s5")], writes=[psr(bank)])
                    eng = "act" if s % 2 == 0 else "dve"
                    if eng == "act":
                        P.op("act", lambda e, bank=bank, bt=bt, s=s, idx=idx, npart=npart: e.activation(
                            out=P8v[0:npart, bt, :, s, :], in_=PS[bank][0:npart, :].rearrange("p (g i) -> p g i", g=32), func=AF.Copy,
                            scale=rstd8[0:npart, idx:idx + 1]), reads=[psr(bank), r("rstd8")], writes=[r("P8", bt, s)])
                    else:
                        P.op("dve", lambda e, bank=bank, bt=bt, s=s, idx=idx, npart=npart: e.tensor_scalar(
                            out=P8v[0:npart, bt, :, s, :], in0=PS[bank][0:npart, :].rearrange("p (g i) -> p g i", g=32),
                            scalar1=rstd8[0:npart, idx:idx + 1], scalar2=None, op0=ALU.mult), reads=[psr(bank), r("rstd8")], writes=[r("P8", bt, s)])
            if "ab" in debug:
                dbg["hT"] = nc.dram_tensor("dbg_hT", [128, 8 * TOK], BF16, kind="ExternalOutput").ap()
                dbg["mixcm"] = nc.dram_tensor("dbg_mixcm", [128, 4, TOK], BF16, kind="ExternalOutput").ap()
                dbg["P8"] = nc.dram_tensor("dbg_P8", [128, 3 * 4096], BF16, kind="ExternalOutput").ap()
                P.dma("sp", [lambda e: e.dma_start(out=dbg["hT"][:, :], in_=actA[:].rearrange("p k t -> p (k t)"))], reads=[r("hT", t) for t in range(NT)], out=True)
                P.dma("sp", [lambda e: e.dma_start(out=dbg["mixcm"][:, :, :], in_=mixT[:, 4:8, :])], reads=[r("mixcm", t) for t in range(NT)], out=True)
                P.dma("sp", [lambda e: e.dma_start(out=dbg["P8"][:, :], in_=P8[:].rearrange("p a c -> p (a c)"))], reads=[r("P8", bt, s) for bt in range(3) for s in range(8)], out=True)
            P.emit()
        if debug == "ab":
            mid.close()
            return nc

        scrF = actA[:].rearrange("p k t -> p (k t)").bitcast(F32)
        scrB = actA[:].rearrange("p k t -> p (k t)")

        def dv(fn, reads, writes, eng="dve"):
            return P.op(eng, fn, reads=[r(*x) if isinstance(x, tuple) else r(x) for x in reads],
                        writes=[r(*x) if isinstance(x, tuple) else r(x) for x in writes])

        def ncdma(e, **kw):
            with nc.allow_non_contiguous_dma(reason="small strided param load"):
                return e.dma_start(**kw)

        W1re = sb(mid, "W1re", (128, 32, 128), BF16); W1im = sb(mid, "W1im", (128, 32, 128), BF16)
        Tm = sb(mid, "Tm", (128, 32, 128), BF16)
        W2rb = sb(mid, "W2rb", (128, 16, 128), BF16); W2ib = sb(mid, "W2ib", (128, 16, 128), BF16)
        s5c = sb(mid, "s5c", (128, 4, 16))
        wglu = sb(mid, "wglu", (128, 4, 512), BF16)
        bglub = sb(mid, "bglub", (1, 512), BF16)
        gs5bc = sb(mid, "gs5bc", (128, 512))
        with ExitStack() as es:
            NTAU = 17
            dtt = sb(es, "dtt", (128, 16))
            lrdt = sb(es, "lrdt", (128, 16)); lidt = sb(es, "lidt", (128, 16))
            CTn = [sb(es, "CTn%d" % i, (64, 512)) for i in range(2)]
            CT = [sb(es, "CT%d" % i, (128, 16, 16)) for i in range(2)]
            tabs = {n: sb(es, "tab_" + n, (128, NTAU, 16)) for n in
                    ("ARGM", "ANG", "MAGP", "MAGM", "SIN", "COS", "APR", "API", "AMR", "AMI", "RS", "RC", "K", "Y")}
            sm = {n: sb(es, "sm_" + n, (128, 16)) for n in ("am1", "den", "t", "rden", "qr", "qi", "u1", "u2")}
            QB = [sb(es, "QB%d" % i, (128, 16, 16)) for i in range(2)]
            big = {n: sb(es, "big_" + n, (128, 16, 8, 16)) for n in ("W2r", "W2i")}
            for k_, n in enumerate(("Lr", "Li", "t1", "t2")):
                big[n] = scrF[:, k_ * 2048:(k_ + 1) * 2048].rearrange("p (P s o) -> p P s o", P=16, s=8)
            big["M1r"] = big["Lr"]; big["M1i"] = big["Li"]
            tmpT = [sb(es, "tmpT%d" % i, (128, 4, 128)) for i in range(2)]
            Lm = [sb(es, "Lm%d" % i, (128, 4, 2, 128)) for i in range(2)]
            bglu32 = sb(es, "bglu32", (1, 512))

            if os.environ.get("KSTOP"):
                P.nops = 0
                P.limit = int(os.environ["KSTOP"])
            for _ in range(int(os.environ.get("KPAD", "0"))):
                P.E["sp"].ops.append(([(P.E["sp"].sem, 0)], None, None))
            P.dma("pool", [lambda e: e.dma_start(out=wglu[:], in_=D["w_glu"].rearrange("(kt p) n -> p kt n", p=128)),
                           lambda e: e.dma_start(out=bglub[:], in_=D["b_glu"][0:1, :])], writes=[r("wglu")])
            P.dma("pool", [lambda e: e.dma_start(out=gs5bc[:], in_=D["g_s5"][0:1, :].partition_broadcast(128))], writes=[r("gs5bc")])

            T = tabs
            dv(lambda e: e.activation(out=dtt[:], in_=ldt[:], func=AF.Exp), ["lam"], ["dtt"], "act")
            dv(lambda e: e.tensor_tensor(out=lrdt[:], in0=lr[:], in1=dtt[:], op=ALU.mult), ["lam", "dtt"], ["lrdt"])
            dv(lambda e: e.tensor_tensor(out=lidt[:], in0=li[:], in1=dtt[:], op=ALU.mult), ["lam", "dtt"], ["lidt"])
            taub = cst[:, C_TAU:C_TAU + NTAU].unsqueeze(2).to_broadcast([128, NTAU, 16])
            dv(lambda e: e.tensor_tensor(out=T["ARGM"][:], in0=taub, in1=lrdt[:].unsqueeze(1).to_broadcast([128, NTAU, 16]), op=ALU.mult),
               ["cst", "lrdt"], ["ARGM"])
            dv(lambda e: e.tensor_tensor(out=T["ANG"][:], in0=taub, in1=lidt[:].unsqueeze(1).to_broadcast([128, NTAU, 16]), op=ALU.mult),
               ["cst", "lidt"], ["ANG"])
            dv(lambda e: e.activation(out=T["MAGP"][:], in_=T["ARGM"][:], func=AF.Exp), ["ARGM"], ["MAGP"], "act")
            dv(lambda e: e.activation(out=T["MAGM"][:], in_=T["ARGM"][:], func=AF.Exp, scale=-1.0), ["ARGM"], ["MAGM"], "act")

            def range_reduce(dst, src, shift, rn_dst, rn_src, Y, K, rY, rK, eng="dve"):
                if shift != 0.0:
                    dv(lambda e: e.tensor_scalar_add(out=Y, in0=src, scalar1=shift), [rn_src], [rY], eng)
                    y = Y; ry = rY
                else:
                    y = src; ry = rn_src
                dv(lambda e: e.tensor_scalar(out=K, in0=y, scalar1=1.0 / TWO_PI, scalar2=MAGIC, op0=ALU.mult, op1=ALU.add), [ry], [rK], eng)
                dv(lambda e: e.tensor_scalar_add(out=K, in0=K, scalar1=-MAGIC), [rK], [rK], eng)
                dv(lambda e: e.scalar_tensor_tensor(out=dst, in0=K, scalar=-CW_C1, in1=y, op0=ALU.mult, op1=ALU.add), [rK, ry], [rn_dst], "dve")
                dv(lambda e: e.scalar_tensor_tensor(out=dst, in0=K, scalar=-CW_C2, in1=dst, op0=ALU.mult, op1=ALU.add), [rK, rn_dst], [rn_dst], "dve")
                dv(lambda e: e.tensor_scalar(out=dst, in0=dst, scalar1=PI_LO, scalar2=-PI_LO, op0=ALU.min, op1=ALU.max), [rn_dst], [rn_dst], eng)

            range_reduce(T["RS"][:], T["ANG"][:], 0.0, "RS", "ANG", T["Y"][:], T["K"][:], "Y", "K")
            dv(lambda e: e.activation(out=T["SIN"][:], in_=T["RS"][:], func=AF.Sin), ["RS"], ["SIN"], "act")
            range_reduce(T["RC"][:], T["ANG"][:], PI / 2, "RC", "ANG", T["Y"][:], T["K"][:], "Y", "K")
            dv(lambda e: e.activation(out=T["COS"][:], in_=T["RC"][:], func=AF.Sin), ["RC"], ["COS"], "act")
            dv(lambda e: e.tensor_tensor(out=T["APR"][:], in0=T["MAGP"][:], in1=T["COS"][:], op=ALU.mult), ["MAGP", "COS"], ["APR"])
            dv(lambda e: e.tensor_tensor(out=T["API"][:], in0=T["MAGP"][:], in1=T["SIN"][:], op=ALU.mult), ["MAGP", "SIN"], ["API"])
            dv(lambda e: e.tensor_tensor(out=T["AMR"][:], in0=T["MAGM"][:], in1=T["COS"][:], op=ALU.mult), ["MAGM", "COS"], ["AMR"])
            dv(lambda e: e.scalar_tensor_tensor(out=T["AMI"][:], in0=T["MAGM"][:], scalar=-1.0, in1=T["SIN"][:], op0=ALU.mult, op1=ALU.mult),
               ["MAGM", "SIN"], ["AMI"])
            dv(lambda e: e.tensor_copy(out=s5c[:, 0, :], in_=T["MAGP"][:, 8, :]), ["MAGP"], ["s5c"])
            dv(lambda e: e.tensor_copy(out=s5c[:, 1, :], in_=T["RS"][:, 8, :]), ["RS"], ["s5c"])
            dv(lambda e: e.tensor_copy(out=s5c[:, 2, :], in_=T["APR"][:, 8, :]), ["APR"], ["s5c"])
            dv(lambda e: e.tensor_copy(out=s5c[:, 3, :], in_=T["API"][:, 8, :]), ["API"], ["s5c"])
            ar1 = T["APR"][:, 1, :]; ai1 = T["API"][:, 1, :]
            dv(lambda e: e.tensor_scalar_add(out=sm["am1"][:], in0=ar1, scalar1=-1.0), ["APR"], ["am1"])
            dv(lambda e: e.tensor_tensor(out=sm["den"][:], in0=lr[:], in1=lr[:], op=ALU.mult), ["lam"], ["den"])
            dv(lambda e: e.tensor_tensor(out=sm["t"][:], in0=li[:], in1=li[:], op=ALU.mult), ["lam"], ["t"])
            dv(lambda e: e.tensor_tensor(out=sm["den"][:], in0=sm["den"][:], in1=sm["t"][:], op=ALU.add), ["den", "t"], ["den"])
            dv(lambda e: e.reciprocal(out=sm["rden"][:], in_=sm["den"][:]), ["den"], ["rden"])
            dv(lambda e: e.tensor_tensor(out=sm["u1"][:], in0=sm["am1"][:], in1=lr[:], op=ALU.mult), ["am1", "lam"], ["u1"])
            dv(lambda e: e.tensor_tensor(out=sm["u2"][:], in0=ai1, in1=li[:], op=ALU.mult), ["API", "lam"], ["u2"])
            dv(lambda e: e.tensor_tensor(out=sm["u1"][:], in0=sm["u1"][:], in1=sm["u2"][:], op=ALU.add), ["u1", "u2"], ["u1"])
            dv(lambda e: e.tensor_tensor(out=sm["qr"][:], in0=sm["u1"][:], in1=sm["rden"][:], op=ALU.mult), ["u1", "rden"], ["qr"])
            dv(lambda e: e.tensor_tensor(out=sm["u1"][:], in0=ai1, in1=lr[:], op=ALU.mult), ["API", "lam", "qr"], ["u1"])
            dv(lambda e: e.tensor_tensor(out=sm["u2"][:], in0=sm["am1"][:], in1=li[:], op=ALU.mult), ["am1", "lam"], ["u2"])
            dv(lambda e: e.tensor_tensor(out=sm["u1"][:], in0=sm["u1"][:], in1=sm["u2"][:], op=ALU.subtract), ["u1", "u2"], ["u1"])
            dv(lambda e: e.tensor_tensor(out=sm["qi"][:], in0=sm["u1"][:], in1=sm["rden"][:], op=ALU.mult), ["u1", "rden"], ["qi"])
            qrb = sm["qr"][:].unsqueeze(2).to_broadcast([128, 16, 16]); qib = sm["qi"][:].unsqueeze(2).to_broadcast([128, 16, 16])
            t1s = big["t1"][:, :, 0, :]; t2s = big["t2"][:, :, 0, :]
            dv(lambda e: e.tensor_tensor(out=t1s, in0=Br[:], in1=qrb, op=ALU.mult), ["BC", "qr"], ["t1"])
            dv(lambda e: e.tensor_tensor(out=t2s, in0=Bi[:], in1=qib, op=ALU.mult), ["BC", "qi"], ["t2"])
            dv(lambda e: e.tensor_tensor(out=QB[0][:], in0=t1s, in1=t2s, op=ALU.subtract), ["t1", "t2"], ["QB0"])
            dv(lambda e: e.tensor_tensor(out=t1s, in0=Bi[:], in1=qrb, op=ALU.mult), ["BC", "qr", "QB0"], ["t1"])
            dv(lambda e: e.tensor_tensor(out=t2s, in0=Br[:], in1=qib, op=ALU.mult), ["BC", "qi", "QB0"], ["t2"])
            dv(lambda e: e.tensor_tensor(out=QB[1][:], in0=t1s, in1=t2s, op=ALU.add), ["t1", "t2"], ["QB1"])
            for c_ in range(2):
                def mmct(e, c_=c_):
                    last = None
                    for t_ in range(4):
                        last = e.matmul(PS[c_][0:64, t_ * 128:(t_ + 1) * 128], lhsT=Cn[c_][:, t_, :], rhs=identf, start=True, stop=True)
                    return last
                P.op("pe", mmct, reads=[r("BC"), r("cst")], writes=[psr(c_)])
                dv(lambda e, c_=c_: e.tensor_copy(out=CTn[c_][:], in_=PS[c_][0:64, :]), [("ps", c_)], [("CTn", c_)])
                ctv = CTn[c_][:].rearrange("n (P e o) -> n P e o", e=2, o=16)
                P.dma("act", [lambda e, c_=c_, ctv=ctv: e.dma_start(out=CT[c_][0:64, :, :], in_=ctv[:, :, 0, :]),
                             lambda e, c_=c_, ctv=ctv: e.dma_start(out=CT[c_][64:128, :, :], in_=ctv[:, :, 1, :])],
                      reads=[r("CTn", c_)], writes=[r("CT", c_)])

            def tauv(name, lo):
                return T[name][:].rearrange("p t P -> p P t")[:, :, lo:lo + 8].unsqueeze(3).to_broadcast([128, 16, 8, 16])

            def cplx_mul(outr, outi, rn_or, rn_oi, Xr, Xi, rn_x, tr, ti, lo, neg_imag=False):
                xr = Xr[:].unsqueeze(2).to_broadcast([128, 16, 8, 16]); xi = Xi[:].unsqueeze(2).to_broadcast([128, 16, 8, 16])
                ar_ = tauv(tr, lo); ai_ = tauv(ti, lo)
                dv(lambda e: e.tensor_tensor(out=big["t1"][:], in0=xr, in1=ar_, op=ALU.mult), rn_x + [tr, rn_or, rn_oi], ["t1"])
                dv(lambda e: e.tensor_tensor(out=big["t2"][:], in0=xi, in1=ai_, op=ALU.mult), rn_x + [ti, rn_or, rn_oi], ["t2"], "pool")
                dv(lambda e: e.tensor_tensor(out=outr[:], in0=big["t1"][:], in1=big["t2"][:], op=ALU.subtract), ["t1", "t2"], [rn_or])
                dv(lambda e: e.tensor_tensor(out=big["t1"][:], in0=xr, in1=ai_, op=ALU.mult), rn_x + [ti, rn_or], ["t1"])
                dv(lambda e: e.tensor_tensor(out=big["t2"][:], in0=xi, in1=ar_, op=ALU.mult), rn_x + [tr, rn_or], ["t2"], "pool")
                if neg_imag:
                    dv(lambda e: e.scalar_tensor_tensor(out=outi[:], in0=big["t1"][:], scalar=-1.0, in1=big["t2"][:], op0=ALU.mult, op1=ALU.subtract),
                       ["t1", "t2"], [rn_oi])
                else:
                    dv(lambda e: e.tensor_tensor(out=outi[:], in0=big["t1"][:], in1=big["t2"][:], op=ALU.add), ["t1", "t2"], [rn_oi])

            cplx_mul(big["M1r"], big["M1i"], "Lr", "Li", QB[0], QB[1], ["QB0", "QB1"], "APR", "API", 9)
            dv(lambda e: e.memset(W1re[:], 0.0), [], ["W1re"])
            dv(lambda e: e.memset(W1im[:], 0.0), [], ["W1im"], "pool")
            for c_, (M1, W1, rn) in enumerate(((big["M1r"], W1re, "W1re"), (big["M1i"], W1im, "W1im"))):
                for quad in range(4):
                    bank = 6 + quad % 2

                    def mmW(e, quad=quad, bank=bank, M1=M1):
                        last = None
                        for pl in range(4):
                            Pp = quad * 4 + pl
                            last = e.matmul(PS[bank][:, pl * 128:(pl + 1) * 128], lhsT=M1[:, Pp, :, :].rearrange("p s i -> p (s i)"), rhs=identf,
                                            start=True, stop=True)
                        return last
                    P.op("pe", mmW, reads=[r("Lr"), r("Li"), r("cst")], writes=[psr(bank)])
                    w1v = W1[:].rearrange("p (P e) c -> p P e c", e=2)
                    psv = PS[bank][:, :].rearrange("p (P c) -> p P c", P=4)
                    dv(lambda e, w1v=w1v, psv=psv, quad=quad: e.tensor_copy(out=w1v[:, quad * 4:quad * 4 + 4, 0, 0:64], in_=psv[:, :, 0:64]),
                       [("ps", bank)], [rn])
                    dv(lambda e, w1v=w1v, psv=psv, quad=quad: e.tensor_copy(out=w1v[:, quad * 4:quad * 4 + 4, 1, 64:128], in_=psv[:, :, 64:128]),
                       [("ps", bank)], [rn])
            cplx_mul(big["Lr"], big["Li"], "Lr", "Li", QB[0], QB[1], ["QB0", "QB1"], "AMR", "AMI", 1)
            cplx_mul(big["W2r"], big["W2i"], "W2r", "W2i", CT[0], CT[1], [("CT", 0), ("CT", 1)], "APR", "API", 1, neg_imag=True)
            dv(lambda e: e.activation(out=W2rb[:], in_=big["W2r"][:].rearrange("p P s o -> p P (s o)"), func=AF.Copy), ["W2r"], ["W2rb"], "act")
            _srcname = os.environ.get("KSRC", "W2i")
            _dst = {"W2ib": W2ib, "Tm": Tm[:, 0:16, :], "W1im": W1im[:, 0:16, :]}[os.environ.get("KDST", "W2ib")]
            dv(lambda e: e.activation(out=_dst[:] if os.environ.get("KDST", "W2ib") == "W2ib" else _dst, in_=big[_srcname][:].rearrange("p P s o -> p P (s o)"), func=AF.Copy), [_srcname], ["W2ib"], "act")
            def t_copies(quad):
                lmb = Lm[quad % 2]
                for gl in range(4):
                    g_ = quad * 4 + gl; Pp = g_ // 2; ee = g_ % 2
                    dv(lambda e, lmb=lmb, gl=gl, Pp=Pp, ee=ee: e.tensor_scalar(out=lmb[:, gl, 0, :], in0=big["Lr"][:, Pp, :, :].rearrange("p s i -> p (s i)"),
                                                                     scalar1=cst[:, C_PM + ee:C_PM + ee + 1], scalar2=None, op0=ALU.mult),
                       ["Lr", "cst"], [("Lm", quad % 2)])
                    dv(lambda e, lmb=lmb, gl=gl, Pp=Pp, ee=ee: e.tensor_scalar(out=lmb[:, gl, 1, :], in0=big["Li"][:, Pp, :, :].rearrange("p s i -> p (s i)"),
                                                                     scalar1=cst[:, C_PM + ee:C_PM + ee + 1], scalar2=None, op0=ALU.mult),
                       ["Li", "cst"], [("Lm", quad % 2)])

            def t_mm(quad):
                bank = 2 + quad % 4
                lmb = Lm[quad % 2]

                def mmT(e, quad=quad, bank=bank, lmb=lmb):
                    last = None
                    for gl in range(4):
                        g_ = quad * 4 + gl; Pp = g_ // 2
                        e.matmul(PS[bank][:, gl * 128:(gl + 1) * 128], lhsT=lmb[:, gl, 0, :],
                                 rhs=big["W2r"][:, Pp, :, :].rearrange("p s o -> p (s o)"), start=True, stop=False)
                        last = e.matmul(PS[bank][:, gl * 128:(gl + 1) * 128], lhsT=lmb[:, gl, 1, :],
                                        rhs=big["W2i"][:, Pp, :, :].rearrange("p s o -> p (s o)"), start=False, stop=True)
                    return last
                P.op("pe", mmT, reads=[r("Lm", quad % 2), r("W2r"), r("W2i")], writes=[psr(bank)])

            def t_evac(quad):
                bank = 2 + quad % 4
                tT = tmpT[quad % 2]
                dv(lambda e, bank=bank, tT=tT: e.tensor_tensor(out=tT[:], in0=PS[bank][:, :].rearrange("p (g c) -> p g c", g=4),
                                                               in1=cst[:, C_TM:C_TM + 128].unsqueeze(1).to_broadcast([128, 4, 128]), op=ALU.mult),
                   [("ps", bank), "cst"], [("tmpT", quad % 2)])
                for gl in range(4):
                    g_ = quad * 4 + gl
                    dv(lambda e, g_=g_, gl=gl, tT=tT: e.scalar_tensor_tensor(out=Tm[:, g_, :], in0=identf, scalar=dcol[:, g_:g_ + 1], in1=tT[:, gl, :],
                                                                             op0=ALU.mult, op1=ALU.add), [("tmpT", quad % 2), "dcol", "cst"], [("Tm", g_)])
            t_copies(0); t_mm(0)
            for quad in range(8):
                if quad + 1 < 8:
                    t_copies(quad + 1); t_mm(quad + 1)
                t_evac(quad)
            if "setup" in debug:
                _dl = [("Tm", Tm, 4096), ("W1re", W1re, 4096), ("W1im", W1im, 4096), ("W2rb", W2rb, 2048), ("W2ib", W2ib, 2048)]
                if os.environ.get("KNODUMP"):
                    _dl = [x for x in _dl if x[0] not in os.environ["KNODUMP"].split(",")]
                for nm, t_, n_ in _dl:
                    dbg[nm] = nc.dram_tensor("dbg_" + nm, [128, n_], BF16, kind="ExternalOutput").ap()
                    P.dma("sp", [lambda e, nm=nm, t_=t_: e.dma_start(out=dbg[nm][:, :], in_=t_[:].rearrange("p a b -> p (a b)"))],
                          reads=[P.res[k] for k in list(P.res) if k[0] == nm], out=True)
                dbg["s5c"] = nc.dram_tensor("dbg_s5c", [128, 64], F32, kind="ExternalOutput").ap()
                P.dma("sp", [lambda e: e.dma_start(out=dbg["s5c"][:, :], in_=s5c[:].rearrange("p a b -> p (a b)"))], reads=[r("s5c")], out=True)
            P.emit()
        if debug.endswith("setup"):
            mid.close()
            return nc

        with ExitStack() as esd:
            U = sb(esd, "U", (128, 32, 256), BF16); Us = sb(esd, "Us", (128, 32, 16), BF16)
            Hp = [sb(esd, "Hp%d" % i, (128, 16, 256), BF16) for i in range(2)]
            Hs = [sb(esd, "Hs%d" % i, (128, 16, 16), BF16) for i in range(2)]
            Hf = sb(esd, "Hf", (128, 2, 16))
            P.dma("sp", [(lambda c_, Pp: (lambda e: _ncd(e, out=H0T[c_][:, Pp, :],
                                                         in_=D["h0r" if c_ == 0 else "h0i"][:, Pp * 128:(Pp + 1) * 128].rearrange("q p -> p q"))))(c_, Pp)
                         for c_ in range(2) for Pp in range(16)], writes=[r("H0T")])
            for c_ in range(2):
                dv(lambda e, c_=c_: e.memset(Hp[c_][:, :, 0:1], 0.0), [], [("Hp0", c_)], "pool")
            for bt in range(2):
                for oc in range(4):
                    bank = (bt * 4 + oc) % 2
                    P.op("pe", lambda e, bt=bt, oc=oc, bank=bank: pe_transposes(
                        e, [PSB[bank][:, gl * 128:(gl + 1) * 128] for gl in range(8)],
                        [P8[:, bt, (oc * 8 + gl) * 128:(oc * 8 + gl + 1) * 128] for gl in range(8)]),
                        reads=[r("P8", bt, s_) for s_ in range(8)] + [r("identb")], writes=[psr(bank)])
                    if oc % 2 == 0:
                        P.op("act", lambda e, bt=bt, oc=oc, bank=bank: e.activation(out=U[:, oc * 8:(oc + 1) * 8, bt * 128:(bt + 1) * 128],
                                                                                     in_=PSB[bank][:, 0:1024].rearrange("p (g c) -> p g c", g=8), func=AF.Copy),
                             reads=[psr(bank)], writes=[r("U", bt, oc)])
                    else:
                        P.op("dve", lambda e, bt=bt, oc=oc, bank=bank: e.tensor_copy(out=U[:, oc * 8:(oc + 1) * 8, bt * 128:(bt + 1) * 128],
                                                                                      in_=PSB[bank][:, 0:1024].rearrange("p (g c) -> p g c", g=8)),
                             reads=[psr(bank)], writes=[r("U", bt, oc)])

            def tr_s(e):
                last = None
                for g_ in range(32):
                    last = e.transpose(PSB[0][:, g_ * 16:(g_ + 1) * 16], P8[0:16, 2, g_ * 128:(g_ + 1) * 128], identb[0:16, 0:16])
                return last
            P.op("pe", tr_s, reads=[r("P8", 2, s_) for s_ in range(8)] + [r("identb")], writes=[psr(0)])
            dv(lambda e: e.tensor_copy(out=Us[:].rearrange("p g q -> p (g q)"), in_=PSB[0][:, 0:512]), [("ps", 0)], ["Us"])

            for c_, W1 in enumerate((W1re, W1im)):
                def mmxs(e, c_=c_, W1=W1):
                    last = None
                    for Pp in range(16):
                        e.matmul(PS[6 + c_][:, Pp * 16:(Pp + 1) * 16], lhsT=W1[:, 2 * Pp, :], rhs=Us[:, 2 * Pp, :], start=True, stop=False)
                        last = e.matmul(PS[6 + c_][:, Pp * 16:(Pp + 1) * 16], lhsT=W1[:, 2 * Pp + 1, :], rhs=Us[:, 2 * Pp + 1, :], start=False, stop=True)
                    return last
                P.op("pe", mmxs, reads=[r("Us"), r("W1re"), r("W1im")], writes=[psr(6 + c_)])
            T1 = scrF[:, 0:2048].rearrange("p (a b) -> p a b", a=8); T2 = scrF[:, 2048:4096].rearrange("p (a b) -> p a b", a=8)
            PH = scrF[:, 4096:6144].rearrange("p (a b) -> p a b", a=8); KK = scrF[:, 6144:8192].rearrange("p (a b) -> p a b", a=8)
            with ExitStack() as esh:
                Xr = sb(esh, "Xr", (128, 8, 256)); Xi = sb(esh, "Xi", (128, 8, 256))
                CS = sb(esh, "CS", (128, 8, 256)); SN = sb(esh, "SN", (128, 8, 256))
                X = (Xr, Xi)
                P8f2 = P8[:].rearrange("p a c -> p (a c)")
                CS1 = P8f2[:, 0:4096].bitcast(F32).rearrange("p (a b) -> p a b", a=8)
                SN1 = P8f2[:, 4096:8192].bitcast(F32).rearrange("p (a b) -> p a b", a=8)
                tabs_cs = (CS[:], CS1); tabs_sn = (SN[:], SN1)
                dv(lambda e: e.memset(KK[:, 0, 0:1], 0.0), [], ["KK"] + [("P8", bt, s_) for bt in range(3) for s_ in range(8)])
                for hf in range(2):
                    P0 = 8 * hf
                    CSh = tabs_cs[hf]; SNh = tabs_sn[hf]; rcs = ("CS", hf); rsn = ("SN", hf)
                    dv(lambda e, P0=P0: e.tensor_tensor(out=PH, in0=cst[:, C_IO:C_IO + 256].unsqueeze(1).to_broadcast([128, 8, 256]),
                                                        in1=s5c[:, 1, P0:P0 + 8].unsqueeze(2).to_broadcast([128, 8, 256]), op=ALU.mult),
                       ["cst", "s5c"], ["PH"], "pool")
                    range_reduce(SNh, PH, 0.0, rsn, "PH", T1, KK, "T1", "KK")
                    dv(lambda e, CSh=CSh, SNh=SNh: e.activation(out=CSh, in_=SNh, func=AF.Abs), [rsn], [rcs], "act")
                    dv(lambda e, CSh=CSh: e.activation(out=CSh, in_=CSh, func=AF.Sin, scale=-1.0, bias=cst[:, C_HPI:C_HPI + 1]), [rcs, "cst"], [rcs], "act")
                    dv(lambda e, SNh=SNh: e.activation(out=SNh, in_=SNh, func=AF.Sin), [rsn, rcs], [rsn], "act")
                for hf in range(2):
                    P0 = 8 * hf
                    CSh = tabs_cs[hf]; SNh = tabs_sn[hf]; rcs = ("CS", hf); rsn = ("SN", hf)
                    for bt in range(2):
                        for quad in range(2):
                            for c_, W1 in enumerate((W1re, W1im)):
                                bank = 2 + ((bt * 2 + quad) * 2 + c_) % 4

                                def mmx(e, bt=bt, quad=quad, W1=W1, bank=bank, P0=P0):
                                    last = None
                                    for pl in range(4):
                                        Pp = P0 + 4 * quad + pl
                                        e.matmul(PS[bank][:, pl * 128:(pl + 1) * 128], lhsT=W1[:, 2 * Pp, :], rhs=U[:, 2 * Pp, bt * 128:(bt + 1) * 128],
                                                 start=True, stop=False)
                                        last = e.matmul(PS[bank][:, pl * 128:(pl + 1) * 128], lhsT=W1[:, 2 * Pp + 1, :],
                                                        rhs=U[:, 2 * Pp + 1, bt * 128:(bt + 1) * 128], start=False, stop=True)
                                    return last
                                P.op("pe", mmx, reads=[r("U", bt, oc) for oc in range(4)] + [r("W1re"), r("W1im")], writes=[psr(bank)])
                                Xc = X[c_]
                                if c_ == 0:
                                    P.op("act", lambda e, Xc=Xc, quad=quad, bt=bt, bank=bank: e.activation(
                                        out=Xc[:, 4 * quad:4 * quad + 4, bt * 128:(bt + 1) * 128], in_=PS[bank][:, :].rearrange("p (a b) -> p a b", a=4), func=AF.Copy),
                                        reads=[psr(bank)], writes=[r("X", c_)])
                                else:
                                    P.op("dve", lambda e, Xc=Xc, quad=quad, bt=bt, bank=bank: e.tensor_copy(
                                        out=Xc[:, 4 * quad:4 * quad + 4, bt * 128:(bt + 1) * 128], in_=PS[bank][:, :].rearrange("p (a b) -> p a b", a=4)),
                                        reads=[psr(bank)], writes=[r("X", c_)])
                    dv(lambda e, CSh=CSh, SNh=SNh: e.tensor_tensor(out=T1, in0=CSh, in1=Xr[:], op=ALU.mult), [rcs, ("X", 0)], ["T1"])
                    dv(lambda e, CSh=CSh, SNh=SNh: e.tensor_tensor(out=T2, in0=SNh, in1=Xi[:], op=ALU.mult), [rsn, ("X", 1)], ["T2"], "pool")
                    dv(lambda e: e.tensor_tensor(out=T1, in0=T1, in1=T2, op=ALU.add), ["T1", "T2"], ["T1"])
                    dv(lambda e, CSh=CSh, SNh=SNh: e.tensor_tensor(out=T2, in0=CSh, in1=Xi[:], op=ALU.mult), [rcs, ("X", 1)], ["T2"], "pool")
                    dv(lambda e, CSh=CSh, SNh=SNh: e.tensor_tensor(out=Xi[:], in0=SNh, in1=Xr[:], op=ALU.mult), [rsn, ("X", 0)], [("X", 1)])
                    dv(lambda e: e.tensor_tensor(out=T2, in0=T2, in1=Xi[:], op=ALU.subtract), ["T2", ("X", 1)], ["T2"])
                    for pl in range(8):
                        Pp = P0 + pl
                        dv(lambda e, pl=pl, Pp=Pp: e.tensor_tensor_scan(out=Xr[:, pl, :], data0=s5c[:, 0, Pp:Pp + 1].to_broadcast([128, 256]),
                                                                        data1=T1[:, pl, :], initial=0.0, op0=ALU.mult, op1=ALU.add),
                           ["T1", "s5c"], [("X", 0)])
                        dv(lambda e, pl=pl, Pp=Pp: e.tensor_tensor_scan(out=Xi[:, pl, :], data0=s5c[:, 0, Pp:Pp + 1].to_broadcast([128, 256]),
                                                                        data1=T2[:, pl, :], initial=0.0, op0=ALU.mult, op1=ALU.add),
                           ["T2", "s5c"], [("X", 1)])
                    dv(lambda e, CSh=CSh, SNh=SNh: e.tensor_tensor(out=T1, in0=CSh, in1=Xr[:], op=ALU.mult), [rcs, ("X", 0)], ["T1"])
                    dv(lambda e, CSh=CSh, SNh=SNh: e.tensor_tensor(out=T2, in0=SNh, in1=Xi[:], op=ALU.mult), [rsn, ("X", 1)], ["T2"], "pool")
                    dv(lambda e, P0=P0: e.tensor_tensor(out=Hp[0][:, P0:P0 + 8, 1:256], in0=T1[:, :, 0:255], in1=T2[:, :, 0:255], op=ALU.subtract),
                       ["T1", "T2"], [("Hp", 0, hf)])
                    dv(lambda e, P0=P0: e.tensor_tensor(out=Hf[:, 0, P0:P0 + 8], in0=T1[:, :, 255], in1=T2[:, :, 255], op=ALU.subtract),
                       ["T1", "T2"], [("Hf", 0, hf)])
                    dv(lambda e, CSh=CSh, SNh=SNh: e.tensor_tensor(out=T1, in0=CSh, in1=Xi[:], op=ALU.mult), [rcs, ("X", 1), ("Hp", 0, hf), ("Hf", 0, hf)], ["T1"])
                    dv(lambda e, CSh=CSh, SNh=SNh: e.tensor_tensor(out=T2, in0=SNh, in1=Xr[:], op=ALU.mult), [rsn, ("X", 0), ("Hp", 0, hf), ("Hf", 0, hf)], ["T2"], "pool")
                    dv(lambda e, P0=P0: e.tensor_tensor(out=Hp[1][:, P0:P0 + 8, 1:256], in0=T1[:, :, 0:255], in1=T2[:, :, 0:255], op=ALU.add),
                       ["T1", "T2"], [("Hp", 1, hf)])
                    dv(lambda e, P0=P0: e.tensor_tensor(out=Hf[:, 1, P0:P0 + 8], in0=T1[:, :, 255], in1=T2[:, :, 255], op=ALU.add),
                       ["T1", "T2"], [("Hf", 1, hf)])
                sft0 = scrF[:, 6144:6400].rearrange("p (a b) -> p a b", a=16); sft1 = scrF[:, 6400:6656].rearrange("p (a b) -> p a b", a=16)
                Sf0 = scrF[:, 4096:4352].rearrange("p (a b) -> p a b", a=16); Sf1 = scrF[:, 4352:4608].rearrange("p (a b) -> p a b", a=16)
                Sf = (Sf0, Sf1)
                a8rb = s5c[:, 2, :].unsqueeze(2).to_broadcast([128, 16, 16]); a8ib = s5c[:, 3, :].unsqueeze(2).to_broadcast([128, 16, 16])
                dv(lambda e: e.tensor_tensor(out=sft0, in0=H0T[0][:], in1=a8rb, op=ALU.mult), ["H0T", "s5c"], ["KK"])
                dv(lambda e: e.tensor_tensor(out=sft1, in0=H0T[1][:], in1=a8ib, op=ALU.mult), ["H0T", "s5c"], ["KK"])
                dv(lambda e: e.tensor_tensor(out=sft0, in0=sft0, in1=sft1, op=ALU.subtract), ["KK"], ["KK"])
                dv(lambda e: e.tensor_tensor(out=Sf0, in0=sft0, in1=PS[6][:, 0:256].rearrange("p (a b) -> p a b", a=16), op=ALU.add),
                   ["KK", ("ps", 6)], ["PH"])
                dv(lambda e: e.tensor_tensor(out=sft0, in0=H0T[1][:], in1=a8rb, op=ALU.mult), ["H0T", "s5c", "PH"], ["KK"])
                dv(lambda e: e.tensor_tensor(out=sft1, in0=H0T[0][:], in1=a8ib, op=ALU.mult), ["H0T", "s5c", "PH"], ["KK"])
                dv(lambda e: e.tensor_tensor(out=sft0, in0=sft0, in1=sft1, op=ALU.add), ["KK"], ["KK"])
                dv(lambda e: e.tensor_tensor(out=Sf1, in0=sft0, in1=PS[7][:, 0:256].rearrange("p (a b) -> p a b", a=16), op=ALU.add),
                   ["KK", ("ps", 7)], ["PH"])

                P.dma("sp", [lambda e: ncdma(e, out=D["pr"].rearrange("(P e) n -> (e n) P", e=2), in_=Hf[:, 0, :]),
                             lambda e: ncdma(e, out=D["pi"].rearrange("(P e) n -> (e n) P", e=2), in_=Hf[:, 1, :])],
                      reads=[r("Hf", c_, hf) for c_ in range(2) for hf in range(2)], out=True)
                Sout = scrF[0:16, 0:2048]
                for c_ in range(2):
                    def mmso(e, c_=c_):
                        last = None
                        for Pp in range(16):
                            last = e.matmul(PS[Pp // 4][0:16, (Pp % 4) * 128:(Pp % 4 + 1) * 128], lhsT=Sf[c_][:, Pp, :], rhs=identf, start=True, stop=True)
                        return last
                    P.op("pe", mmso, reads=[r("PH"), r("cst")], writes=[psr(0), psr(1), psr(2), psr(3)])
                    for bk in range(4):
                        dv(lambda e, bk=bk: e.tensor_copy(out=Sout[:, bk * 512:(bk + 1) * 512], in_=PS[bk][0:16, :]), [("ps", bk)], ["Sout", "T1"])
                    dn = "sr" if c_ == 0 else "si"
                    P.dma("sp", [lambda e, dn=dn: e.dma_start(out=D[dn][:, :], in_=Sout)], reads=[r("Sout")], out=True)
            if "da" in debug:
                for nm, t_ in (("U", U), ("Hp0", Hp[0]), ("Hp1", Hp[1])):
                    dbg[nm] = nc.dram_tensor("dbg_" + nm, list(t_.shape), BF16, kind="ExternalOutput").ap()
                    P.dma("sp", [lambda e, nm=nm, t_=t_: e.dma_start(out=dbg[nm][:, :, :], in_=t_[:])],
                          reads=[P.res[k] for k in list(P.res) if k[0] in ("U", "Hp", "Hp0")], out=True)

            P.emit()
            if debug.endswith("da"):
                esd.close(); mid.close()
                return nc
            with ExitStack() as esb:
                P8f = P8[:].rearrange("p a c -> p (a c)")
                G8_bufs = (scrF[:, 0:4096], P8f[:, 0:8192].bitcast(F32))
                gT = scrB[:, 8192:12288].rearrange("p (j c) -> p j c", j=4)
                G8b0 = sb(esb, "G8b", (128, 4096), BF16); O8b = sb(esb, "O8b", (128, 4096), BF16)
                G8b_bufs = (G8b0[:], P8f[:, 8192:12288])
                tmpz = [sb(esb, "tmpz%d" % i, (128, 512)) for i in range(2)]
                junk2 = sb(esb, "junk2", (128, 512), BF16)
                for bt in range(3):
                    if bt == 2:
                        for c_ in range(2):
                            dv(lambda e, c_=c_: e.tensor_copy(out=Hs[c_][:], in_=H0T[c_][:]), ["H0T"], [("Hs", c_)], "pool")
                    bf_ = bt % 2
                    G8 = G8_bufs[bf_]; G8b = G8b_bufs[bf_]
                    G8v = G8.rearrange("p (s g o) -> p g s o", s=8, g=32)
                    G8bv = G8b.rearrange("p (s g o) -> p g s o", s=8, g=32)
                    npt = 128 if bt < 2 else 16
                    for gq in range(8):
                        bank = 4 + gq % 2

                        def mmy(e, bt=bt, gq=gq, bank=bank, npt=npt):
                            last = None
                            for gl in range(4):
                                g_ = 4 * gq + gl; Pp = g_ // 2; ee = g_ % 2; lo, hi = ee * 64, ee * 64 + 64
                                o_ = PS[bank][0:npt, gl * 128:(gl + 1) * 128]
                                if bt < 2:
                                    u_ = U[:, g_, bt * 128:(bt + 1) * 128]; hr_ = Hp[0][lo:hi, Pp, bt * 128:(bt + 1) * 128]; hi_ = Hp[1][lo:hi, Pp, bt * 128:(bt + 1) * 128]
                                else:
                                    u_ = Us[:, g_, :]; hr_ = Hs[0][lo:hi, Pp, :]; hi_ = Hs[1][lo:hi, Pp, :]
                                e.matmul(o_, lhsT=u_, rhs=Tm[:, g_, :], start=True, stop=False)
                                e.matmul(o_, lhsT=hr_, rhs=W2rb[lo:hi, Pp, :], start=False, stop=False)
                                last = e.matmul(o_, lhsT=hi_, rhs=W2ib[lo:hi, Pp, :], start=False, stop=True)
                            return last
                        rd = ([r("U", bt, oc) for oc in range(4)] + [r("Hp", c_, hf) for c_ in range(2) for hf in range(2)] + [r("Hp0", 0), r("Hp0", 1)]) if bt < 2 \
                            else [r("Us"), r("Hs", 0), r("Hs", 1)]
                        P.op("pe", mmy, reads=rd + [r("Tm", g_) for g_ in range(4 * gq, 4 * gq + 4)] + [r("W2rb"), r("W2ib")], writes=[psr(bank)])
                        P.op("act", lambda e, gq=gq, bank=bank, npt=npt, G8v=G8v: e.activation(
                            out=G8v[0:npt, 4 * gq:4 * gq + 4, :, :], in_=PS[bank][0:npt, :].rearrange("p (g s o) -> p g s o", g=4, s=8), func=AF.Gelu_apprx_tanh),
                            reads=[psr(bank)], writes=[r("G8", bf_, gq)])
                        P.op("dve", lambda e, gq=gq, npt=npt, G8bv=G8bv, G8v=G8v: e.tensor_copy(
                            out=G8bv[0:npt, 4 * gq:4 * gq + 4, :, :], in_=G8v[0:npt, 4 * gq:4 * gq + 4, :, :]),
                            reads=[r("G8", bf_, gq)], writes=[r("G8b", bf_, gq)])
                    def phaseA(s_, bt=bt, npt=npt, bf_=bf_, G8=G8, G8b=G8b):
                        bk = s_ % 2; idx = bt * 8 + s_
                        P.op("pe", lambda e, s_=s_, bk=bk, npt=npt: pe_transposes(
                            e, [PSB[bk][:, j * npt:(j + 1) * npt] for j in range(4)],
                            [G8b[0:npt, s_ * 512 + j * 128:s_ * 512 + (j + 1) * 128] for j in range(4)], npart=npt),
                            reads=[r("G8b", bf_, gq) for gq in range(8)] + [r("identb")], writes=[psr(bk)])
                        dv(lambda e, s_=s_, bk=bk, npt=npt: e.tensor_copy(out=gT[:, :, s_ * npt:(s_ + 1) * npt],
                                                                          in_=PSB[bk][:, 0:4 * npt].rearrange("p (j c) -> p j c", j=4)),
                           [("ps", bk)], [("gT", s_)])
                        zb = 6 + s_ % 2

                        def mmz(e, s_=s_, zb=zb, npt=npt):
                            for j in range(4):
                                e.matmul(PS[zb][0:npt, :], lhsT=gT[:, j, s_ * npt:(s_ + 1) * npt], rhs=wglu[:, j, :], start=(j == 0), stop=False)
                            return e.matmul(PS[zb][0:npt, :], lhsT=onesb[0:1, 0:npt], rhs=bglub[0:1, :], start=False, stop=True)
                        P.op("pe", mmz, reads=[r("gT", s_), r("wglu"), r("onesb")], writes=[psr(zb)])

                    def phaseB(s_, bt=bt, npt=npt, bf_=bf_, G8=G8, G8b=G8b):
                        bk = s_ % 2; idx = bt * 8 + s_; zb = 6 + s_ % 2
                        tz = tmpz[s_ % 2]
                        P.op("act", lambda e, tz=tz, zb=zb, npt=npt: e.activation(out=tz[0:npt, :], in_=PS[zb][0:npt, :], func=AF.Tanh, scale=0.5),
                             reads=[psr(zb)], writes=[r("tmpz", s_ % 2)])
                        g8s = G8[0:npt, s_ * 512:(s_ + 1) * 512]
                        dv(lambda e, tz=tz, g8s=g8s, npt=npt: e.scalar_tensor_tensor(out=g8s, in0=tz[0:npt, :], scalar=1.0, in1=g8s, op0=ALU.add, op1=ALU.mult),
                           [("tmpz", s_ % 2)] + [("G8", bf_, gq) for gq in range(8)], [("G8s", bf_, s_)])
                        P.op("act", lambda e, g8s=g8s, idx=idx, npt=npt: e.activation(out=junk2[0:npt, :], in_=g8s, func=AF.Square, accum_out=ssq5[0:npt, idx:idx + 1]),
                             reads=[r("G8s", bf_, s_)], writes=[r("junk2"), r("ssq5")])
                        dv(lambda e, g8s=g8s, s_=s_, npt=npt: e.tensor_tensor(out=O8b[0:npt, s_ * 512:(s_ + 1) * 512], in0=g8s, in1=gs5bc[0:npt, :], op=ALU.mult),
                           [("G8s", bf_, s_), "gs5bc"], [("O8b", s_)], "pool")
                        bk2 = 2 + s_ % 2
                        P.op("pe", lambda e, s_=s_, bk2=bk2, npt=npt: pe_transposes(
                            e, [PSB[bk2][:, j * npt:(j + 1) * npt] for j in range(4)],
                            [O8b[0:npt, s_ * 512 + j * 128:s_ * 512 + (j + 1) * 128] for j in range(4)], npart=npt),
                            reads=[r("O8b", s_), r("identb")], writes=[psr(bk2)])

                    def phaseC(s_, bt=bt, npt=npt):
                        bk2 = 2 + s_ % 2
                        if bt < 2:
                            mo = mixT[:, 0:4, bt * 1024 + s_:bt * 1024 + 1024:8]
                        else:
                            mo = mixT[:, 0:4, 2048 + s_:2176:8]
                        P.op("act", lambda e, mo=mo, bk2=bk2, npt=npt: e.activation(out=mo, in_=PSB[bk2][:, 0:4 * npt].rearrange("p (j c) -> p j c", j=4), func=AF.Copy),
                             reads=[psr(bk2)], writes=[r("mixs5", bt, s_)])

                    phaseA(0)
                    for s_ in range(8):
                        if s_ + 1 < 8:
                            phaseA(s_ + 1)
                        phaseB(s_)
                        if s_ >= 1:
                            phaseC(s_ - 1)
                    phaseC(7)
                    for gq in range(8):
                        dst = r("G8", bf_, gq)
                        for s_ in range(8):
                            src = r("G8s", bf_, s_)
                            for tok in list(src.rs.values()) + ([src.w] if src.w is not None else []):
                                k = id(tok[0])
                                if k not in dst.rs or dst.rs[k][1] < tok[1]:
                                    dst.rs[k] = tok
                if "db" in debug:
                    dbg["mixs5"] = nc.dram_tensor("dbg_mixs5", [128, 4, TOK], BF16, kind="ExternalOutput").ap()
                    P.dma("sp", [lambda e: e.dma_start(out=dbg["mixs5"][:, :, :], in_=mixT[:, 0:4, :])],
                          reads=[r("mixs5", bt, s_) for bt in range(3) for s_ in range(8)], out=True)
                    dbg["ssq5"] = nc.dram_tensor("dbg_ssq5", [128, 24], F32, kind="ExternalOutput").ap()
                    P.dma("sp", [lambda e: e.dma_start(out=dbg["ssq5"][:, :], in_=ssq5[:])], reads=[r("ssq5")], out=True)
                P.emit()
        mid.close()
        if debug.endswith("d"):
            return nc

        X1 = sb(top, "X1", (128, NT, 1024))
        ssq2 = sb(top, "ssq2", (128, NT)); rstd2 = sb(top, "rstd2", (128, NT)); srtE = sb(top, "srtE", (128, NT))
        ssqf = sb(top, "ssqf", (128, NT)); rstdf = sb(top, "rstdf", (128, NT))
        wd = sb(top, "wd", (128, 8, 1024), BF16)
        wg = [sb(top, "wg%d" % i, (128, 8, 128), BF16) for i in range(3)]
        wu = [sb(top, "wu%d" % i, (128, 8, 128), BF16) for i in range(3)]
        def load_f(f):
            sl = f % 3
            P.dma("pool", [lambda e, f=f, sl=sl: e.dma_start(out=wg[sl][:], in_=D["w_gate"][:, f * 128:(f + 1) * 128].rearrange("(kt p) n -> p kt n", p=128)),
                           lambda e, f=f, sl=sl: e.dma_start(out=wu[sl][:], in_=D["w_up"][:, f * 128:(f + 1) * 128].rearrange("(kt p) n -> p kt n", p=128))],
                  writes=[r("wgu", sl)])

        def load_wd(f, fl):
            P.dma("pool", [lambda e, f=f, fl=fl: e.dma_start(out=wd[:, fl, :], in_=D["w_down"][f * 128:(f + 1) * 128, :])], writes=[r("wd", fl)])

        with ExitStack() as es:
            wout = sb(es, "wout", (128, 8, 1024), BF16)
            g2bc = sb(es, "g2bc", (128, 1024))
            xt = [sb(es, "ext%d" % i, (128, 1024)) for i in range(2)]
            xb = [sb(es, "exb%d" % i, (128, 1024), BF16) for i in range(2)]
            wov = D["w_out"].rearrange("(kt p) n -> p kt n", p=128)
            P.dma("pool", [lambda e: e.dma_start(out=wout[:, 4:8, :], in_=wov[:, 4:8, :])], writes=[r("wout", 1)])
            P.dma("pool", [lambda e: e.dma_start(out=wout[:, 0:4, :], in_=wov[:, 0:4, :])], writes=[r("wout", 0)])
            P.dma("sp", [lambda e: e.dma_start(out=g2bc[:], in_=D["norm2"][0:1, :].partition_broadcast(128))], writes=[r("g2bc")])
            P.dma("sp", [lambda e: ncdma(e, out=scr2[0:2048].rearrange("(a b s) -> b a s", a=2, s=8), in_=ssq5[:, 0:16].rearrange("p (a s) -> p a s", a=2)),
                         lambda e: ncdma(e, out=scr2[2048:2176].rearrange("(q s) -> q s", s=8), in_=ssq5[0:16, 16:24])],
                  reads=[r("ssq5")], writes=[r("scr2")])
            P.dma("sp", [lambda e: ncdma(e, out=ssq5n[:], in_=scr2.rearrange("(t p) -> p t", p=128))], reads=[r("scr2")], writes=[r("ssq5n")])
            dv(lambda e: e.activation(out=srtE[:], in_=ssq5n[:], func=AF.Sqrt, bias=eps4c, scale=1.0 / 512), ["ssq5n", "cst"], ["srtE"], "act")
            dv(lambda e: e.reciprocal(out=rstd5[:], in_=srtE[:]), ["srtE"], ["rstd5"])
            dv(lambda e: e.activation(out=srtE[:], in_=ssqcm[:], func=AF.Sqrt, bias=epsc, scale=1.0 / 512),
               [("ssqcm", t) for t in range(NT)] + ["cst", "rstd5"], ["srtE"], "act")
            dv(lambda e: e.reciprocal(out=rstdcm[:], in_=srtE[:]), ["srtE"], ["rstdcm"])
            for t in range(NT):
                xs_ = xt[t % 2]; rx = r("ext", t % 2); b0 = 4 * (t % 2)
                P.dma("sp", [lambda e, xs_=xs_, t=t: e.dma_start(out=xs_[:], in_=xsrc(t))], writes=[rx])
                for part in range(2):
                    for half in range(2):
                        bank = b0 + 2 * part + half
                        k0 = 4 if part == 0 else 0

                        def mmo(e, t=t, bank=bank, k0=k0, half=half):
                            last = None
                            for j in range(4):
                                last = e.matmul(PS[bank][:, :], lhsT=mixT[:, k0 + j, t * 128:(t + 1) * 128], rhs=wout[:, k0 + j, half * 512:(half + 1) * 512],
                                                start=(j == 0), stop=(j == 3))
                            return last
                        rd = [r("mixcm", t)] if part == 0 else [r("mixs5", bt, s_) for bt in range(3) for s_ in range(8)]
                        P.op("pe", mmo, reads=rd + [r("wout", 1 - part)], writes=[psr(bank)])
                for half in range(2):
                    hs = slice(half * 512, (half + 1) * 512)
                    dv(lambda e, t=t, hs=hs, xs_=xs_, bank=b0 + half: e.scalar_tensor_tensor(out=X1[:, t, hs], in0=PS[bank][:, :], scalar=rstdcm[:, t:t + 1], in1=xs_[:, hs],
                                                                                     op0=ALU.mult, op1=ALU.add),
                       [("ps", b0 + half), "rstdcm", ("ext", t % 2)], [("X1", t)])
                    dv(lambda e, t=t, hs=hs, bank=b0 + 2 + half: e.scalar_tensor_tensor(out=X1[:, t, hs], in0=PS[bank][:, :], scalar=rstd5[:, t:t + 1], in1=X1[:, t, hs],
                                                                                op0=ALU.mult, op1=ALU.add),
                       [("ps", b0 + 2 + half), "rstd5", ("X1", t)], [("X1", t)])
                dv(lambda e, t=t: e.activation(out=xb[t % 2][:], in_=X1[:, t, :], func=AF.Square, accum_out=ssq2[:, t:t + 1]), [("X1", t)], [("exb", t % 2), ("ssq2", t)], "act")
            for f in range(3):
                load_f(f)
            for fl in range(8):
                load_wd(fl, fl)
            dv(lambda e: e.activation(out=srtE[:], in_=ssq2[:], func=AF.Sqrt, bias=epsc, scale=1.0 / 1024),
               [("ssq2", t) for t in range(NT)] + ["cst", "rstdcm"], ["srtE"], "act")
            dv(lambda e: e.reciprocal(out=rstd2[:], in_=srtE[:]), ["srtE"], ["rstd2"])
            for t in range(NT):
                xbt = xb[t % 2]; pb = t % 2
                dv(lambda e, t=t, xbt=xbt: e.scalar_tensor_tensor(out=xbt[:], in0=X1[:, t, :], scalar=rstd2[:, t:t + 1], in1=g2bc[:], op0=ALU.mult, op1=ALU.mult),
                   [("X1", t), "rstd2", "g2bc"], [("exb", t % 2)])
                P.op("pe", lambda e, xbt=xbt, pb=pb: pe_transposes(e, [PSB[pb][:, k * 128:(k + 1) * 128] for k in range(8)],
                                                               [xbt[:, k * 128:(k + 1) * 128] for k in range(8)]),
                     reads=[r("exb", t % 2), r("identb")], writes=[psr(pb)])
                P.op("act", lambda e, t=t, pb=pb: e.activation(out=actA[:, :, t * 128:(t + 1) * 128], in_=PSB[pb][:, 0:1024].rearrange("p (k c) -> p k c", k=8), func=AF.Copy),
                     reads=[psr(pb)], writes=[r("x2T", t)])
            if "e" in debug.split("-"):
                dbg["X1"] = nc.dram_tensor("dbg_X1", [128, NT, 1024], F32, kind="ExternalOutput").ap()
                P.dma("sp", [lambda e: e.dma_start(out=dbg["X1"][:, :, :], in_=X1[:])], reads=[r("X1", t) for t in range(NT)], out=True)
                dbg["x2T"] = nc.dram_tensor("dbg_x2T", [128, 8, TOK], BF16, kind="ExternalOutput").ap()
                P.dma("sp", [lambda e: e.dma_start(out=dbg["x2T"][:, :, :], in_=actA[:])], reads=[r("x2T", t) for t in range(NT)], out=True)
            P.emit()
        if debug.endswith("-e"):
            return nc

        with ExitStack() as es:
            hF = mixT
            sg = [sb(es, "sg%d" % i, (128, 512)) for i in range(2)]
            gfbc = sb(es, "gfbc", (128, 1024))
            yo = [sb(es, "yo%d" % i, (128, 1024)) for i in range(2)]
            gjunk = sb(es, "gjunk", (128, 1024), BF16)
            P.dma("sp", [lambda e: e.dma_start(out=gfbc[:], in_=D["norm_f"][0:1, :].partition_broadcast(128))], writes=[r("gfbc")])
            fgroups = [(0, 8), (8, 16), (16, 22)]
            tgs = [(0, 512), (512, 1024), (1024, 1536), (1536, 2048), (2048, 2176)]

            cnt = 0
            for gi, (f0, f1) in enumerate(fgroups):
                nf = f1 - f0
                if gi > 0:
                    for fl in range(nf):
                        load_wd(f0 + fl, fl)
                for f in range(f0, f1):
                    sl = f % 3; fl = f - f0
                    for ti, (c0, c1) in enumerate(tgs):
                        n = c1 - c0; bg = cnt % 2; bu = 2 + cnt % 2; sgt = sg[cnt % 2]; rsg = r("sg", cnt % 2); cnt += 1
                        tiles = list(range(c0 // 128, (c1 + 127) // 128))

                        def mmgu(e, W, bank, sl=sl, c0=c0, c1=c1, n=n):
                            last = None
                            for k in range(8):
                                last = e.matmul(PS[bank][:, 0:n], lhsT=W[sl][:, k, :], rhs=actA[:, k, c0:c1], start=(k == 0), stop=(k == 7))
                            return last
                        P.op("pe", lambda e, bg=bg, mmgu=mmgu: mmgu(e, wg, bg), reads=[r("x2T", t) for t in tiles] + [r("wgu", sl)], writes=[psr(bg)])
                        P.op("pe", lambda e, bu=bu, mmgu=mmgu: mmgu(e, wu, bu), reads=[r("x2T", t) for t in tiles] + [r("wgu", sl)], writes=[psr(bu)])
                        P.op("act", lambda e, sgt=sgt, bg=bg, n=n: e.activation(out=sgt[:, 0:n], in_=PS[bg][:, 0:n], func=AF.Silu), reads=[psr(bg)], writes=[rsg])
                        P.op("dve", lambda e, sgt=sgt, bu=bu, n=n, fl=fl, c0=c0, c1=c1: e.tensor_tensor(out=hF[:, fl, c0:c1], in0=sgt[:, 0:n], in1=PS[bu][:, 0:n], op=ALU.mult),
                             reads=[rsg, psr(bu)], writes=[r("hF", fl, ti)])
                    if f + 3 < NFF:
                        load_f(f + 3)
                for t in range(NT):
                    ti = min(t // 4, 4)
                    for half in range(2):
                        bank = 4 + (t * 2 + half) % 4

                        def mmd(e, t=t, half=half, bank=bank, nf=nf):
                            last = None
                            for fl in range(nf):
                                last = e.matmul(PS[bank][:, :], lhsT=hF[:, fl, t * 128:(t + 1) * 128], rhs=wd[:, fl, half * 512:(half + 1) * 512],
                                                start=(fl == 0), stop=(fl == nf - 1))
                            return last
                        P.op("pe", mmd, reads=[r("hF", fl, ti) for fl in range(nf)] + [r("wd", fl) for fl in range(nf)], writes=[psr(bank)])
                        hs = slice(half * 512, (half + 1) * 512)
                        dv(lambda e, t=t, hs=hs, bank=bank: e.tensor_tensor(out=X1[:, t, hs], in0=X1[:, t, hs], in1=PS[bank][:, :], op=ALU.add),
                           [("ps", bank), ("X1", t)], [("X1", t)])
                    if gi == len(fgroups) - 1:
                        yot = yo[t % 2]
                        dv(lambda e, t=t: e.activation(out=gjunk[:], in_=X1[:, t, :], func=AF.Square, accum_out=ssqf[:, t:t + 1]), [("X1", t)], ["gjunk", ("ssqf", t)], "act")
                        dv(lambda e, t=t: e.activation(out=srtE[:, t:t + 1], in_=ssqf[:, t:t + 1], func=AF.Sqrt, bias=epsc, scale=1.0 / 1024),
                           [("ssqf", t), "cst"], [("srtf", t)], "act")
                        dv(lambda e, t=t: e.reciprocal(out=rstdf[:, t:t + 1], in_=srtE[:, t:t + 1]), [("srtf", t)], [("rstdf", t)])
                        dv(lambda e, t=t, yot=yot: e.scalar_tensor_tensor(out=yot[:], in0=X1[:, t, :], scalar=rstdf[:, t:t + 1], in1=gfbc[:], op0=ALU.mult, op1=ALU.mult),
                           [("X1", t), ("rstdf", t), "gfbc"], [("yo", t % 2)])
                        P.dma("sp", [lambda e, t=t, yot=yot: e.dma_start(out=ydst(t), in_=yot[:])], reads=[r("yo", t % 2)], out=True)
            P.emit()


    return nc


_CACHE = {}


def _prep_inputs(inputs, c):
    f = lambda a: np.ascontiguousarray(np.asarray(a, dtype=np.float32))
    m = {
        "xp": f(inputs["x_prompt"][c]),
        "xs": f(inputs["x_sample"][16 * c:16 * c + 16]).reshape(128, 1024),
        "h0r": f(inputs["state_s5_re"][0, 16 * c:16 * c + 16]).reshape(16, 2048),
        "h0i": f(inputs["state_s5_im"][0, 16 * c:16 * c + 16]).reshape(16, 2048),
        "norm1": f(inputs["norm1"]).reshape(1, 1024),
        "w_in": f(inputs["w_in"][0]),
        "lam_re": f(inputs["lam_re"][0]), "lam_im": f(inputs["lam_im"][0]),
        "log_dt": f(inputs["log_dt"]).reshape(1, 32),
        "b_re": f(inputs["b_re"][0]).reshape(2048, 16), "b_im": f(inputs["b_im"][0]).reshape(2048, 16),
        "c_re": f(inputs["c_re"][0]).reshape(512, 64), "c_im": f(inputs["c_im"][0]).reshape(512, 64),
        "d_skip": f(inputs["d_skip"]).reshape(1, 512), "w_glu": f(inputs["w_glu"][0]), "b_glu": f(inputs["b_glu"]).reshape(1, 512),
        "cm_ln_g": f(inputs["cm_ln_g"]).reshape(1, 512), "cm_ln_b": f(inputs["cm_ln_b"]).reshape(1, 512),
        "w_s": f(inputs["w_s"][0]).reshape(1024, 128), "b_s": f(inputs["b_s"][0]),
        "g_s5": f(inputs["g_s5"]).reshape(1, 512), "g_cm": f(inputs["g_cm"]).reshape(1, 512),
        "w_out": f(inputs["w_out"][0]), "norm2": f(inputs["norm2"]).reshape(1, 1024),
        "w_gate": f(inputs["w_gate"][0]), "w_up": f(inputs["w_up"][0]), "w_down": f(inputs["w_down"][0]),
        "norm_f": f(inputs["norm_f"]).reshape(1, 1024),
        "consts": make_consts(),
    }
    return m


def kernel(**inputs):
    nc = build(KDEBUG)
    in_maps = [_prep_inputs(inputs, c) for c in range(8)]
    res = run_bass_kernel_spmd(nc, in_maps, core_ids=list(range(8)))
    R = res.results
    yp = np.stack([R[c]["yp"] for c in range(8)]).astype(np.float32)
    ys = np.concatenate([R[c]["ys"].reshape(16, 8, 1024) for c in range(8)]).astype(np.float32)
    pr = np.stack([R[c]["pr"] for c in range(8)])[None].astype(np.float32)
    pi = np.stack([R[c]["pi"] for c in range(8)])[None].astype(np.float32)
    pv = np.stack([R[c]["pv"] for c in range(8)])[None].astype(np.float32)
    sr = np.concatenate([R[c]["sr"].reshape(16, 32, 64) for c in range(8)])[None].astype(np.float32)
    si = np.concatenate([R[c]["si"].reshape(16, 32, 64) for c in range(8)])[None].astype(np.float32)
    sv = np.concatenate([R[c]["sv"].reshape(16, 8, 512) for c in range(8)])[None].astype(np.float32)
    return (yp, ys, pr, pi, pv, sr, si, sv)
```
